# Optimizing a Trainium2 kernel written in Bass

```python
import jax, jax.numpy as jnp
from jax import lax
import numpy as np

D_MODEL = 1024
BATCH = 8
SEQ = 2048
DEPTH = 1
DEC_BATCH = 128
DEC_SEQ = 8
PAST_LEN = 16384
PAGE_SIZE = 128

HEAD_DIM = 64
D_RWKV = D_MODEL // 2
N_RWKV_HEADS = D_RWKV // HEAD_DIM
D_CONV = D_MODEL // 4
CONV_WIDTH = 31
D_XATTN = D_MODEL // 4
N_XATTN_HEADS = 4
XATTN_HEAD_DIM = D_XATTN // N_XATTN_HEADS
N_MEM = 256
LORA_DECAY = 64
LORA_A = 64
LORA_G = 160
D_SHIFT = 3 * D_RWKV + LORA_DECAY + LORA_A + LORA_G
N_BRANCH = 3
D_IN = D_SHIFT + 2 * D_CONV + D_XATTN + N_BRANCH * D_MODEL
D_FF = 2816
RMS_EPS = 1e-6
LN_EPS = 1e-5
GN_EPS = 64e-5

kernel_name = 'hybrid_rwkv7_conformer_conv_memxattn_macaron_step'


def _rmsnorm(x, g):
    xf = x.astype(jnp.float32)
    return xf * lax.rsqrt(jnp.mean(xf * xf, axis=-1, keepdims=True) + RMS_EPS) * g


def _swiglu(x, w_up, w_down):
    hg = x @ w_up
    return (jax.nn.silu(hg[..., :D_FF]) * hg[..., D_FF:]) @ w_down


def _memory_kv(mem, w_mem_kv):
    b = mem.shape[0]
    kv = mem.astype(jnp.float32) @ w_mem_kv
    k = kv[..., :D_XATTN].reshape(b, N_MEM, N_XATTN_HEADS, XATTN_HEAD_DIM)
    v = kv[..., D_XATTN:].reshape(b, N_MEM, N_XATTN_HEADS, XATTN_HEAD_DIM)
    return k, v


def _wkv7_scan(s0, r, w, k, v, kk, kka):
    def step(s, inp):
        r_t, w_t, k_t, v_t, kk_t, kka_t = inp
        sa = jnp.einsum('bhij,bhj->bhi', s, -kk_t)
        s = (s * w_t[:, :, None, :] + sa[..., None] * kka_t[:, :, None, :]
             + v_t[..., None] * k_t[:, :, None, :])
        return s, jnp.einsum('bhij,bhj->bhi', s, r_t)
    xs = tuple(jnp.moveaxis(a, 1, 0) for a in (r, w, k, v, kk, kka))
    s, o = lax.scan(step, s0, xs)
    return s, jnp.moveaxis(o, 0, 1)


def _token_mixing(h, shift_prev, conv_prev, wkv0, mem_k, mem_v, p):
    f32 = jnp.float32
    b, t, _ = h.shape
    z = h @ p['w_in']
    o1 = D_SHIFT
    o2 = o1 + 2 * D_CONV
    o3 = o2 + D_XATTN
    zs, zc, zq, zg = z[..., :o1], z[..., o1:o2], z[..., o2:o3], z[..., o3:]

    prev = jnp.concatenate([shift_prev[:, None, :].astype(f32), zs[:, :-1]], axis=1)
    xm = zs + (prev - zs) * p['mu_shift']
    c1, c2, c3 = D_RWKV, 2 * D_RWKV, 3 * D_RWKV
    c4 = c3 + LORA_DECAY
    c5 = c4 + LORA_A
    r, k, v = xm[..., :c1], xm[..., c1:c2], xm[..., c2:c3]
    wd, ad, gd = xm[..., c3:c4], xm[..., c4:c5], xm[..., c5:]
    w_log = -jax.nn.softplus(-(p['w0'] + jnp.tanh(wd) @ p['w_decay_up'])) - 0.5
    decay = jnp.exp(-jnp.exp(w_log))
    a = jax.nn.sigmoid(p['a0'] + ad @ p['w_a_up'])
    g = jax.nn.sigmoid(gd) @ p['w_g_up']

    def hd(u):
        return u.reshape(b, t, N_RWKV_HEADS, HEAD_DIM)

    kk = hd(k * p['k_k'])
    kk = kk / jnp.maximum(jnp.sqrt(jnp.sum(kk * kk, axis=-1, keepdims=True)), 1e-12)
    a_h = hd(a)
    k_h = hd(k * (1.0 + (a - 1.0) * p['k_a']))
    r_h, v_h = hd(r), hd(v)
    wkv_new, o = _wkv7_scan(wkv0.astype(f32), r_h, hd(decay), k_h, v_h, kk, kk * a_h)
    mean = jnp.mean(o, axis=-1, keepdims=True)
    var = jnp.mean(jnp.square(o - mean), axis=-1, keepdims=True)
    gn_g = p['gn_g'].reshape(N_RWKV_HEADS, HEAD_DIM)
    gn_b = p['gn_b'].reshape(N_RWKV_HEADS, HEAD_DIM)
    o = (o - mean) * lax.rsqrt(var + GN_EPS) * gn_g + gn_b
    o = o + jnp.sum(r_h * k_h * p['r_k'], axis=-1, keepdims=True) * v_h
    y_a = (o.reshape(b, t, D_RWKV) * g) @ p['w_rwkv_out']

    zc = zc + p['glu_b']
    u = zc[..., :D_CONV] * jax.nn.sigmoid(zc[..., D_CONV:])
    u_ext = jnp.concatenate([conv_prev.astype(f32), u], axis=1)
    c = lax.conv_general_dilated(u_ext, p['conv_w'][:, None, :], (1,), 'VALID',
                                 dimension_numbers=('NWC', 'WIO', 'NWC'),
                                 feature_group_count=D_CONV) + p['conv_b']
    conv_new = u_ext[:, u_ext.shape[1] - (CONV_WIDTH - 1):]
    cm = jnp.mean(c, axis=-1, keepdims=True)
    cv = jnp.mean(jnp.square(c - cm), axis=-1, keepdims=True)
    c = (c - cm) * lax.rsqrt(cv + LN_EPS) * p['conv_ln_g'] + p['conv_ln_b']
    y_b = jax.nn.silu(c) @ p['w_conv_out']

    q = zq.reshape(b, t, N_XATTN_HEADS, XATTN_HEAD_DIM)
    s = jnp.einsum('btnd,bmnd->bntm', q, mem_k.astype(f32)) * (XATTN_HEAD_DIM ** -0.5)
    pr = jax.nn.softmax(s, axis=-1)
    o_c = jnp.einsum('bntm,bmnd->btnd', pr, mem_v.astype(f32)).reshape(b, t, D_XATTN)
    y_c = o_c @ p['w_xattn_out']

    gates = jax.nn.sigmoid(zg).reshape(b, t, N_BRANCH, D_MODEL)
    merged = gates[:, :, 0] * y_a + gates[:, :, 1] * y_b + gates[:, :, 2] * y_c
    return merged @ p['w_o'], wkv_new, zs[:, -1], conv_new


def _layer(x, shift_prev, conv_prev, wkv0, mem_k, mem_v, p):
    x = x.astype(jnp.float32)
    x = x + 0.5 * _swiglu(_rmsnorm(x, p['ffn1_norm']), p['ffn1_w_up'], p['ffn1_w_down'])
    m, wkv, shift, conv = _token_mixing(_rmsnorm(x, p['mix_norm']), shift_prev, conv_prev,
                                        wkv0, mem_k, mem_v, p)
    x = x + m
    x = x + 0.5 * _swiglu(_rmsnorm(x, p['ffn2_norm']), p['ffn2_w_up'], p['ffn2_w_down'])
    return x, wkv, shift, conv


def setup_inputs(seed: int = 0) -> dict:
    key = jax.random.key(seed)
    ks = iter(jax.random.split(key, 48))
    L = DEPTH

    def nrm(shape, scale):
        return jax.random.normal(next(ks), shape, jnp.float32) * scale

    return {
        'x_prompt': nrm((BATCH, SEQ, D_MODEL), 1.0),
        'mem_prompt': nrm((BATCH, N_MEM, D_MODEL), 1.0),
        'x_sample': nrm((DEC_BATCH, DEC_SEQ, D_MODEL), 1.0),
        'state_wkv': nrm((L, DEC_BATCH, N_RWKV_HEADS, HEAD_DIM, HEAD_DIM), 0.3),
        'state_shift': nrm((L, DEC_BATCH, D_SHIFT), 1.0),
        'state_conv': nrm((L, DEC_BATCH, CONV_WIDTH - 1, D_CONV), 0.5),
        'cache_mem_k': nrm((L, DEC_BATCH, N_MEM, N_XATTN_HEADS, XATTN_HEAD_DIM), 1.0),
        'cache_mem_v': nrm((L, DEC_BATCH, N_MEM, N_XATTN_HEADS, XATTN_HEAD_DIM), 1.0),
        'ffn1_norm': 1.0 + nrm((L, D_MODEL), 0.02),
        'ffn1_w_up': nrm((L, D_MODEL, 2 * D_FF), D_MODEL ** -0.5),
        'ffn1_w_down': nrm((L, D_FF, D_MODEL), D_FF ** -0.5),
        'mix_norm': 1.0 + nrm((L, D_MODEL), 0.02),
        'w_in': nrm((L, D_MODEL, D_IN), D_MODEL ** -0.5),
        'mu_shift': jax.random.uniform(next(ks), (L, D_SHIFT), jnp.float32),
        'w0': jax.random.uniform(next(ks), (L, D_RWKV), jnp.float32, minval=-4.0, maxval=0.0),
        'w_decay_up': nrm((L, LORA_DECAY, D_RWKV), 0.1),
        'a0': nrm((L, D_RWKV), 0.1),
        'w_a_up': nrm((L, LORA_A, D_RWKV), 0.1),
        'w_g_up': nrm((L, LORA_G, D_RWKV), LORA_G ** -0.5),
        'k_k': 0.85 + nrm((L, D_RWKV), 0.02),
        'k_a': 1.0 + nrm((L, D_RWKV), 0.02),
        'r_k': nrm((L, N_RWKV_HEADS, HEAD_DIM), 0.1),
        'gn_g': 1.0 + nrm((L, D_RWKV), 0.02),
        'gn_b': nrm((L, D_RWKV), 0.02),
        'w_rwkv_out': nrm((L, D_RWKV, D_MODEL), D_RWKV ** -0.5),
        'glu_b': nrm((L, 2 * D_CONV), 0.02),
        'conv_w': nrm((L, CONV_WIDTH, D_CONV), CONV_WIDTH ** -0.5),
        'conv_b': nrm((L, D_CONV), 0.02),
        'conv_ln_g': 1.0 + nrm((L, D_CONV), 0.02),
        'conv_ln_b': nrm((L, D_CONV), 0.02),
        'w_conv_out': nrm((L, D_CONV, D_MODEL), D_CONV ** -0.5),
        'w_mem_kv': nrm((L, D_MODEL, 2 * D_XATTN), D_MODEL ** -0.5),
        'w_xattn_out': nrm((L, D_XATTN, D_MODEL), D_XATTN ** -0.5),
        'w_o': nrm((L, D_MODEL, D_MODEL), D_MODEL ** -0.5),
        'ffn2_norm': 1.0 + nrm((L, D_MODEL), 0.02),
        'ffn2_w_up': nrm((L, D_MODEL, 2 * D_FF), D_MODEL ** -0.5),
        'ffn2_w_down': nrm((L, D_FF, D_MODEL), D_FF ** -0.5),
        'final_norm': 1.0 + nrm((D_MODEL,), 0.02),
    }


def reference(x_prompt, mem_prompt, x_sample, state_wkv, state_shift, state_conv,
              cache_mem_k, cache_mem_v, ffn1_norm, ffn1_w_up, ffn1_w_down, mix_norm, w_in,
              mu_shift, w0, w_decay_up, a0, w_a_up, w_g_up, k_k, k_a, r_k, gn_g, gn_b,
              w_rwkv_out, glu_b, conv_w, conv_b, conv_ln_g, conv_ln_b, w_conv_out, w_mem_kv,
              w_xattn_out, w_o, ffn2_norm, ffn2_w_up, ffn2_w_down, final_norm):
    f32 = jnp.float32
    layers = []
    for l in range(DEPTH):
        layers.append({
            'ffn1_norm': ffn1_norm[l].astype(f32), 'ffn1_w_up': ffn1_w_up[l].astype(f32),
            'ffn1_w_down': ffn1_w_down[l].astype(f32), 'mix_norm': mix_norm[l].astype(f32),
            'w_in': w_in[l].astype(f32), 'mu_shift': mu_shift[l].astype(f32),
            'w0': w0[l].astype(f32), 'w_decay_up': w_decay_up[l].astype(f32),
            'a0': a0[l].astype(f32), 'w_a_up': w_a_up[l].astype(f32),
            'w_g_up': w_g_up[l].astype(f32), 'k_k': k_k[l].astype(f32),
            'k_a': k_a[l].astype(f32), 'r_k': r_k[l].astype(f32),
            'gn_g': gn_g[l].astype(f32), 'gn_b': gn_b[l].astype(f32),
            'w_rwkv_out': w_rwkv_out[l].astype(f32), 'glu_b': glu_b[l].astype(f32),
            'conv_w': conv_w[l].astype(f32), 'conv_b': conv_b[l].astype(f32),
            'conv_ln_g': conv_ln_g[l].astype(f32), 'conv_ln_b': conv_ln_b[l].astype(f32),
            'w_conv_out': w_conv_out[l].astype(f32), 'w_mem_kv': w_mem_kv[l].astype(f32),
            'w_xattn_out': w_xattn_out[l].astype(f32), 'w_o': w_o[l].astype(f32),
            'ffn2_norm': ffn2_norm[l].astype(f32), 'ffn2_w_up': ffn2_w_up[l].astype(f32),
            'ffn2_w_down': ffn2_w_down[l].astype(f32),
        })
    g_final = final_norm.astype(f32)

    xp = x_prompt
    wkv_p, shift_p, conv_p, mk_p, mv_p = [], [], [], [], []
    for l in range(DEPTH):
        p = layers[l]
        mk, mv = _memory_kv(mem_prompt, p['w_mem_kv'])
        xp, wkv, sh, cv = _layer(
            xp, jnp.zeros((BATCH, D_SHIFT), f32), jnp.zeros((BATCH, CONV_WIDTH - 1, D_CONV), f32),
            jnp.zeros((BATCH, N_RWKV_HEADS, HEAD_DIM, HEAD_DIM), f32), mk, mv, p)
        wkv_p.append(wkv)
        shift_p.append(sh)
        conv_p.append(cv)
        mk_p.append(mk)
        mv_p.append(mv)
    y_prompt = _rmsnorm(xp, g_final).astype(x_prompt.dtype)

    xs = x_sample
    wkv_s, shift_s, conv_s = [], [], []
    for l in range(DEPTH):
        xs, wkv, sh, cv = _layer(xs, state_shift[l], state_conv[l], state_wkv[l],
                                 cache_mem_k[l], cache_mem_v[l], layers[l])
        wkv_s.append(wkv)
        shift_s.append(sh)
        conv_s.append(cv)
    y_sample = _rmsnorm(xs, g_final).astype(x_sample.dtype)

    return (y_prompt, y_sample, jnp.stack(wkv_p), jnp.stack(shift_p), jnp.stack(conv_p),
            jnp.stack(mk_p), jnp.stack(mv_p), jnp.stack(wkv_s), jnp.stack(shift_s),
            jnp.stack(conv_s))
```

```python
import contextlib
import os
import numpy as np
import concourse.bass as bass
import concourse.mybir as mybir
from concourse.bass_utils import run_bass_kernel_spmd

F32 = mybir.dt.float32
BF16 = mybir.dt.bfloat16
AF = mybir.ActivationFunctionType
ALU = mybir.AluOpType
AX = mybir.AxisListType

SD = BF16
SAME_ENGINE_SYNC = True
ARENA_WORDS = 53200
DFF = 2816
NJ = 22
C_DEC = 0.6065306597126334

R_N1, R_NM, R_N2, R_NF = 0, 8, 16, 24
R_MU = 32
R_W0, R_A0, R_KK, R_KA, R_RK, R_GG, R_GB = 47, 51, 55, 59, 63, 67, 71
R_GLU = 75
R_CW = 79
R_CB, R_LG, R_LB = 141, 143, 145
NROWS = 147


class Res:
    __slots__ = ("name", "w", "r", "excl")

    def __init__(self, name="", excl=False):
        self.name = name
        self.w = None
        self.r = {}
        self.excl = excl


class T:
    __slots__ = ("ap", "res")

    def __init__(self, ap, res):
        self.ap = ap
        self.res = res

    def __getitem__(self, idx):
        return T(self.ap[idx], self.res)

    def v(self, ap):
        return T(ap, self.res)

    def bc(self, shape):
        return T(self.ap.broadcast_to(shape), self.res)


def _ap(x):
    return x.ap if isinstance(x, T) else x


def _res(*xs):
    out = []
    for x in xs:
        if isinstance(x, T):
            out.append(x.res)
    return out


class Sched:
    def __init__(self, nc, stack):
        self.nc = nc
        self.stack = stack
        self.eng = {"pe": nc.tensor, "act": nc.scalar, "dve": nc.vector, "pool": nc.gpsimd, "sp": nc.sync}
        self.sems = {}
        self.cnt = {}
        for k in self.eng:
            self.sems[k] = stack.enter_context(nc.semaphore("sem_" + k))
            self.cnt[k] = 0
        self.known = {k: {} for k in self.eng}
        self.ninst = {k: 0 for k in self.eng}
        self.nwait = 0
        self.ndma = 0
        self.res2sem = {}
        self.dma_free = []
        self.keep = []

    def _need(self, reads, writes, e=None):
        need = {}
        for R in reads:
            if R.w is not None:
                s, v = R.w
                if need.get(s, 0) < v:
                    need[s] = v
        for R in writes:
            if R.w is not None:
                s, v = R.w
                if s != e and need.get(s, 0) < v:
                    need[s] = v
            for s, v in R.r.items():
                if s != e and need.get(s, 0) < v:
                    need[s] = v
        return need

    def _emit_waits(self, e, need):
        kn = self.known[e]
        for s, v in need.items():
            if s == e and (not SAME_ENGINE_SYNC or e in ("pe", "sp")):
                continue
            if kn.get(s, 0) >= v:
                continue
            self.eng[e].wait_ge(self.sems[s], v)
            self.nwait += 1
            kn[s] = v

    def op(self, e, fn, reads=(), writes=()):
        ex = [R for R in reads if R.excl]
        if ex:
            writes = list(writes) + [R for R in ex if R not in writes]
            reads = [R for R in reads if not R.excl]
        self._emit_waits(e, self._need(reads, writes, e))
        ins = fn(self.eng[e])
        self.cnt[e] += 1
        ins.then_inc(self.sems[e], 1)
        tok = (e, self.cnt[e])
        self.ninst[e] += 1
        for R in writes:
            R.w = tok
            R.r = {}
        for R in reads:
            if R.r.get(e, 0) < tok[1]:
                R.r[e] = tok[1]

    def dma(self, q, out, in_, reads=(), writes=(), **kw):
        self._emit_waits(q, self._need(reads, writes))
        key = writes[0] if writes else reads[0]
        sk = self.res2sem.get(id(key))
        if sk is None:
            if self.dma_free:
                sk = self.dma_free.pop()
            else:
                sk = "dma_%d" % len(self.sems)
                self.sems[sk] = self.stack.enter_context(self.nc.semaphore("sd%d" % len(self.sems)))
                self.cnt[sk] = 0
            self.res2sem[id(key)] = sk
            self.keep.append(key)
        ins = self.eng[q].dma_start(out=out, in_=in_, **kw)
        self.cnt[sk] += 16
        ins.then_inc(self.sems[sk], 16)
        tok = (sk, self.cnt[sk])
        self.ndma += 1
        for R in writes:
            R.w = tok
            R.r = {}
        for R in reads:
            if R.r.get(sk, 0) < tok[1]:
                R.r[sk] = tok[1]

    def barrier(self):
        allc = {k: v for k, v in self.cnt.items() if v > 0}
        for e in self.eng:
            self._emit_waits(e, allc)
        self.dma_free.extend(self.res2sem.values())
        self.res2sem = {}


class Builder:
    def __init__(self, nc, st, io, stage=99, dbg=None):
        self.nc = nc
        self.st = st
        self.io = io
        self.stage = stage
        self.S = Sched(nc, st)
        self.arena = st.enter_context(nc.sbuf_tensor("arena", [128, ARENA_WORDS], F32))
        self.off = 0
        self.banks = []
        for i in range(8):
            p = st.enter_context(nc.psum_tensor("bank%d" % i, [128, 512], F32))
            self.banks.append(T(p[:], Res("bank%d" % i, excl=True)))
        self.bi = 0
        self.outres = []
        self.live = []
        self.freed = []
        self.rng_of = {}

    def alloc(self, name, shape, dt=F32):
        P = shape[0]
        fs = list(shape[1:])
        n = 1
        for d in fs:
            n *= d
        words = n if dt == F32 else (n + 1) // 2
        words = (words + 7) // 8 * 8
        assert self.off + words <= ARENA_WORDS, "arena overflow at %s (%d + %d)" % (name, self.off, words)
        ap = self.arena[0:P, self.off:self.off + words]
        rng = (self.off, self.off + words)
        self.off += words
        if dt != F32:
            ap = ap.bitcast(dt)
        ap = ap[:, 0:n]
        if len(fs) > 1:
            names = "abcdef"[:len(fs)]
            pat = "p (" + " ".join(names) + ") -> p " + " ".join(names)
            ap = ap.rearrange(pat, **{names[i]: fs[i] for i in range(len(fs))})
        res = Res(name)
        self._inherit(res, rng)
        self.live.append((rng[0], rng[1], res))
        t = T(ap, res)
        self.rng_of[id(res)] = rng
        return t

    def _inherit(self, res, rng):
        for (a, b, old) in self.freed:
            if a < rng[1] and rng[0] < b:
                toks = list(old.r.items())
                if old.w is not None:
                    toks.append(old.w)
                for s, v in toks:
                    if res.r.get(s, 0) < v:
                        res.r[s] = v

    def sub(self, base, name):
        rng = self.rng_of[id(base.res)]
        res = Res(name)
        self._inherit(res, rng)
        self.live.append((rng[0], rng[1], res))
        self.rng_of[id(res)] = rng
        return res

    def mark(self):
        return self.off

    def release(self, m, hard=False):
        keep = []
        for rec in self.live:
            if rec[0] >= m:
                self.freed.append(rec)
            else:
                keep.append(rec)
        self.live = keep
        self.off = m
        if hard:
            self.S.barrier()
            self.freed = []

    def psum(self):
        b = self.banks[self.bi]
        self.bi = (self.bi + 1) % 8
        return b

    def mm(self, out, lhsT, rhs, start=True, stop=True):
        self.S.op("pe", lambda e: e.matmul(_ap(out), lhsT=_ap(lhsT), rhs=_ap(rhs), start=start, stop=stop),
                  reads=_res(lhsT, rhs), writes=_res(out))

    def tr(self, out, in_, ident):
        self.S.op("pe", lambda e: e.transpose(_ap(out), _ap(in_), _ap(ident)), reads=_res(in_, ident), writes=_res(out))

    def act(self, out, in_, func, bias=None, scale=None, accum=None):
        kw = {}
        if bias is not None:
            kw["bias"] = _ap(bias)
        if scale is not None:
            kw["scale"] = _ap(scale)
        if accum is not None:
            kw["accum_out"] = _ap(accum)
        self.S.op("act", lambda e: e.activation(out=_ap(out), in_=_ap(in_), func=func, **kw),
                  reads=_res(in_, bias, scale), writes=_res(out, accum))

    def cp(self, eng, out, in_):
        if eng == "act":
            self.S.op("act", lambda e: e.copy(out=_ap(out), in_=_ap(in_)), reads=_res(in_), writes=_res(out))
        else:
            self.S.op(eng, lambda e: e.tensor_copy(out=_ap(out), in_=_ap(in_)), reads=_res(in_), writes=_res(out))

    def tt(self, eng, out, a, b, op):
        self.S.op(eng, lambda e: e.tensor_tensor(out=_ap(out), in0=_ap(a), in1=_ap(b), op=op), reads=_res(a, b), writes=_res(out))

    def ts(self, eng, out, a, s1, op0, s2=None, op1=None):
        if op1 is None:
            self.S.op(eng, lambda e: e.tensor_scalar(out=_ap(out), in0=_ap(a), scalar1=_ap(s1), scalar2=None, op0=op0),
                      reads=_res(a, s1), writes=_res(out))
        else:
            self.S.op(eng, lambda e: e.tensor_scalar(out=_ap(out), in0=_ap(a), scalar1=_ap(s1), scalar2=_ap(s2), op0=op0, op1=op1),
                      reads=_res(a, s1, s2), writes=_res(out))

    def stt(self, out, a, scalar, b, op0, op1):
        self.S.op("dve", lambda e: e.scalar_tensor_tensor(out=_ap(out), in0=_ap(a), scalar=_ap(scalar), in1=_ap(b), op0=op0, op1=op1),
                  reads=_res(a, scalar, b), writes=_res(out))

    def recip(self, out, in_):
        self.S.op("dve", lambda e: e.reciprocal(out=_ap(out), in_=_ap(in_)), reads=_res(in_), writes=_res(out))

    def memset(self, eng, out, val):
        self.S.op(eng, lambda e: e.memset(_ap(out), val), writes=_res(out))

    def asel(self, out, in_, pattern, cmp, fill, base, cm):
        self.S.op("pool", lambda e: e.affine_select(out=_ap(out), in_=_ap(in_), pattern=pattern, compare_op=cmp, fill=fill,
                                                    base=base, channel_multiplier=cm), reads=_res(in_), writes=_res(out))

    def load(self, out, src, q="sp"):
        self.S.dma(q, _ap(out), src, writes=_res(out))

    def store(self, dst, in_, q="sp"):
        self.S.dma(q, dst, _ap(in_), reads=_res(in_))
        self.outres.append(in_.res)

    def setup(self):
        io = self.io
        A = self.alloc
        self.ident_f = A("ident_f", [128, 128])
        self.memset("pool", self.ident_f, 0.0)
        self.asel(self.ident_f, self.ident_f, [[-1, 128]], ALU.not_equal, 1.0, 0, 1)
        self.ident_b = A("ident_b", [128, 128], BF16)
        self.cp("dve", self.ident_b, self.ident_f)
        self.ident_s = self.ident_b if SD == BF16 else self.ident_f
        self.ones_b = A("ones_b", [128, 128], BF16)
        self.memset("pool", self.ones_b, 1.0)
        self.blk64 = A("blk64", [128, 128])
        self.blk1 = A("blk1", [128, 128])
        self.memset("pool", self.blk64, 0.0)
        self.memset("pool", self.blk1, 0.0)
        for lo in (0, 64):
            self.memset("pool", self.blk64[lo:lo + 64, lo:lo + 64], 1.0 / 64)
            self.memset("pool", self.blk1[lo:lo + 64, lo:lo + 64], 1.0)
        self.blk1b = A("blk1b", [128, 128], BF16)
        self.cp("pool", self.blk1b, self.blk1)
        self.ones256 = A("ones256", [128, 128])
        self.memset("pool", self.ones256, 1.0 / 256)
        self.M4p = A("M4p", [128, 4, 128])
        self.M4s = A("M4s", [128, 4, 128])
        self.seqmask = A("seqmask", [128, 16, 16, 8], SD)
        self.seqmask_b = self.seqmask
        self.rowmask = A("rowmask", [128, 16])
        self.rs_p = A("rs_p", [128, 128])
        self.rs_s = A("rs_s", [128, 16, 8])
        self.eps = A("eps", [128, 4])
        self.memset("pool", self.eps[:, 0:1], 1e-6)
        self.memset("pool", self.eps[:, 1:2], 1e-5)
        self.memset("pool", self.eps[:, 2:3], 64e-5)
        self.memset("pool", self.eps[:, 3:4], 0.0)
        self.PC = A("PC", [128, NROWS])
        self.omka = A("omka", [128, 4])
        self.xT = A("xT", [128, 8, 1152])
        self.hnT = A("hnT", [128, 8, 1152], BF16)
        self.xres = [self.sub(self.xT, "xT%d" % i) for i in range(3)]
        self.hres_ = [self.sub(self.hnT, "hnT%d" % i) for i in range(3)]
        self.H = A("H", [128, 4, 64])
        self.Hsd = A("Hsd", [128, 4, 64], SD)
        self.memset("dve", self.H, 0.0)
        self.memset("dve", self.Hsd, 0.0)
        self.carry = A("carry", [128, 15])
        self.memset("dve", self.carry, 0.0)
        self.utail = A("utail", [128, 2, 30])
        self.memset("dve", self.utail, 0.0)
        self.KTp = A("KTp", [128, 2, 256], BF16)
        self.Vp = A("Vp", [128, 2, 256], BF16)
        self.wdu = A("wdu", [64, 512])
        self.wau = A("wau", [128, 512], BF16)
        self.wgu = A("wgu", [128, 2, 512], BF16)
        self.load(self.wdu, io["w_du"])
        self.load(self.wau[64:128, :], io["w_au"], q="pool")
        self.load(self.wgu[:, 0, :], io["w_gu"][0:128, :], q="pool")
        self.load(self.wgu[0:32, 1, :], io["w_gu"][128:160, :], q="pool")
        self.xpre_mark = self.mark()
        self.xpre = self.x_prefetch(0)
        mtmp = self.mark()
        pin = A("pin", [128, 2, 128])
        self.load(pin[:, 0, :], io["pp"][0:128, :])
        self.load(pin[0:NROWS - 128, 1, :], io["pp"][128:NROWS, :])
        ps = self.psum()
        self.tr(ps[:, 0:128], pin[:, 0, :], self.ident_f)
        self.tr(ps[:, 128:128 + NROWS - 128], pin[0:NROWS - 128, 1, :], self.ident_f[0:NROWS - 128, 0:NROWS - 128])
        self.cp("dve", self.PC, ps[:, 0:NROWS])
        self.ts("dve", self.omka, self.PC[:, R_KA:R_KA + 4], -1.0, ALU.mult, 1.0, ALU.add)
        self.release(mtmp)

    def setup_late(self):
        A = self.alloc
        self.memset("pool", self.seqmask, 1.0)
        self.asel(self.seqmask, self.seqmask, [[-1, 16], [1, 16], [0, 8]], ALU.is_equal, 0.0, 0, 0)
        self.memset("pool", self.rowmask, 1.0)
        self.asel(self.rowmask, self.rowmask, [[-8, 16]], ALU.is_ge, 0.0, 0, 1)
        self.asel(self.rowmask, self.rowmask, [[8, 16]], ALU.is_ge, 0.0, 7, -1)
        self.memset("pool", self.rs_p, 1.0)
        self.memset("pool", self.rs_p[:, 0:1], 0.0)
        self.memset("pool", self.rs_s, 1.0)
        self.memset("pool", self.rs_s[:, :, 0:1], 0.0)
        mtmp = self.mark()
        su = A("su", [128, 128])
        iu = A("iu", [128, 128])
        for m in (su, iu):
            self.memset("pool", m, 1.0)
        self.asel(su, su, [[1, 128]], ALU.is_gt, 0.0, 0, -1)
        self.asel(iu, iu, [[1, 128]], ALU.is_ge, 0.0, 0, -1)
        bm = A("bm", [128, 16, 8])
        self.memset("pool", bm, 1.0)
        self.asel(bm, bm, [[-8, 16], [0, 8]], ALU.is_ge, 0.0, 0, 1)
        self.asel(bm, bm, [[8, 16], [0, 8]], ALU.is_ge, 0.0, 7, -1)
        bm2 = bm.v(bm.ap.rearrange("p a b -> p (a b)"))
        for i in range(4):
            self.cp("pool", self.M4p[:, i, :], su if i % 2 == 0 else iu)
            self.tt("pool", self.M4s[:, i, :], su if i % 2 == 0 else iu, bm2, ALU.mult)
        self.release(mtmp)

    def dump(self, name, t, n):
        import os
        if os.environ.get("DBG_BLK") is None:
            return
        dt = t.ap.dtype
        d = self.nc.dram_tensor("dbg_" + name, [t.ap.shape[0], n], dt, kind="ExternalOutput").ap()
        src_ap = t.ap
        if len(src_ap.shape) > 2:
            names = "abcdef"[:len(src_ap.shape) - 1]
            src_ap = src_ap.rearrange("p " + " ".join(names) + " -> p (" + " ".join(names) + ")")
        self.S.dma("sp", d, src_ap, reads=_res(t))
        self.outres.append(t.res)

    def xt(self, t0):
        return T(self.xT.ap, self.xres[t0 // 512])

    def hn(self, t0):
        return T(self.hnT.ap, self.hres_[t0 // 512])

    def pc(self, row):
        return self.PC[:, row:row + 1]

    def x_sources(self, blk):
        io = self.io
        if blk == 0:
            return [(io["xp"][ch * 128:(ch + 1) * 128, :], ch * 128) for ch in range(8)]
        return [(io["xp"][1024 + ch * 128:1024 + (ch + 1) * 128, :], ch * 128) for ch in range(8)] + [(io["xs"], 1024)]

    def x_prefetch(self, blk, nslots=4):
        xin = [self.alloc("xin%d" % i, [128, 1024]) for i in range(nslots)]
        srcs = self.x_sources(blk)
        for i in range(min(nslots, len(srcs))):
            self.load(xin[i], srcs[i][0], q="sp")
        return xin

    def load_x(self, blk, pre=None):
        srcs = self.x_sources(blk)
        m = self.mark()
        if pre is None:
            xin = self.x_prefetch(blk)
        else:
            xin = pre
        ns = len(xin)
        for ch, (sap, tok) in enumerate(srcs):
            xi = xin[ch % ns]
            if ch >= ns:
                self.load(xi, sap, q="pool")
            for g in range(2):
                ps = self.psum()
                for c in range(4):
                    cc = g * 4 + c
                    self.tr(ps[:, c * 128:(c + 1) * 128], xi[:, cc * 128:(cc + 1) * 128], self.ident_f)
                dst = self.xt(tok)[:, g * 4:(g + 1) * 4, tok:tok + 128]
                self.cp("act" if g == 0 else "dve", dst, ps.v(ps.ap.rearrange("p (c t) -> p c t", c=4)))
        if pre is None:
            self.release(m)

    def rmsnorm(self, row, tiles, out, local=False):
        m_ = self.mark()
        sqs = [self.alloc("sq%d" % i, [128, 8, 512], BF16) for i in range(2)]
        rstds = [self.alloc("rstd%d" % i, [128, 512]) for i in range(2)]
        for k, (t0, tn) in enumerate(tiles):
            sq, rstd = sqs[k % 2], rstds[k % 2]
            self.act(sq[:, :, 0:tn], self.xt(t0)[:, :, t0:t0 + tn], AF.Square)
            ps = self.psum()
            for c in range(8):
                self.mm(ps[:, 0:tn], self.ones_b, sq[:, c, 0:tn], start=(c == 0), stop=(c == 7))
            self.act(rstd[:, 0:tn], ps[:, 0:tn], AF.Ln, bias=self.eps[:, 0:1], scale=1.0 / 1024)
            self.act(rstd[:, 0:tn], rstd[:, 0:tn], AF.Exp, scale=-0.5)
            o0 = 0 if local else t0
            for c in range(8):
                o_ = out if local else self.hn(t0)
                self.stt(o_[:, c, o0:o0 + tn], self.xt(t0)[:, c, t0:t0 + tn], self.pc(row + c), rstd[:, 0:tn], ALU.mult, ALU.mult)
        self.release(m_)

    def ffn(self, w_up, w_dn, nrow, tiles, NT):
        io = self.io
        self.rmsnorm(nrow, tiles, self.hnT)
        m = self.mark()
        hT = self.alloc("hT", [128, NJ, 1152], BF16)
        hres = [self.sub(hT, "hT%d" % i) for i in range(len(tiles))]
        sg = [self.alloc("sg%d" % i, [128, 512], BF16) for i in range(2)]
        wdn = [self.alloc("wdn%d" % i, [128, NJ, 128], BF16) for i in range(3)]
        m_up = self.mark()
        wup = [self.alloc("wup%d" % i, [128, 8, 2, 256], BF16) for i in range(3)]
        wv = w_up.rearrange("(c p) f -> p c f", p=128)
        dv = w_dn.rearrange("(j p) d -> p j d", p=128)
        nsg = 0
        for jj in range(NJ // 2):
            wb = wup[jj % 3]
            self.load(wb[:, :, 0, :], wv[:, :, jj * 256:(jj + 1) * 256], q="pool")
            self.load(wb[:, :, 1, :], wv[:, :, DFF + jj * 256: DFF + (jj + 1) * 256], q="pool")
            if jj < 3:
                self.load(wdn[jj], dv[:, :, jj * 128:(jj + 1) * 128], q="pool")
            for j2 in range(2):
                j = jj * 2 + j2
                for ti, (t0, tn) in enumerate(tiles):
                    pg = self.psum()
                    pv = self.psum()
                    for c in range(8):
                        self.mm(pg[:, 0:tn], wb[:, c, 0, j2 * 128:(j2 + 1) * 128], self.hn(t0)[:, c, t0:t0 + tn], start=(c == 0), stop=(c == 7))
                    for c in range(8):
                        self.mm(pv[:, 0:tn], wb[:, c, 1, j2 * 128:(j2 + 1) * 128], self.hn(t0)[:, c, t0:t0 + tn], start=(c == 0), stop=(c == 7))
                    s = sg[nsg % 2]
                    nsg += 1
                    self.act(s[:, 0:tn], pg[:, 0:tn], AF.Silu)
                    self.tt("dve", T(hT.ap[:, j, t0:t0 + tn], hres[ti]), s[:, 0:tn], pv[:, 0:tn], ALU.mult)
        self.release(m_up)
        wdn = wdn + [self.alloc("wdn%d" % i, [128, NJ, 128], BF16) for i in range(3, 8)]
        for dc in range(3, 8):
            self.load(wdn[dc], dv[:, :, dc * 128:(dc + 1) * 128], q="pool")
        for ti, (t0, tn) in enumerate(tiles):
            for dc in range(8):
                wd = wdn[dc]
                ps = self.psum()
                for j in range(NJ):
                    self.mm(ps[:, 0:tn], wd[:, j, :], T(hT.ap[:, j, t0:t0 + tn], hres[ti]), start=(j == 0), stop=(j == NJ - 1))
                self.stt(self.xt(t0)[:, dc, t0:t0 + tn], ps[:, 0:tn], 0.5, self.xt(t0)[:, dc, t0:t0 + tn], ALU.mult, ALU.add)
        self.release(m)

    def final_out(self, dst, tok0, nchunks):
        m = self.mark()
        yT = self.alloc("yT", [128, 8, 512])
        yo = [self.alloc("yo%d" % i, [128, 1024]) for i in range(2)]
        done = 0
        k = 0
        while done < nchunks:
            nch = min(4, nchunks - done)
            t0 = tok0 + done * 128
            tn = nch * 128
            self.rmsnorm(R_NF, [(t0, tn)], yT, local=True)
            for ch in range(nch):
                y = yo[k % 2]
                k += 1
                for g in range(2):
                    ps = self.psum()
                    for c in range(4):
                        self.tr(ps[:, c * 128:(c + 1) * 128], yT[:, g * 4 + c, ch * 128:(ch + 1) * 128], self.ident_f)
                    self.cp("act" if g == 0 else "dve", y[:, g * 512:(g + 1) * 512], ps)
                self.store(dst[(done + ch) * 128:(done + ch + 1) * 128, :], y)
            done += nch
        self.release(m)

    def run_block(self, blk):
        io = self.io
        if blk == 0:
            tiles = [(0, 512), (512, 512)]
            NT = 1024
            self.load_x(0, self.xpre)
            self.release(self.xpre_mark)
        else:
            tiles = [(0, 512), (512, 512), (1024, 128)]
            NT = 1152
            self.load_x(1, self.xpre)
            self.release(self.xpre_mark)
        self.ffn(io["w_up1"], io["w_dn1"], R_N1, tiles, NT)
        if blk == 0:
            self.setup_late()
        if self.stage >= 2:
            self.mixer(blk, tiles, NT)
        if self.stage >= 3:
            self.ffn(io["w_up2"], io["w_dn2"], R_N2, tiles, NT)
        if blk == 0:
            self.xpre_mark = self.mark()
            self.xpre = self.x_prefetch(1)
            self.final_out(io["y_p"][0:1024, :], 0, 8)
        else:
            self.final_out(io["y_p"][1024:2048, :], 0, 8)
            self.final_out(io["y_s"], 1024, 1)
        self.release(self.mark(), hard=(os.environ.get("HARD_BLK", "0") == "1"))

    def mixer(self, blk, tiles, NT):
        self.rmsnorm(R_NM, tiles, self.hnT)
        m0 = self.mark()
        ogT = self.alloc("ogT", [128, 4, 1152], BF16)
        self.wv_in = self.io["w_in"].rearrange("(c p) f -> p c f", p=128)
        import os
        parts = os.environ.get("MIX_PARTS", "conv,xattn,rwkv,merge").split(",")
        if "rwkv" in parts:
            self.rwkv_branch(blk, ogT)
        else:
            self.memset("dve", ogT, 0.0)
        csT = self.alloc("csT", [128, 2, 1152], BF16)
        ocT = self.alloc("ocT", [128, 2, 1152], BF16)
        for nm, tl in (("conv", csT), ("xattn", ocT)):
            if nm not in parts:
                self.memset("dve", tl, 0.0)
        KTs = None
        if blk == 1 and "xattn" in parts:
            KTs = self.alloc("KTs", [128, 2, 16, 256], BF16)
            mk_ = self.mark()
            ckb = self.alloc("ckb", [128, 16, 2, 256], BF16)
            ld_ck = lambda: self.load(ckb, self.io["ck"].rearrange("q (mc p) c -> p q mc c", p=128), q="pool")
            if "conv" not in parts:
                ld_ck()
        else:
            ld_ck = None
        if "conv" in parts:
            self.conv_branch(blk, tiles, csT, ld_ck)
        if KTs is not None:
            for q in range(16):
                ps = self.psum()
                pb16 = ps.v(ps.ap.bitcast(BF16))
                for cc in range(2):
                    for mc in range(2):
                        o = cc * 256 + mc * 128
                        self.tr(pb16[:, o:o + 128], ckb[:, q, mc, cc * 128:(cc + 1) * 128], self.ident_b)
                self.cp("act" if q % 2 == 0 else "dve", KTs[:, :, q, :], pb16.v(pb16.ap[:, 0:512].rearrange("p (c m) -> p c m", c=2)))
            self.release(mk_)
        pre_w = None
        if "xattn" in parts and "merge" in parts:
            pw0 = self.alloc("wg0", [128, 8, 3, 128], BF16)
            pwr = self.alloc("wro", [128, 4, 1024], BF16)
            pre_w = (pw0, pwr)
            gv_ = self.wv_in[:, :, 2592:5664].rearrange("p c (b f) -> p c b f", b=3)

            def ld_pre():
                for b in range(3):
                    self.load(pw0[:, :, b, :], gv_[:, :, b, 0:128], q="pool")
                self.load(pwr, self.io["w_ro"].rearrange("(c p) f -> p c f", p=128), q="pool")
        else:
            ld_pre = None
        if "xattn" in parts:
            self.xattn_branch(blk, tiles, ocT, KTs, ld_pre)
        if os.environ.get("DBG_BLK") == str(blk):
            self.dump("csT", csT, 2 * 1152)
            self.dump("ocT", ocT, 2 * 1152)
            self.dump("ogT", ogT, 4 * 1152)
            self.dump("hnT", self.hnT, 8 * 1152)
        if "merge" in parts:
            self.merge(blk, tiles, csT, ocT, ogT, pre_w)
        self.release(m0)

    def conv_branch(self, blk, tiles, csT, after_wc=None):
        io = self.io
        m = self.mark()
        wc = self.alloc("wc", [128, 8, 512], BF16)
        self.load(wc, self.wv_in[:, :, 1824:2336], q="pool")
        if after_wc is not None:
            after_wc()
        uP = self.alloc("uP", [128, 2, 1054])
        cT = self.alloc("cT", [128, 2, 1152])
        sgl = self.alloc("sgl", [128, 512])
        self.cp("act", uP[:, :, 0:30], self.utail)
        if blk == 1:
            uS = self.alloc("uS", [128, 2, 16, 38])
            sc = self.alloc("sc", [120, 4, 256])
            self.load(sc, io["sconv"].rearrange("(g r) c -> r g c", r=120))
            for g in range(4):
                ps = self.psum()
                for ch in range(2):
                    self.tr(ps[:, ch * 120:(ch + 1) * 120], sc[0:120, g, ch * 128:(ch + 1) * 128], self.ident_f[0:120, 0:120])
                self.cp("act", uS[:, :, 4 * g:4 * g + 4, 0:30], ps.v(ps.ap[:, 0:240].rearrange("p (c q t) -> p c q t", c=2, q=4)))
        for (t0, tn) in tiles:
            for ch in range(2):
                pa = self.psum()
                pb = self.psum()
                for c in range(8):
                    self.mm(pa[:, 0:tn], wc[:, c, ch * 128:(ch + 1) * 128], self.hn(t0)[:, c, t0:t0 + tn], start=(c == 0), stop=(c == 7))
                for c in range(8):
                    self.mm(pb[:, 0:tn], wc[:, c, 256 + ch * 128:256 + (ch + 1) * 128], self.hn(t0)[:, c, t0:t0 + tn], start=(c == 0), stop=(c == 7))
                self.act(sgl[:, 0:tn], pb[:, 0:tn], AF.Sigmoid, bias=self.pc(R_GLU + 2 + ch))
                if t0 < 1024:
                    self.stt(uP[:, ch, 30 + t0:30 + t0 + tn], pa[:, 0:tn], self.pc(R_GLU + ch), sgl[:, 0:tn], ALU.add, ALU.mult)
                else:
                    self.stt(uS[:, ch, :, 30:38], pa.v(pa.ap[:, 0:128].rearrange("p (q t) -> p q t", q=16)), self.pc(R_GLU + ch),
                             sgl.v(sgl.ap[:, 0:128].rearrange("p (q t) -> p q t", q=16)), ALU.add, ALU.mult)
        uPb = self.alloc("uPb", [128, 2, 1054], BF16)
        self.cp("act", uPb[:, 0, :], uP[:, 0, :])
        self.cp("dve", uPb[:, 1, :], uP[:, 1, :])
        if blk == 1:
            uSb = self.alloc("uSb", [128, 2, 16, 38], BF16)
            self.cp("act", uSb, uS)
        dg = [self.alloc("dg%d" % i, [128, 128], BF16) for i in range(4)]
        nd = 0
        for ch in range(2):
            pts = [self.psum(), self.psum()]
            pss_ = self.psum() if blk == 1 else None
            for w in range(31):
                d = dg[nd % 4]
                nd += 1
                self.ts("dve", d, self.ident_b, self.pc(R_CW + 2 * w + ch), ALU.mult)
                for ti in range(2):
                    self.mm(pts[ti], d, uPb[:, ch, w + ti * 512:w + ti * 512 + 512], start=(w == 0), stop=(w == 30))
                if blk == 1:
                    self.mm(pss_[:, 0:128], d, uSb[:, ch, :, w:w + 8], start=(w == 0), stop=(w == 30))
            for ti in range(2):
                self.act(cT[:, ch, ti * 512:(ti + 1) * 512], pts[ti], AF.Identity, bias=self.pc(R_CB + ch))
            if blk == 1:
                self.act(cT[:, ch, 1024:1152], pss_[:, 0:128], AF.Identity, bias=self.pc(R_CB + ch))
        self.cp("act", self.utail, uP[:, :, 1024:1054])
        if blk == 1:
            cvo = self.alloc("cvo", [30, 256])
            ps = self.psum()
            for ch in range(2):
                self.tr(ps[0:30, ch * 128:(ch + 1) * 128], uP[:, ch, 1024:1054], self.ident_f)
            self.cp("act", cvo, ps[0:30, 0:256])
            self.store(io["conv_p"], cvo)
            cso = self.alloc("cso", [120, 4, 256])
            tmpc = self.alloc("tmpc", [128, 2, 120])
            for g in range(4):
                ps = self.psum()
                self.cp("act", tmpc.v(tmpc.ap.rearrange("p c (q t) -> p c q t", q=4)), uS[:, :, 4 * g:4 * g + 4, 8:38])
                for ch in range(2):
                    self.tr(ps[0:120, ch * 128:(ch + 1) * 128], tmpc[:, ch, :], self.ident_f)
                self.cp("act", cso[:, g, :], ps[0:120, 0:256])
            self.store(io["conv_s"].rearrange("(g r) c -> r g c", r=120), cso)
        nt_ = len(tiles)
        sqf = [self.alloc("sqf%d" % i, [128, 2, 512]) for i in range(nt_)]
        rsd = [self.alloc("rsd%d" % i, [128, 512]) for i in range(nt_)]
        cres = [self.sub(cT, "cT%d" % i) for i in range(nt_)]
        cTt = [T(cT.ap, cres[i]) for i in range(nt_)]
        pms = []
        for i, (t0, tn) in enumerate(tiles):
            pm = self.psum()
            for ch in range(2):
                self.mm(pm[:, 0:tn], self.ones256, cT[:, ch, t0:t0 + tn], start=(ch == 0), stop=(ch == 1))
            pms.append(pm)
        for i, (t0, tn) in enumerate(tiles):
            for ch in range(2):
                self.tt("dve", cTt[i][:, ch, t0:t0 + tn], cT[:, ch, t0:t0 + tn], pms[i][:, 0:tn], ALU.subtract)
            self.act(sqf[i][:, :, 0:tn], cTt[i][:, :, t0:t0 + tn], AF.Square)
        pvs = []
        for i, (t0, tn) in enumerate(tiles):
            pv = self.psum()
            for ch in range(2):
                self.mm(pv[:, 0:tn], self.ones256, sqf[i][:, ch, 0:tn], start=(ch == 0), stop=(ch == 1))
            pvs.append(pv)
        for i, (t0, tn) in enumerate(tiles):
            self.act(rsd[i][:, 0:tn], pvs[i][:, 0:tn], AF.Ln, bias=self.eps[:, 1:2])
            self.act(rsd[i][:, 0:tn], rsd[i][:, 0:tn], AF.Exp, scale=-0.5)
        for i, (t0, tn) in enumerate(tiles):
            for ch in range(2):
                self.tt("dve", cTt[i][:, ch, t0:t0 + tn], cTt[i][:, ch, t0:t0 + tn], rsd[i][:, 0:tn], ALU.mult)
                self.act(csT[:, ch, t0:t0 + tn], cTt[i][:, ch, t0:t0 + tn], AF.Silu, bias=self.pc(R_LB + ch), scale=self.pc(R_LG + ch))
        self.release(m)

    def xattn_branch(self, blk, tiles, ocT, KTs=None, after_loads=None):
        io = self.io
        m = self.mark()
        wq = self.alloc("wq", [128, 8, 256], BF16)
        self.load(wq, self.wv_in[:, :, 2336:2592], q="pool")
        qT = self.alloc("qT", [128, 2, 1152], BF16)
        for (t0, tn) in tiles:
            for cc in range(2):
                ps = self.psum()
                for c in range(8):
                    self.mm(ps[:, 0:tn], wq[:, c, cc * 128:(cc + 1) * 128], self.hn(t0)[:, c, t0:t0 + tn], start=(c == 0), stop=(c == 7))
                self.cp("act", qT[:, cc, t0:t0 + tn], ps[:, 0:tn])
        if blk == 0:
            m1 = self.mark()
            memT = self.alloc("memT", [128, 8, 256], BF16)
            wkv = self.alloc("wkv", [128, 8, 512], BF16)
            self.load(wkv, io["w_kv"].rearrange("(c p) f -> p c f", p=128), q="pool")
            mi = self.alloc("mi", [128, 2, 1024])
            kvo = self.alloc("kvo", [128, 2, 512])
            for mc in range(2):
                self.load(mi[:, mc, :], io["mem"][mc * 128:(mc + 1) * 128, :])
                for g in range(2):
                    ps = self.psum()
                    for c in range(4):
                        self.tr(ps[:, c * 128:(c + 1) * 128], mi[:, mc, (g * 4 + c) * 128:(g * 4 + c + 1) * 128], self.ident_f)
                    self.cp("act", memT[:, g * 4:(g + 1) * 4, mc * 128:(mc + 1) * 128], ps.v(ps.ap.rearrange("p (c t) -> p c t", c=4)))
            for cc in range(2):
                ps = self.psum()
                for c in range(8):
                    self.mm(ps[:, 0:256], wkv[:, c, cc * 128:(cc + 1) * 128], memT[:, c, :], start=(c == 0), stop=(c == 7))
                self.cp("act", self.KTp[:, cc, :], ps[:, 0:256])
            for mc in range(2):
                ps = self.psum()
                for c in range(8):
                    self.mm(ps, memT[:, c, mc * 128:(mc + 1) * 128], wkv[:, c, :], start=(c == 0), stop=(c == 7))
                self.cp("act", kvo[:, mc, :], ps)
                self.cp("dve", self.Vp[:, mc, :], ps[:, 256:512])
                self.store(io["mk_p"][mc * 128:(mc + 1) * 128, :], kvo[:, mc, 0:256])
                self.store(io["mv_p"][mc * 128:(mc + 1) * 128, :], kvo[:, mc, 256:512])
            self.release(m1)
        WT = []
        for i in range(2):
            WT.append(dict(mx=self.alloc("mx%d" % i, [128, 4]), nmx=self.alloc("nmx%d" % i, [128, 4]), rsum=self.alloc("rsum%d" % i, [128, 4]),
                           rinv=self.alloc("rinv%d" % i, [128, 4]), Pb=self.alloc("Pb%d" % i, [128, 4, 256], BF16),
                           PT=self.alloc("PT%d" % i, [128, 8, 128], BF16),
                           octok=self.alloc("octok%d" % i, [128, 256], BF16)))
        chunks = [(ch * 128, False) for ch in range(8)]
        if blk == 1:
            chunks.append((1024, True))
            cvb = self.alloc("cvb", [128, 16, 2, 256], BF16)
            self.load(cvb, io["cv"].rearrange("q (mc p) c -> p q mc c", p=128), q="pool")
        if after_loads is not None:
            after_loads()
        if blk == 1:
            qmask = self.alloc("qmask", [128, 2, 16, 128], BF16)
            PTm = [self.alloc("PTm%d" % i, [128, 2, 16, 128], BF16) for i in range(2)]
            smv = self.seqmask_b.v(self.seqmask_b.ap.rearrange("p q a b -> p q (a b)"))
        def xchunk(t0, samp, W):
            mx, nmx, rsum, rinv, Pb, PT, octok = W["mx"], W["nmx"], W["rsum"], W["rinv"], W["Pb"], W["PT"], W["octok"]
            psA = [self.psum(), self.psum()]
            if samp:
                for cc in range(2):
                    self.tt("dve", qmask[:, cc], qT[:, cc:cc + 1, t0:t0 + 128].bc([128, 16, 128]), smv, ALU.mult)
            for h in range(4):
                rows = slice((h % 2) * 64, (h % 2) * 64 + 64)
                out = psA[h % 2][:, (h // 2) * 256:(h // 2) * 256 + 256]
                if not samp:
                    self.mm(out, qT[rows, h // 2, t0:t0 + 128], self.KTp[rows, h // 2, :])
                else:
                    for q in range(16):
                        self.mm(out, qmask[rows, h // 2, q, :], KTs[rows, h // 2, q, :], start=(q == 0), stop=(q == 15))
            for i in range(2):
                self.S.op("dve", lambda e: e.tensor_reduce(out=_ap(mx[:, i:4:2]), in_=psA[i].ap.rearrange("p (h m) -> p h m", h=2),
                                                           axis=AX.X, op=ALU.max), reads=_res(psA[i]), writes=_res(mx))
            self.ts("dve", nmx, mx, -0.125, ALU.mult)
            for h in range(4):
                self.act(Pb[:, h, :], psA[h % 2][:, (h // 2) * 256:(h // 2) * 256 + 256], AF.Exp, bias=nmx[:, h:h + 1], scale=0.125)
            self.S.op("dve", lambda e: e.tensor_reduce(out=_ap(rsum), in_=_ap(Pb), axis=AX.X, op=ALU.add), reads=_res(Pb), writes=_res(rsum))
            self.recip(rinv, rsum)
            yield
            ps = self.psum()
            pb16 = ps.v(ps.ap.bitcast(BF16))
            for h in range(4):
                for mc in range(2):
                    j = h * 2 + mc
                    self.tr(pb16[:, j * 128:(j + 1) * 128], Pb[:, h, mc * 128:(mc + 1) * 128], self.ident_b)
            self.cp("act", PT, pb16.v(pb16.ap.rearrange("p (j t) -> p j t", j=8)))
            yield
            pso = self.psum()
            for h in range(4):
                o = pso[:, h * 64:(h + 1) * 64]
                if not samp:
                    for mc in range(2):
                        self.mm(o, PT[:, h * 2 + mc, :], self.Vp[:, mc, h * 64:(h + 1) * 64], start=(mc == 0), stop=(mc == 1))
                else:
                    pm = PTm[h % 2]
                    for mc in range(2):
                        self.tt("dve", pm[:, mc], PT[:, h * 2 + mc:h * 2 + mc + 1, :].bc([128, 16, 128]), smv, ALU.mult)
                    n = 0
                    for q in range(16):
                        for mc in range(2):
                            self.mm(o, pm[:, mc, q, :], cvb[:, q, mc, h * 64:(h + 1) * 64], start=(n == 0), stop=(n == 31))
                            n += 1
            self.tt("dve", octok.v(octok.ap.rearrange("p (h d) -> p h d", h=4)), pso.v(pso.ap[:, 0:256].rearrange("p (h d) -> p h d", h=4)),
                    rinv.v(rinv.ap.rearrange("p (h o) -> p h o", o=1)).bc([128, 4, 64]), ALU.mult)
            yield
            ps2 = self.psum()
            p216 = ps2.v(ps2.ap.bitcast(BF16))
            for cc in range(2):
                self.tr(p216[:, cc * 128:(cc + 1) * 128], octok[:, cc * 128:(cc + 1) * 128], self.ident_b)
            self.cp("act", ocT[:, :, t0:t0 + 128], p216.v(p216.ap[:, 0:256].rearrange("p (c t) -> p c t", c=2)))
            yield
        gens = [xchunk(t0, samp, WT[i % 2]) for i, (t0, samp) in enumerate(chunks)]
        active = []
        gi = 0
        while gi < len(gens) or active:
            while len(active) < 2 and gi < len(gens):
                active.append(gens[gi])
                gi += 1
            for g in list(active):
                try:
                    next(g)
                except StopIteration:
                    active.remove(g)
        self.release(m)

    def merge(self, blk, tiles, csT, ocT, ogT, pre_w=None):
        io = self.io
        m = self.mark()
        wro = pre_w[1] if pre_w is not None else self.alloc("wro", [128, 4, 1024], BF16)
        wco = self.alloc("wco", [128, 2, 1024], BF16)
        wxo = self.alloc("wxo", [128, 2, 1024], BF16)
        wo = self.alloc("wo", [128, 8, 1024], BF16)
        wg = [pre_w[0] if (i == 0 and pre_w is not None) else self.alloc("wg%d" % i, [128, 8, 3, 128], BF16) for i in range(3)]
        mT = self.alloc("mT", [128, 8, 1152], BF16)
        gs = [self.alloc("gs%d" % i, [128, 512], BF16) for i in range(6)]
        tm = [self.alloc("tm%d" % i, [128, 512]) for i in range(6)]
        gv = self.wv_in[:, :, 2592:5664].rearrange("p c (b f) -> p c b f", b=3)
        ng = 0
        nt = 0
        def ldg(dc_):
            for b in range(3):
                self.load(wg[dc_ % 3][:, :, b, :], gv[:, :, b, dc_ * 128:(dc_ + 1) * 128], q="pool")
        if pre_w is None:
            ldg(0)
            self.load(wro, io["w_ro"].rearrange("(c p) f -> p c f", p=128), q="pool")
        self.load(wco, io["w_co"].rearrange("(c p) f -> p c f", p=128), q="pool")
        self.load(wxo, io["w_xo"].rearrange("(c p) f -> p c f", p=128), q="pool")
        ldg(1)
        self.load(wo, io["w_o"].rearrange("(c p) f -> p c f", p=128), q="pool")
        for dc in range(8):
            w = wg[dc % 3]
            if dc + 2 < 8:
                ldg(dc + 2)
            for (t0, tn) in tiles:
                g3 = []
                for b in range(3):
                    ps = self.psum()
                    for c in range(8):
                        self.mm(ps[:, 0:tn], w[:, c, b, :], self.hn(t0)[:, c, t0:t0 + tn], start=(c == 0), stop=(c == 7))
                    g = gs[ng % 6]
                    ng += 1
                    self.act(g[:, 0:tn], ps[:, 0:tn], AF.Sigmoid)
                    g3.append(g)
                ys = []
                for (wt, src, nk) in ((wro, ogT, 4), (wco, csT, 2), (wxo, ocT, 2)):
                    ps = self.psum()
                    for c in range(nk):
                        self.mm(ps[:, 0:tn], wt[:, c, dc * 128:(dc + 1) * 128], src[:, c, t0:t0 + tn], start=(c == 0), stop=(c == nk - 1))
                    ys.append(ps)
                t0_ = tm[nt % 6]
                t1_ = tm[(nt + 1) % 6]
                t2_ = tm[(nt + 2) % 6]
                nt += 3
                self.tt("dve", t0_[:, 0:tn], g3[0][:, 0:tn], ys[0][:, 0:tn], ALU.mult)
                self.tt("dve", t1_[:, 0:tn], g3[1][:, 0:tn], ys[1][:, 0:tn], ALU.mult)
                self.tt("dve", t2_[:, 0:tn], g3[2][:, 0:tn], ys[2][:, 0:tn], ALU.mult)
                self.tt("dve", t0_[:, 0:tn], t0_[:, 0:tn], t1_[:, 0:tn], ALU.add)
                self.tt("dve", mT[:, dc, t0:t0 + tn], t0_[:, 0:tn], t2_[:, 0:tn], ALU.add)
        import os
        if os.environ.get("DBG_BLK") == str(blk):
            self.dump("mT", mT, 8 * 1152)
        for (t0, tn) in tiles:
            for dc in range(8):
                ps = self.psum()
                for c in range(8):
                    self.mm(ps[:, 0:tn], wo[:, c, dc * 128:(dc + 1) * 128], mT[:, c, t0:t0 + tn], start=(c == 0), stop=(c == 7))
                self.tt("dve", self.xt(t0)[:, dc, t0:t0 + tn], self.xt(t0)[:, dc, t0:t0 + tn], ps[:, 0:tn], ALU.add)
        self.release(m)

    def rwkv_branch(self, blk, ogT):
        io = self.io
        mtop = self.mark()
        wzs = self.alloc("wzs", [128, 8, 1920], BF16)
        wzg = [None] * 4
        for gi_, (c0, cn) in ((3, (1536, 288)), (1, (512, 512)), (0, (0, 512)), (2, (1024, 512))):
            tg = T(wzs.ap, self.sub(wzs, "wzs%d" % c0))
            if c0 == 1536:
                self.memset("dve", T(wzs.ap[:, :, 1824:1920], tg.res), 0.0)
            self.S.dma("pool", wzs.ap[:, :, c0:c0 + cn], self.wv_in[:, :, c0:c0 + cn], writes=[tg.res])
            wzg[gi_] = tg
        Ks = []
        for i in range(2):
            K = {}
            for nm, shp, dt in (("arT", [128, 4, 2, 128], SD), ("btT", [128, 4, 128], SD), ("ktT", [128, 4, 128], SD),
                                ("vsd", [128, 4, 128], SD), ("Et", [128, 4, 128], F32), ("gT", [128, 4, 128], F32),
                                ("bv", [128, 4, 128], F32), ("Vtok", [128, 512], SD), ("Btok", [128, 512], SD),
                                ("Ktok", [128, 512], SD), ("oT", [128, 4, 128], F32), ("Usd", [128, 512], SD)):
                K[nm] = self.alloc(nm + str(i), shp, dt)
            K["wzs"] = wzg
            K["ogT"] = ogT
            Ks.append(K)
        K = Ks[0]
        mp = self.mark()
        cache = {}
        alias = {"Sp": "aT"}

        def A_cached(name, shape, dt=F32):
            if name in alias:
                base = cache[alias[name]]
                ap = base.ap
                if len(ap.shape) > 2:
                    names = "abcdef"[:len(ap.shape) - 1]
                    ap = ap.rearrange("p " + " ".join(names) + " -> p (" + " ".join(names) + ")")
                return T(ap[0:shape[0], 0:shape[1]], base.res)
            if name not in cache:
                cache[name] = self.alloc(name, shape, dt)
            return cache[name]

        gens = [self._rwkv_chunk(Ks[ch % 2], A_cached, False, ch * 128, False, blk == 1 and ch == 7, None) for ch in range(8)]

        def run_until(g, tag):
            for t in g:
                if t == tag:
                    return

        st = [0] * 9

        def step(k):
            if st[k] == 4:
                return
            try:
                t = next(gens[k])
            except StopIteration:
                st[k] = 4
                return
            if t == "ZS_done":
                st[k] = 1
            elif t == "XM_done":
                st[k] = 2
            elif t == "R1_done":
                st[k] = 3
        st[8] = 4
        while st[0] < 3:
            step(0)
        for i in range(8):
            while st[i] < 4 or (i + 1 < 8 and st[i + 1] < 3):
                if st[i] < 4:
                    step(i)
                if i + 1 < 8 and st[i + 1] < 3:
                    step(i + 1)
                if i + 2 < 8 and st[i + 1] >= 2 and st[i + 2] < 1:
                    step(i + 2)
        self.release(mp)
        if blk == 1:
            A = self.alloc
            H0f = A("H0f", [128, 16, 4, 64])
            ssT = A("ssT", [128, 15, 16])
            lastc = A("lastc", [128, 15, 16])
            m1 = self.mark()
            sst = A("sst", [16, 1824])
            self.load(sst, io["sshift"])
            ps = self.psum()
            for cc in range(15):
                n = 128 if cc < 14 else 32
                self.tr(ps[0:n, cc * 16:(cc + 1) * 16], sst[0:16, cc * 128:cc * 128 + n], self.ident_f[0:16, 0:16])
            self.cp("act", ssT[:, 0:14, :], ps.v(ps.ap[:, 0:224].rearrange("p (c q) -> p c q", c=14)))
            self.cp("act", ssT[0:32, 14, :], ps[0:32, 224:240])
            Sall = A("Sall", [64, 16, 8, 64])
            sv = io["swkv"].rearrange("q h v k -> v q h k")
            for g in range(4):
                self.load(T(Sall.ap[:, 4 * g:4 * g + 4], self.sub(Sall, "Sall%d" % g)), sv[:, 4 * g:4 * g + 4])
            self.S.barrier()
            for q0 in range(0, 16, 2):
                ps = self.psum()
                for qi in range(2):
                    for c in range(4):
                        g = qi * 4 + c
                        self.tr(ps[:, g * 64:(g + 1) * 64], Sall.v(Sall.ap[:, q0 + qi, 2 * c:2 * c + 2, :].rearrange("p h k -> p (h k)")),
                                self.ident_f[0:64, 0:64])
                self.cp("act" if (q0 // 2) % 2 == 0 else "dve", H0f[:, q0:q0 + 2], ps.v(ps.ap.rearrange("p (q c v) -> p q c v", q=2, c=4)))
            self.release(m1)
            for _ in self._rwkv_chunk(K, self.alloc, True, 1024, True, False, (H0f, ssT, lastc)):
                pass
        self.release(mtop)
        if blk == 1:
            So = self.alloc("So", [64, 16, 8, 64])
            for q in range(16):
                ps = self.psum()
                for c in range(4):
                    self.tr(ps[0:64, c * 128:(c + 1) * 128], H0f[:, q, c, :], self.ident_f)
                self.cp("act" if q % 2 == 0 else "dve", So.v(So.ap[:, q].rearrange("p h k -> p (h k)")), ps[0:64, :])
                if q % 4 == 3:
                    self.store(io["wkv_s"].rearrange("q h v k -> v q h k")[:, q - 3:q + 1], So[:, q - 3:q + 1])
            self.release(mtop)

    def _rwkv_chunk(self, K, A, scoped, t0, samp, last_prompt, sx):
        io = self.io
        arT, btT, ktT, vsd, Et, gT, bv = K["arT"], K["btT"], K["ktT"], K["vsd"], K["Et"], K["gT"], K["bv"]
        Vtok, Btok, Ktok, oT, Usd, wzs, ogT = K["Vtok"], K["Btok"], K["Ktok"], K["oT"], K["Usd"], K["wzs"], K["ogT"]
        if samp:
            H0f, ssT, lastc = sx
        mk = (lambda: self.mark()) if scoped else (lambda: None)
        rl = (lambda m: self.release(m)) if scoped else (lambda m: None)
        m1 = mk()
        GR = {"L": [12, 13, 14], "K": [4, 5, 6, 7], "R": [0, 1, 2, 3], "V": [8, 9, 10, 11]}
        zs = {g: A("zs" + g, [128, len(GR[g]), 144]) for g in GR}
        xm = {g: A("xm" + g, [128, len(GR[g]), 128]) for g in GR}
        dzt = [A("dzt%d" % i, [128, 128]) for i in range(2)]
        tw = A("tw", [64, 128])
        sg0 = A("sg0", [128, 128], BF16)
        sg1 = A("sg1", [32, 128], BF16)
        adb = A("adb", [128, 128], BF16)
        sgw = A("sgw", [128, 4, 128])
        aT = A("aT", [128, 4, 128])
        cs = A("cs", [128, 4, 128])
        Ei = A("Ei", [128, 4, 128])
        Ep = A("Ep", [128, 4, 128])
        rn = A("rn", [128, 4, 128])
        kk = A("kk", [128, 4, 128])
        kh = A("kh", [128, 4, 128])
        sqb = A("sqb", [128, 4, 128], BF16)
        carry3 = self.carry.v(self.carry.ap.rearrange("p (c o) -> p c o", o=1))
        if samp:
            v3 = lambda t: t.v(t.ap.rearrange("p (q l) -> p q l", l=8))
            zq = {g: zs[g].v(zs[g].ap.rearrange("p c (q l) -> p c q l", l=9)) for g in GR}
            ss4 = ssT.v(ssT.ap.rearrange("p c (q o) -> p c q o", o=1))
            lc4 = lastc.v(lastc.ap.rearrange("p c (q o) -> p c q o", o=1))
        else:
            v3 = lambda t: t
        ndz = [0]

        def zs_group(g):
            ccs = GR[g]
            c0 = ccs[0]
            if not samp:
                self.cp("act", zs[g][:, :, 0:1], carry3[:, c0:c0 + len(ccs), :])
            else:
                self.cp("act", zq[g][:, :, :, 0:1], ss4[:, c0:c0 + len(ccs)])
            for i, cc in enumerate(ccs):
                n = 128 if cc < 14 else 32
                P = slice(0, n)
                ps = self.psum()
                for c in range(8):
                    self.mm(ps[:, 0:128], wzs[cc // 4][:, c, cc * 128:(cc + 1) * 128], self.hn(t0)[:, c, t0:t0 + 128], start=(c == 0), stop=(c == 7))
                if not samp:
                    self.cp("act", zs[g][P, i, 1:129], ps[P, 0:128])
                else:
                    self.cp("act", zq[g][P, i, :, 1:9], v3(ps[P, 0:128]))
                yield "s"
            if not samp:
                self.cp("act", carry3[:, c0:c0 + len(ccs), :], zs[g][:, :, 128:129])
            else:
                self.cp("act", lc4[:, c0:c0 + len(ccs)], zq[g][:, :, :, 8:9])

        def xm_group(g):
            for i, cc in enumerate(GR[g]):
                n = 128 if cc < 14 else 32
                P = slice(0, n)
                d = dzt[ndz[0] % 2]
                ndz[0] += 1
                if not samp:
                    cur_, prv_ = zs[g][P, i, 1:129], zs[g][P, i, 0:128]
                else:
                    cur_, prv_ = zq[g][P, i, :, 1:9], zq[g][P, i, :, 0:8]
                self.tt("dve", v3(d[P, :]), prv_, cur_, ALU.subtract)
                self.stt(v3(xm[g][P, i, :]), v3(d[P, :]), self.pc(R_MU + cc)[P], cur_, ALU.mult, ALU.add)

        r_, k_, v_, xl = xm["R"], xm["K"], xm["V"], xm["L"]
        yield from zs_group("L")
        yield from zs_group("K")
        yield from zs_group("R")
        yield from zs_group("V")
        yield "ZS_done"
        xm_group("L")
        self.act(tw, xl[0:64, 0, :], AF.Tanh)
        self.act(sg0, xl[:, 1, :], AF.Sigmoid)
        self.act(sg1, xl[0:32, 2, :], AF.Sigmoid)
        self.cp("act", adb[64:128, :], xl[64:128, 0, :])
        xm_group("K")
        yield "s"
        psW = self.psum()
        psA = self.psum()
        psG = self.psum()
        for c in range(4):
            cs_ = slice(c * 128, (c + 1) * 128)
            self.mm(psW[:, cs_], self.wdu[0:64, cs_], tw)
            self.mm(psA[:, cs_], self.wau[64:128, cs_], adb[64:128, :])
            self.mm(psG[:, cs_], self.wgu[:, 0, cs_], sg0, start=True, stop=False)
            self.mm(psG[:, cs_], self.wgu[0:32, 1, cs_], sg1, start=False, stop=True)
        for c in range(4):
            cs_ = slice(c * 128, (c + 1) * 128)
            self.act(sgw[:, c, :], psW[:, cs_], AF.Sigmoid, bias=self.pc(R_W0 + c))
            self.act(aT[:, c, :], psA[:, cs_], AF.Sigmoid, bias=self.pc(R_A0 + c))
        self.cp("act", gT, psG.v(psG.ap.rearrange("p (c t) -> p c t", c=4)))
        yield "s"
        for c in range(4):
            self.act(sqb[:, c, :], k_[:, c, :], AF.Square, scale=self.pc(R_KK + c))
        xm_group("R")
        xm_group("V")
        yield "XM_done"
        ps = self.psum()
        self.mm(ps, self.blk1b, sqb.v(sqb.ap.rearrange("p c t -> p (c t)")))
        rnf = rn.v(rn.ap.rearrange("p c t -> p (c t)"))
        self.ts("dve", rnf, ps, 1e-24, ALU.max)
        self.act(rnf, rnf, AF.Ln)
        self.act(rnf, rnf, AF.Exp, scale=-0.5)
        yield "s"
        rsm = self.rs_s.v(self.rs_s.ap.rearrange("p q l -> p (q l)")) if samp else self.rs_p
        for c in range(4):
            self.S.op("dve", lambda e: e.tensor_tensor_scan(out=_ap(cs[:, c, :]), data0=_ap(rsm), data1=_ap(sgw[:, c, :]), initial=0.0,
                                                            op0=ALU.mult, op1=ALU.add), reads=_res(rsm, sgw), writes=_res(cs))
        self.act(Et, cs, AF.Exp, scale=-C_DEC)
        self.act(Ei, cs, AF.Exp, scale=C_DEC)
        self.tt("dve", sgw, cs, sgw, ALU.subtract)
        self.act(Ep, sgw, AF.Exp, scale=-C_DEC)
        yield "s"
        for c in range(4):
            self.stt(kk[:, c, :], k_[:, c, :], self.pc(R_KK + c), rn[:, c, :], ALU.mult, ALU.mult)
        for c in range(4):
            self.ts("dve", kh[:, c, :], aT[:, c, :], self.pc(R_KA + c), ALU.mult, self.omka[:, c:c + 1], ALU.add)
        self.tt("dve", kh, kh, k_, ALU.mult)
        yield "s"
        self.tt("dve", rn, kk, aT, ALU.mult)
        self.stt(arT[:, :, 0, :], kk, -1.0, Ep, ALU.mult, ALU.mult)
        self.tt("dve", arT[:, :, 1, :], r_, Et, ALU.mult)
        self.tt("dve", btT, rn, Ei, ALU.mult)
        self.tt("dve", ktT, kh, Ei, ALU.mult)
        self.cp("act", vsd, v_)
        yield "s"
        for c in range(4):
            self.stt(sqb[:, c, :], r_[:, c, :], self.pc(R_RK + c), kh[:, c, :], ALU.mult, ALU.mult)
        ps = self.psum()
        self.mm(ps, self.blk1b, sqb.v(sqb.ap.rearrange("p c t -> p (c t)")))
        self.tt("dve", bv, ps.v(ps.ap.rearrange("p (c t) -> p c t", c=4)), v_, ALU.mult)
        yield "s"
        for (src_, dst) in ((vsd, Vtok), (btT, Btok), (ktT, Ktok)):
            ps = self.psum()
            pv_ = ps.v(ps.ap.bitcast(SD)) if SD != F32 else ps
            for c in range(4):
                self.tr(pv_[:, c * 128:(c + 1) * 128], src_[:, c, :], self.ident_s)
            self.cp("act", dst, pv_[:, 0:512])
            yield "s"
        rl(m1)
        yield "R1_done"
        m2 = mk()
        AR = A("AR", [128, 8, 512], SD)
        Pk = [[A("P%d_%d" % (i, g), [128, 4, 128], SD) for g in range(2)] for i in range(2)]
        Ptk = [[A("Pt%d_%d" % (i, g), [128, 4, 128], SD) for g in range(2)] for i in range(2)]
        TT = [A("TT%d" % g, [128, 4, 128], SD) for g in range(2)]
        Xsd = A("Xsd", [128, 512], SD)
        M4 = self.M4s if samp else self.M4p
        M4f = M4.v(M4.ap.rearrange("p a t -> p (a t)"))
        if samp:
            smv = self.seqmask.v(self.seqmask.ap.rearrange("p q a b -> p q (a b)"))
            amk = [A("amk%d" % i, [128, 16, 128], SD) for i in range(2)]
            rmk = [A("rmk%d" % i, [128, 16, 128], SD) for i in range(2)]
            H0s = [A("H0s%d" % i, [128, 16, 64], SD) for i in range(2)]
        hrows = lambda h: slice((h % 2) * 64, (h % 2) * 64 + 64)
        idb = self.ident_s.v(self.ident_s.ap.rearrange("p (o t) -> p o t", o=1)).bc([128, 4, 128])
        for g in range(2):
            pss = []
            for hi in range(4):
                h = g * 4 + hi
                c = h // 2
                rows = hrows(h)
                ps = self.psum()
                rhs = arT.v(arT.ap[rows, c].rearrange("p a t -> p (a t)"))
                self.mm(ps[:, 0:256], btT[rows, c, :], rhs)
                self.mm(ps[:, 256:512], ktT[rows, c, :], rhs)
                pss.append(ps)
            for hi in range(4):
                self.tt("dve", AR[:, g * 4 + hi, :], pss[hi], M4f, ALU.mult)
            ps2 = self.psum()
            p2 = ps2.v(ps2.ap.bitcast(SD)) if SD != F32 else ps2
            for hi in range(4):
                self.tr(p2[:, hi * 128:(hi + 1) * 128], AR[:, g * 4 + hi, 0:128], self.ident_s)
            self.cp("act", Pk[0][g], p2.v(p2.ap[:, 0:512].rearrange("p (h t) -> p h t", h=4)))
            self.tt("dve", TT[g], AR[:, g * 4:g * 4 + 4, 0:128], idb, ALU.add)
            yield "s"
        L = 3 if samp else 7
        for j in range(1, L + 1):
            cur_i, prev_i = j % 2, (j - 1) % 2
            for g in range(2):
                bP = self.psum() if j <= L - 1 else None
                bPt = self.psum() if j <= L - 2 else None
                bT = self.psum() if j >= 2 else None
                for hi in range(4):
                    h = g * 4 + hi
                    hs = slice(hi * 128, (hi + 1) * 128)
                    Pp = Pk[prev_i][g][:, hi, :]
                    Ptp = AR[:, h, 0:128] if j == 1 else Ptk[prev_i][g][:, hi, :]
                    if bP is not None:
                        self.mm(bP[:, hs], Ptp, Pp)
                    if bPt is not None:
                        self.mm(bPt[:, hs], Pp, Ptp)
                    if bT is not None:
                        self.mm(bT[:, hs], self.ident_s, TT[g][:, hi, :], start=True, stop=False)
                        self.mm(bT[:, hs], Pp, TT[g][:, hi, :], start=False, stop=True)
                v4 = lambda b: b.v(b.ap.rearrange("p (h t) -> p h t", h=4))
                if bP is not None:
                    self.cp("act", Pk[cur_i][g], v4(bP))
                if bPt is not None:
                    self.cp("act", Ptk[cur_i][g], v4(bPt))
                if bT is not None:
                    self.cp("dve" if g == 0 else "act", TT[g], v4(bT))
                yield "s"
        psX = self.psum()
        for h in range(8):
            c = h // 2
            rows = hrows(h)
            hs = slice(h * 64, (h + 1) * 64)
            if samp and h % 2 == 0:
                i = c % 2
                self.tt("dve", amk[i], arT[:, c, 0:1, :].bc([128, 16, 128]), smv, ALU.mult)
                self.cp("act", H0s[i], H0f[:, :, c, :])
            self.mm(psX[:, hs], AR[:, h, 256:384], Vtok[:, hs], start=True, stop=False)
            if not samp:
                self.mm(psX[:, hs], arT[rows, c, 0, :], self.Hsd[rows, c, :], start=False, stop=True)
            else:
                i = c % 2
                for q in range(16):
                    self.mm(psX[:, hs], amk[i][rows, q, :], H0s[i][rows, q, :], start=False, stop=(q == 15))
        self.cp("act", Xsd, psX)
        yield "s"
        psU = self.psum()
        for h in range(8):
            hs = slice(h * 64, (h + 1) * 64)
            self.mm(psU[:, hs], TT[h // 4][:, h % 4, :], Xsd[:, hs])
        self.cp("act", Usd, psU)
        yield "s"
        psO = self.psum()
        for h in range(8):
            c = h // 2
            rows = hrows(h)
            hs = slice(h * 64, (h + 1) * 64)
            out = psO[rows, c * 128:(c + 1) * 128]
            self.mm(out, Usd[:, hs], AR[:, h, 128:256], start=True, stop=False)
            self.mm(out, Vtok[:, hs], AR[:, h, 384:512], start=False, stop=False)
            if not samp:
                self.mm(out, self.Hsd[rows, c, :], arT[rows, c, 1, :], start=False, stop=True)
            else:
                i = c % 2
                if h % 2 == 0:
                    self.tt("dve", rmk[i], arT[:, c, 1:2, :].bc([128, 16, 128]), smv, ALU.mult)
                    self.cp("act", H0s[i], H0f[:, :, c, :])
                for q in range(16):
                    self.mm(out, H0s[i][rows, q, :], rmk[i][rows, q, :], start=False, stop=(q == 15))
        self.cp("act", oT, psO.v(psO.ap.rearrange("p (c t) -> p c t", c=4)))
        yield "s"
        if not samp:
            psH = self.psum()
            for h in range(8):
                c = h // 2
                rows = hrows(h)
                hs = slice(h * 64, (h + 1) * 64)
                out = psH[rows, c * 64:(c + 1) * 64]
                self.mm(out, Btok[:, hs], Usd[:, hs], start=True, stop=False)
                self.mm(out, Ktok[:, hs], Vtok[:, hs], start=False, stop=True)
            self.tt("dve", self.H, self.H, psH.v(psH.ap[:, 0:256].rearrange("p (c v) -> p c v", c=4)), ALU.add)
            self.tt("dve", self.H, self.H, Et[:, :, 127:128].bc([128, 4, 64]), ALU.mult)
            self.cp("act", self.Hsd, self.H)
        rl(m2)
        yield "R2a_done"
        m3 = mk()
        if scoped:
            dd = A("dd", [128, 512])
            sq2 = A("sq2", [128, 512])
            rs2 = A("rs2", [128, 512])
            o3 = A("o3", [128, 512])
        else:
            arf = AR.v(AR.ap.rearrange("p h t -> p (h t)").bitcast(F32))
            dd, sq2, rs2, o3 = arf[:, 0:512], arf[:, 512:1024], arf[:, 1024:1536], arf[:, 1536:2048]
        oTf = oT.v(oT.ap.rearrange("p c t -> p (c t)"))
        psM = self.psum()
        self.mm(psM, self.blk64, oTf)
        self.tt("dve", dd, oTf, psM, ALU.subtract)
        self.act(sq2, dd, AF.Square)
        psV = self.psum()
        self.mm(psV, self.blk64, sq2)
        self.act(rs2, psV, AF.Ln, bias=self.eps[:, 2:3])
        self.act(rs2, rs2, AF.Exp, scale=-0.5)
        yield "s"
        self.tt("dve", dd, dd, rs2, ALU.mult)
        for c in range(4):
            cs_ = slice(c * 128, (c + 1) * 128)
            self.stt(o3[:, cs_], dd[:, cs_], self.pc(R_GG + c), bv[:, c, :], ALU.mult, ALU.add)
            self.stt(ogT[:, c, t0:t0 + 128], o3[:, cs_], self.pc(R_GB + c), gT[:, c, :], ALU.add, ALU.mult)
        yield "s"
        if samp:
            shs = A("shs", [16, 1920])
            for g in range(4):
                ps = self.psum()
                for i in range(4):
                    cc = g * 4 + i
                    if cc >= 15:
                        break
                    n = 128 if cc < 14 else 32
                    self.tr(ps[0:16, i * 128:i * 128 + n], lastc[0:n, cc, :], self.ident_f[0:n, 0:n])
                w = 512 if g < 3 else 288
                self.cp("act", shs[0:16, g * 512:g * 512 + w], ps[0:16, 0:w])
            self.store(io["shift_s"], shs[0:16, 0:1824])
        if last_prompt:
            sho = A("sho", [1, 1920]) if scoped else arf[0:1, 0:1920]
            for g in range(4):
                ps = self.psum()
                for i in range(4):
                    cc = g * 4 + i
                    if cc >= 15:
                        break
                    n = 128 if cc < 14 else 32
                    self.tr(ps[0:1, i * 128:i * 128 + n], self.carry[0:n, cc:cc + 1], self.ident_f[0:n, 0:n])
                w = 512 if g < 3 else 288
                self.cp("act", sho[0:1, g * 512:g * 512 + w], ps[0:1, 0:w])
            self.store(io["shift_p"], sho[0:1, 0:1824])
            Sp = A("Sp", [64, 512])
            ps = self.psum()
            for c in range(4):
                self.tr(ps[0:64, c * 128:(c + 1) * 128], self.H[:, c, :], self.ident_f)
            self.cp("act", Sp, ps[0:64, :])
            self.store(io["wkv_p"].rearrange("(c hp) v k -> v c hp k", hp=2), Sp.v(Sp.ap.rearrange("p (c hp k) -> p c hp k", c=4, hp=2)))
        if samp:
            UVm = A("UVm", [128, 2, 2, 512], SD)
            wcs = Et.v(Et.ap.rearrange("p c (q l) -> p q c l", l=8))
            for q0 in range(0, 16, 2):
                for qi in range(2):
                    self.ts("dve", UVm[:, qi, 0, :], Usd, self.rowmask[:, q0 + qi:q0 + qi + 1], ALU.mult)
                    self.ts("dve", UVm[:, qi, 1, :], Vtok, self.rowmask[:, q0 + qi:q0 + qi + 1], ALU.mult)
                psH = self.psum()
                for qi in range(2):
                    for h in range(8):
                        c = h // 2
                        rows = hrows(h)
                        hs = slice(h * 64, (h + 1) * 64)
                        g = qi * 4 + c
                        out = psH[rows, g * 64:(g + 1) * 64]
                        self.mm(out, Btok[:, hs], UVm[:, qi, 0, hs], start=True, stop=False)
                        self.mm(out, Ktok[:, hs], UVm[:, qi, 1, hs], start=False, stop=True)
                hv = H0f[:, q0:q0 + 2]
                self.tt("dve", hv, hv, psH.v(psH.ap.rearrange("p (q c v) -> p q c v", q=2, c=4)), ALU.add)
                self.tt("dve", hv, hv, wcs[:, q0:q0 + 2, :, 7:8].bc([128, 2, 4, 64]), ALU.mult)
        rl(m3)

    def finish(self):
        S = self.S
        need = {}
        for R in self.outres:
            for s, v in R.r.items():
                need[s] = max(need.get(s, 0), v)
        S._emit_waits("sp", need)
        S.barrier()


IN_SPECS = [
    ("xp", [2048, 1024]), ("xs", [128, 1024]), ("mem", [256, 1024]), ("swkv", [16, 8, 64, 64]),
    ("sshift", [16, 1824]), ("sconv", [480, 256]), ("ck", [16, 256, 256]), ("cv", [16, 256, 256]),
    ("pp", [NROWS, 128]), ("w_up1", [1024, 5632]), ("w_dn1", [2816, 1024]), ("w_in", [1024, 5664]),
    ("w_du", [64, 512]), ("w_au", [64, 512]), ("w_gu", [160, 512]), ("w_ro", [512, 1024]),
    ("w_co", [256, 1024]), ("w_kv", [1024, 512]), ("w_xo", [256, 1024]), ("w_o", [1024, 1024]),
    ("w_up2", [1024, 5632]), ("w_dn2", [2816, 1024]),
]
OUT_SPECS = [
    ("y_p", [2048, 1024]), ("y_s", [128, 1024]), ("wkv_p", [8, 64, 64]), ("shift_p", [1, 1824]),
    ("conv_p", [30, 256]), ("mk_p", [256, 256]), ("mv_p", [256, 256]), ("wkv_s", [16, 8, 64, 64]),
    ("shift_s", [16, 1824]), ("conv_s", [480, 256]),
]


def build_program(stage=99):
    nc = bass.Bass("TRN2", target_bir_lowering=False)
    io = {}
    for name, shape in IN_SPECS:
        io[name] = nc.dram_tensor(name, shape, F32, kind="ExternalInput").ap()
    for name, shape in OUT_SPECS:
        io[name] = nc.dram_tensor(name, shape, F32, kind="ExternalOutput").ap()
    with contextlib.ExitStack() as st:
        B = Builder(nc, st, io, stage=stage)
        B.setup()
        B.run_block(0)
        B.run_block(1)
        B.finish()
        print("ninst", B.S.ninst, "nwait", B.S.nwait, "ndma", B.S.ndma, "sems", len(B.S.sems))
    return nc


def pack_params(inp):
    def rows(a, pad=None):
        a = np.asarray(a, np.float32).reshape(-1)
        if pad is not None:
            a = np.concatenate([a, np.zeros(pad - a.size, np.float32)])
        return a.reshape(-1, 128)
    parts = [rows(inp["ffn1_norm"]), rows(inp["mix_norm"]), rows(inp["ffn2_norm"]), rows(inp["final_norm"]),
             rows(inp["mu_shift"], 15 * 128), rows(inp["w0"]), rows(inp["a0"]), rows(inp["k_k"]), rows(inp["k_a"]),
             rows(inp["r_k"]), rows(inp["gn_g"]), rows(inp["gn_b"]), rows(inp["glu_b"]), rows(inp["conv_w"]),
             rows(inp["conv_b"]), rows(inp["conv_ln_g"]), rows(inp["conv_ln_b"])]
    pp = np.concatenate(parts, 0)
    assert pp.shape == (NROWS, 128), pp.shape
    return np.ascontiguousarray(pp)


def make_in_maps(inp):
    f = lambda a: np.ascontiguousarray(np.asarray(a, np.float32))
    pp = pack_params(inp)
    shared = {
        "pp": pp, "w_up1": f(inp["ffn1_w_up"][0]), "w_dn1": f(inp["ffn1_w_down"][0]), "w_in": f(inp["w_in"][0]),
        "w_du": f(inp["w_decay_up"][0]), "w_au": f(inp["w_a_up"][0]), "w_gu": f(inp["w_g_up"][0]),
        "w_ro": f(inp["w_rwkv_out"][0]), "w_co": f(inp["w_conv_out"][0]), "w_kv": f(inp["w_mem_kv"][0]),
        "w_xo": f(inp["w_xattn_out"][0]), "w_o": f(inp["w_o"][0]), "w_up2": f(inp["ffn2_w_up"][0]),
        "w_dn2": f(inp["ffn2_w_down"][0]),
    }
    maps = []
    for c in range(8):
        sl = slice(16 * c, 16 * c + 16)
        m = dict(shared)
        m["xp"] = f(inp["x_prompt"][c])
        m["xs"] = f(inp["x_sample"][sl]).reshape(128, 1024)
        m["mem"] = f(inp["mem_prompt"][c])
        m["swkv"] = f(inp["state_wkv"][0, sl])
        m["sshift"] = f(inp["state_shift"][0, sl])
        m["sconv"] = f(inp["state_conv"][0, sl]).reshape(480, 256)
        m["ck"] = f(inp["cache_mem_k"][0, sl]).reshape(16, 256, 256)
        m["cv"] = f(inp["cache_mem_v"][0, sl]).reshape(16, 256, 256)
        maps.append(m)
    return maps


def gather(results):
    g = lambda k: [np.asarray(r[k], np.float32) for r in results]
    y_p = np.stack(g("y_p"), 0)
    y_s = np.concatenate(g("y_s"), 0).reshape(128, 8, 1024)
    wkv_p = np.stack(g("wkv_p"), 0)[None]
    shift_p = np.concatenate(g("shift_p"), 0)[None]
    conv_p = np.stack(g("conv_p"), 0)[None]
    mk_p = np.stack(g("mk_p"), 0).reshape(8, 256, 4, 64)[None]
    mv_p = np.stack(g("mv_p"), 0).reshape(8, 256, 4, 64)[None]
    wkv_s = np.concatenate(g("wkv_s"), 0)[None]
    shift_s = np.concatenate(g("shift_s"), 0)[None]
    conv_s = np.concatenate(g("conv_s"), 0).reshape(128, 30, 256)[None]
    return (y_p, y_s, wkv_p, shift_p, conv_p, mk_p, mv_p, wkv_s, shift_s, conv_s)


_NC_CACHE = {}


def kernel(**inputs):
    if "nc" not in _NC_CACHE:
        _NC_CACHE["nc"] = build_program()
    nc = _NC_CACHE["nc"]
    in_maps = make_in_maps(inputs)
    res = run_bass_kernel_spmd(nc, in_maps, core_ids=list(range(8)))
    return gather(res.results)
```

```python
import contextlib
import os
import numpy as np
import concourse.bass as bass
import concourse.mybir as mybir
from concourse.bass_utils import run_bass_kernel_spmd

F32 = mybir.dt.float32
BF16 = mybir.dt.bfloat16
AF = mybir.ActivationFunctionType
ALU = mybir.AluOpType
AX = mybir.AxisListType

SD = BF16
SAME_ENGINE_SYNC = True
ARENA_WORDS = 53200
DFF = 2816
NJ = 22
C_DEC = 0.6065306597126334

R_N1, R_NM, R_N2, R_NF = 0, 8, 16, 24
R_MU = 32
R_W0, R_A0, R_KK, R_KA, R_RK, R_GG, R_GB = 47, 51, 55, 59, 63, 67, 71
R_GLU = 75
R_CW = 79
R_CB, R_LG, R_LB = 141, 143, 145
NROWS = 147


class Res:
    __slots__ = ("name", "w", "r", "excl")

    def __init__(self, name="", excl=False):
        self.name = name
        self.w = None
        self.r = {}
        self.excl = excl


class T:
    __slots__ = ("ap", "res")

    def __init__(self, ap, res):
        self.ap = ap
        self.res = res

    def __getitem__(self, idx):
        return T(self.ap[idx], self.res)

    def v(self, ap):
        return T(ap, self.res)

    def bc(self, shape):
        return T(self.ap.broadcast_to(shape), self.res)


def _ap(x):
    return x.ap if isinstance(x, T) else x


def _res(*xs):
    out = []
    for x in xs:
        if isinstance(x, T):
            out.append(x.res)
    return out


class Sched:
    def __init__(self, nc, stack):
        self.nc = nc
        self.stack = stack
        self.eng = {"pe": nc.tensor, "act": nc.scalar, "dve": nc.vector, "pool": nc.gpsimd, "sp": nc.sync}
        self.sems = {}
        self.cnt = {}
        for k in self.eng:
            self.sems[k] = stack.enter_context(nc.semaphore("sem_" + k))
            self.cnt[k] = 0
        self.known = {k: {} for k in self.eng}
        self.ninst = {k: 0 for k in self.eng}
        self.nwait = 0
        self.ndma = 0
        self.res2sem = {}
        self.dma_free = []
        self.keep = []

    def _need(self, reads, writes, e=None):
        need = {}
        for R in reads:
            if R.w is not None:
                s, v = R.w
                if need.get(s, 0) < v:
                    need[s] = v
        for R in writes:
            if R.w is not None:
                s, v = R.w
                if s != e and need.get(s, 0) < v:
                    need[s] = v
            for s, v in R.r.items():
                if s != e and need.get(s, 0) < v:
                    need[s] = v
        return need

    def _emit_waits(self, e, need):
        kn = self.known[e]
        for s, v in need.items():
            if s == e and (not SAME_ENGINE_SYNC or e in ("pe", "sp")):
                continue
            if kn.get(s, 0) >= v:
                continue
            self.eng[e].wait_ge(self.sems[s], v)
            self.nwait += 1
            kn[s] = v

    def op(self, e, fn, reads=(), writes=()):
        ex = [R for R in reads if R.excl]
        if ex:
            writes = list(writes) + [R for R in ex if R not in writes]
            reads = [R for R in reads if not R.excl]
        self._emit_waits(e, self._need(reads, writes, e))
        ins = fn(self.eng[e])
        self.cnt[e] += 1
        ins.then_inc(self.sems[e], 1)
        tok = (e, self.cnt[e])
        self.ninst[e] += 1
        for R in writes:
            R.w = tok
            R.r = {}
        for R in reads:
            if R.r.get(e, 0) < tok[1]:
                R.r[e] = tok[1]

    def dma(self, q, out, in_, reads=(), writes=(), **kw):
        self._emit_waits(q, self._need(reads, writes))
        key = writes[0] if writes else reads[0]
        sk = self.res2sem.get(id(key))
        if sk is None:
            if self.dma_free:
                sk = self.dma_free.pop()
            else:
                sk = "dma_%d" % len(self.sems)
                self.sems[sk] = self.stack.enter_context(self.nc.semaphore("sd%d" % len(self.sems)))
                self.cnt[sk] = 0
            self.res2sem[id(key)] = sk
            self.keep.append(key)
        ins = self.eng[q].dma_start(out=out, in_=in_, **kw)
        self.cnt[sk] += 16
        ins.then_inc(self.sems[sk], 16)
        tok = (sk, self.cnt[sk])
        self.ndma += 1
        for R in writes:
            R.w = tok
            R.r = {}
        for R in reads:
            if R.r.get(sk, 0) < tok[1]:
                R.r[sk] = tok[1]

    def barrier(self):
        allc = {k: v for k, v in self.cnt.items() if v > 0}
        for e in self.eng:
            self._emit_waits(e, allc)
        self.dma_free.extend(self.res2sem.values())
        self.res2sem = {}


class Builder:
    def __init__(self, nc, st, io, stage=99, dbg=None):
        self.nc = nc
        self.st = st
        self.io = io
        self.stage = stage
        self.S = Sched(nc, st)
        self.arena = st.enter_context(nc.sbuf_tensor("arena", [128, ARENA_WORDS], F32))
        self.off = 0
        self.banks = []
        for i in range(8):
            p = st.enter_context(nc.psum_tensor("bank%d" % i, [128, 512], F32))
            self.banks.append(T(p[:], Res("bank%d" % i, excl=True)))
        self.bi = 0
        self.outres = []
        self.live = []
        self.freed = []
        self.rng_of = {}

    def alloc(self, name, shape, dt=F32):
        P = shape[0]
        fs = list(shape[1:])
        n = 1
        for d in fs:
            n *= d
        words = n if dt == F32 else (n + 1) // 2
        words = (words + 7) // 8 * 8
        assert self.off + words <= ARENA_WORDS, "arena overflow at %s (%d + %d)" % (name, self.off, words)
        ap = self.arena[0:P, self.off:self.off + words]
        rng = (self.off, self.off + words)
        self.off += words
        if dt != F32:
            ap = ap.bitcast(dt)
        ap = ap[:, 0:n]
        if len(fs) > 1:
            names = "abcdef"[:len(fs)]
            pat = "p (" + " ".join(names) + ") -> p " + " ".join(names)
            ap = ap.rearrange(pat, **{names[i]: fs[i] for i in range(len(fs))})
        res = Res(name)
        self._inherit(res, rng)
        self.live.append((rng[0], rng[1], res))
        t = T(ap, res)
        self.rng_of[id(res)] = rng
        return t

    def _inherit(self, res, rng):
        for (a, b, old) in self.freed:
            if a < rng[1] and rng[0] < b:
                toks = list(old.r.items())
                if old.w is not None:
                    toks.append(old.w)
                for s, v in toks:
                    if res.r.get(s, 0) < v:
                        res.r[s] = v

    def sub(self, base, name):
        rng = self.rng_of[id(base.res)]
        res = Res(name)
        self._inherit(res, rng)
        self.live.append((rng[0], rng[1], res))
        self.rng_of[id(res)] = rng
        return res

    def mark(self):
        return self.off

    def release(self, m, hard=False):
        keep = []
        for rec in self.live:
            if rec[0] >= m:
                self.freed.append(rec)
            else:
                keep.append(rec)
        self.live = keep
        self.off = m
        if hard:
            self.S.barrier()
            self.freed = []

    def psum(self):
        b = self.banks[self.bi]
        self.bi = (self.bi + 1) % 8
        return b

    def mm(self, out, lhsT, rhs, start=True, stop=True):
        self.S.op("pe", lambda e: e.matmul(_ap(out), lhsT=_ap(lhsT), rhs=_ap(rhs), start=start, stop=stop),
                  reads=_res(lhsT, rhs), writes=_res(out))

    def tr(self, out, in_, ident):
        self.S.op("pe", lambda e: e.transpose(_ap(out), _ap(in_), _ap(ident)), reads=_res(in_, ident), writes=_res(out))

    def act(self, out, in_, func, bias=None, scale=None, accum=None):
        kw = {}
        if bias is not None:
            kw["bias"] = _ap(bias)
        if scale is not None:
            kw["scale"] = _ap(scale)
        if accum is not None:
            kw["accum_out"] = _ap(accum)
        self.S.op("act", lambda e: e.activation(out=_ap(out), in_=_ap(in_), func=func, **kw),
                  reads=_res(in_, bias, scale), writes=_res(out, accum))

    def cp(self, eng, out, in_):
        if eng == "act":
            self.S.op("act", lambda e: e.copy(out=_ap(out), in_=_ap(in_)), reads=_res(in_), writes=_res(out))
        else:
            self.S.op(eng, lambda e: e.tensor_copy(out=_ap(out), in_=_ap(in_)), reads=_res(in_), writes=_res(out))

    def tt(self, eng, out, a, b, op):
        self.S.op(eng, lambda e: e.tensor_tensor(out=_ap(out), in0=_ap(a), in1=_ap(b), op=op), reads=_res(a, b), writes=_res(out))

    def ts(self, eng, out, a, s1, op0, s2=None, op1=None):
        if op1 is None:
            self.S.op(eng, lambda e: e.tensor_scalar(out=_ap(out), in0=_ap(a), scalar1=_ap(s1), scalar2=None, op0=op0),
                      reads=_res(a, s1), writes=_res(out))
        else:
            self.S.op(eng, lambda e: e.tensor_scalar(out=_ap(out), in0=_ap(a), scalar1=_ap(s1), scalar2=_ap(s2), op0=op0, op1=op1),
                      reads=_res(a, s1, s2), writes=_res(out))

    def stt(self, out, a, scalar, b, op0, op1):
        self.S.op("dve", lambda e: e.scalar_tensor_tensor(out=_ap(out), in0=_ap(a), scalar=_ap(scalar), in1=_ap(b), op0=op0, op1=op1),
                  reads=_res(a, scalar, b), writes=_res(out))

    def recip(self, out, in_):
        self.S.op("dve", lambda e: e.reciprocal(out=_ap(out), in_=_ap(in_)), reads=_res(in_), writes=_res(out))

    def memset(self, eng, out, val):
        self.S.op(eng, lambda e: e.memset(_ap(out), val), writes=_res(out))

    def asel(self, out, in_, pattern, cmp, fill, base, cm):
        self.S.op("pool", lambda e: e.affine_select(out=_ap(out), in_=_ap(in_), pattern=pattern, compare_op=cmp, fill=fill,
                                                    base=base, channel_multiplier=cm), reads=_res(in_), writes=_res(out))

    def load(self, out, src, q="sp"):
        self.S.dma(q, _ap(out), src, writes=_res(out))

    def store(self, dst, in_, q="sp"):
        self.S.dma(q, dst, _ap(in_), reads=_res(in_))
        self.outres.append(in_.res)

    def setup(self):
        io = self.io
        A = self.alloc
        self.ident_f = A("ident_f", [128, 128])
        self.memset("pool", self.ident_f, 0.0)
        self.asel(self.ident_f, self.ident_f, [[-1, 128]], ALU.not_equal, 1.0, 0, 1)
        self.ident_b = A("ident_b", [128, 128], BF16)
        self.cp("dve", self.ident_b, self.ident_f)
        self.ident_s = self.ident_b if SD == BF16 else self.ident_f
        self.ones_b = A("ones_b", [128, 128], BF16)
        self.memset("pool", self.ones_b, 1.0)
        self.blk64 = A("blk64", [128, 128])
        self.blk1 = A("blk1", [128, 128])
        self.memset("pool", self.blk64, 0.0)
        self.memset("pool", self.blk1, 0.0)
        for lo in (0, 64):
            self.memset("pool", self.blk64[lo:lo + 64, lo:lo + 64], 1.0 / 64)
            self.memset("pool", self.blk1[lo:lo + 64, lo:lo + 64], 1.0)
        self.blk1b = A("blk1b", [128, 128], BF16)
        self.cp("pool", self.blk1b, self.blk1)
        self.ones256 = A("ones256", [128, 128])
        self.memset("pool", self.ones256, 1.0 / 256)
        self.M4p = A("M4p", [128, 4, 128])
        self.M4s = A("M4s", [128, 4, 128])
        self.seqmask = A("seqmask", [128, 16, 16, 8], SD)
        self.seqmask_b = self.seqmask
        self.rowmask = A("rowmask", [128, 16])
        self.rs_p = A("rs_p", [128, 128])
        self.rs_s = A("rs_s", [128, 16, 8])
        self.eps = A("eps", [128, 4])
        self.memset("pool", self.eps[:, 0:1], 1e-6)
        self.memset("pool", self.eps[:, 1:2], 1e-5)
        self.memset("pool", self.eps[:, 2:3], 64e-5)
        self.memset("pool", self.eps[:, 3:4], 0.0)
        self.PC = A("PC", [128, NROWS])
        self.omka = A("omka", [128, 4])
        self.xT = A("xT", [128, 8, 1152])
        self.hnT = A("hnT", [128, 8, 1152], BF16)
        self.xres = [self.sub(self.xT, "xT%d" % i) for i in range(3)]
        self.hres_ = [self.sub(self.hnT, "hnT%d" % i) for i in range(3)]
        self.H = A("H", [128, 4, 64])
        self.Hsd = A("Hsd", [128, 4, 64], SD)
        self.memset("dve", self.H, 0.0)
        self.memset("dve", self.Hsd, 0.0)
        self.carry = A("carry", [128, 15])
        self.memset("dve", self.carry, 0.0)
        self.utail = A("utail", [128, 2, 30])
        self.memset("dve", self.utail, 0.0)
        self.KTp = A("KTp", [128, 2, 256], BF16)
        self.Vp = A("Vp", [128, 2, 256], BF16)
        self.wdu = A("wdu", [64, 512])
        self.wau = A("wau", [128, 512], BF16)
        self.wgu = A("wgu", [128, 2, 512], BF16)
        self.load(self.wdu, io["w_du"])
        self.load(self.wau[64:128, :], io["w_au"], q="pool")
        self.load(self.wgu[:, 0, :], io["w_gu"][0:128, :], q="pool")
        self.load(self.wgu[0:32, 1, :], io["w_gu"][128:160, :], q="pool")
        self.xpre_mark = self.mark()
        self.xpre = self.x_prefetch(0)
        mtmp = self.mark()
        pin = A("pin", [128, 2, 128])
        self.load(pin[:, 0, :], io["pp"][0:128, :])
        self.load(pin[0:NROWS - 128, 1, :], io["pp"][128:NROWS, :])
        ps = self.psum()
        self.tr(ps[:, 0:128], pin[:, 0, :], self.ident_f)
        self.tr(ps[:, 128:128 + NROWS - 128], pin[0:NROWS - 128, 1, :], self.ident_f[0:NROWS - 128, 0:NROWS - 128])
        self.cp("dve", self.PC, ps[:, 0:NROWS])
        self.ts("dve", self.omka, self.PC[:, R_KA:R_KA + 4], -1.0, ALU.mult, 1.0, ALU.add)
        self.release(mtmp)

    def setup_late(self):
        A = self.alloc
        self.memset("pool", self.seqmask, 1.0)
        self.asel(self.seqmask, self.seqmask, [[-1, 16], [1, 16], [0, 8]], ALU.is_equal, 0.0, 0, 0)
        self.memset("pool", self.rowmask, 1.0)
        self.asel(self.rowmask, self.rowmask, [[-8, 16]], ALU.is_ge, 0.0, 0, 1)
        self.asel(self.rowmask, self.rowmask, [[8, 16]], ALU.is_ge, 0.0, 7, -1)
        self.memset("pool", self.rs_p, 1.0)
        self.memset("pool", self.rs_p[:, 0:1], 0.0)
        self.memset("pool", self.rs_s, 1.0)
        self.memset("pool", self.rs_s[:, :, 0:1], 0.0)
        mtmp = self.mark()
        su = A("su", [128, 128])
        iu = A("iu", [128, 128])
        for m in (su, iu):
            self.memset("pool", m, 1.0)
        self.asel(su, su, [[1, 128]], ALU.is_gt, 0.0, 0, -1)
        self.asel(iu, iu, [[1, 128]], ALU.is_ge, 0.0, 0, -1)
        bm = A("bm", [128, 16, 8])
        self.memset("pool", bm, 1.0)
        self.asel(bm, bm, [[-8, 16], [0, 8]], ALU.is_ge, 0.0, 0, 1)
        self.asel(bm, bm, [[8, 16], [0, 8]], ALU.is_ge, 0.0, 7, -1)
        bm2 = bm.v(bm.ap.rearrange("p a b -> p (a b)"))
        for i in range(4):
            self.cp("pool", self.M4p[:, i, :], su if i % 2 == 0 else iu)
            self.tt("pool", self.M4s[:, i, :], su if i % 2 == 0 else iu, bm2, ALU.mult)
        self.release(mtmp)

    def dump(self, name, t, n):
        import os
        if os.environ.get("DBG_BLK") is None:
            return
        dt = t.ap.dtype
        d = self.nc.dram_tensor("dbg_" + name, [t.ap.shape[0], n], dt, kind="ExternalOutput").ap()
        src_ap = t.ap
        if len(src_ap.shape) > 2:
            names = "abcdef"[:len(src_ap.shape) - 1]
            src_ap = src_ap.rearrange("p " + " ".join(names) + " -> p (" + " ".join(names) + ")")
        self.S.dma("sp", d, src_ap, reads=_res(t))
        self.outres.append(t.res)

    def xt(self, t0):
        return T(self.xT.ap, self.xres[t0 // 512])

    def hn(self, t0):
        return T(self.hnT.ap, self.hres_[t0 // 512])

    def pc(self, row):
        return self.PC[:, row:row + 1]

    def x_sources(self, blk):
        io = self.io
        if blk == 0:
            return [(io["xp"][ch * 128:(ch + 1) * 128, :], ch * 128) for ch in range(8)]
        return [(io["xp"][1024 + ch * 128:1024 + (ch + 1) * 128, :], ch * 128) for ch in range(8)] + [(io["xs"], 1024)]

    def x_prefetch(self, blk, nslots=4):
        xin = [self.alloc("xin%d" % i, [128, 1024]) for i in range(nslots)]
        srcs = self.x_sources(blk)
        for i in range(min(nslots, len(srcs))):
            self.load(xin[i], srcs[i][0], q="sp")
        return xin

    def load_x(self, blk, pre=None):
        srcs = self.x_sources(blk)
        m = self.mark()
        if pre is None:
            xin = self.x_prefetch(blk)
        else:
            xin = pre
        ns = len(xin)
        for ch, (sap, tok) in enumerate(srcs):
            xi = xin[ch % ns]
            if ch >= ns:
                self.load(xi, sap, q="pool")
            for g in range(2):
                ps = self.psum()
                for c in range(4):
                    cc = g * 4 + c
                    self.tr(ps[:, c * 128:(c + 1) * 128], xi[:, cc * 128:(cc + 1) * 128], self.ident_f)
                dst = self.xt(tok)[:, g * 4:(g + 1) * 4, tok:tok + 128]
                self.cp("act" if g == 0 else "dve", dst, ps.v(ps.ap.rearrange("p (c t) -> p c t", c=4)))
        if pre is None:
            self.release(m)

    def rmsnorm(self, row, tiles, out, local=False):
        m_ = self.mark()
        sqs = [self.alloc("sq%d" % i, [128, 8, 512], BF16) for i in range(2)]
        rstds = [self.alloc("rstd%d" % i, [128, 512]) for i in range(2)]
        for k, (t0, tn) in enumerate(tiles):
            sq, rstd = sqs[k % 2], rstds[k % 2]
            self.act(sq[:, :, 0:tn], self.xt(t0)[:, :, t0:t0 + tn], AF.Square)
            ps = self.psum()
            for c in range(8):
                self.mm(ps[:, 0:tn], self.ones_b, sq[:, c, 0:tn], start=(c == 0), stop=(c == 7))
            self.act(rstd[:, 0:tn], ps[:, 0:tn], AF.Ln, bias=self.eps[:, 0:1], scale=1.0 / 1024)
            self.act(rstd[:, 0:tn], rstd[:, 0:tn], AF.Exp, scale=-0.5)
            o0 = 0 if local else t0
            for c in range(8):
                o_ = out if local else self.hn(t0)
                self.stt(o_[:, c, o0:o0 + tn], self.xt(t0)[:, c, t0:t0 + tn], self.pc(row + c), rstd[:, 0:tn], ALU.mult, ALU.mult)
        self.release(m_)

    def ffn(self, w_up, w_dn, nrow, tiles, NT):
        io = self.io
        self.rmsnorm(nrow, tiles, self.hnT)
        m = self.mark()
        hT = self.alloc("hT", [128, NJ, 1152], BF16)
        hres = [self.sub(hT, "hT%d" % i) for i in range(len(tiles))]
        sg = [self.alloc("sg%d" % i, [128, 512], BF16) for i in range(2)]
        wdn = [self.alloc("wdn%d" % i, [128, NJ, 128], BF16) for i in range(3)]
        m_up = self.mark()
        wup = [self.alloc("wup%d" % i, [128, 8, 2, 256], BF16) for i in range(3)]
        wv = w_up.rearrange("(c p) f -> p c f", p=128)
        dv = w_dn.rearrange("(j p) d -> p j d", p=128)
        nsg = 0
        for jj in range(NJ // 2):
            wb = wup[jj % 3]
            self.load(wb[:, :, 0, :], wv[:, :, jj * 256:(jj + 1) * 256], q="pool")
            self.load(wb[:, :, 1, :], wv[:, :, DFF + jj * 256: DFF + (jj + 1) * 256], q="pool")
            if jj < 3:
                self.load(wdn[jj], dv[:, :, jj * 128:(jj + 1) * 128], q="pool")
            for j2 in range(2):
                j = jj * 2 + j2
                for ti, (t0, tn) in enumerate(tiles):
                    pg = self.psum()
                    pv = self.psum()
                    for c in range(8):
                        self.mm(pg[:, 0:tn], wb[:, c, 0, j2 * 128:(j2 + 1) * 128], self.hn(t0)[:, c, t0:t0 + tn], start=(c == 0), stop=(c == 7))
                    for c in range(8):
                        self.mm(pv[:, 0:tn], wb[:, c, 1, j2 * 128:(j2 + 1) * 128], self.hn(t0)[:, c, t0:t0 + tn], start=(c == 0), stop=(c == 7))
                    s = sg[nsg % 2]
                    nsg += 1
                    self.act(s[:, 0:tn], pg[:, 0:tn], AF.Silu)
                    self.tt("dve", T(hT.ap[:, j, t0:t0 + tn], hres[ti]), s[:, 0:tn], pv[:, 0:tn], ALU.mult)
        self.release(m_up)
        wdn = wdn + [self.alloc("wdn%d" % i, [128, NJ, 128], BF16) for i in range(3, 8)]
        for dc in range(3, 8):
            self.load(wdn[dc], dv[:, :, dc * 128:(dc + 1) * 128], q="pool")
        for ti, (t0, tn) in enumerate(tiles):
            for dc in range(8):
                wd = wdn[dc]
                ps = self.psum()
                for j in range(NJ):
                    self.mm(ps[:, 0:tn], wd[:, j, :], T(hT.ap[:, j, t0:t0 + tn], hres[ti]), start=(j == 0), stop=(j == NJ - 1))
                self.stt(self.xt(t0)[:, dc, t0:t0 + tn], ps[:, 0:tn], 0.5, self.xt(t0)[:, dc, t0:t0 + tn], ALU.mult, ALU.add)
        self.release(m)

    def final_out(self, dst, tok0, nchunks):
        m = self.mark()
        yT = self.alloc("yT", [128, 8, 512])
        yo = [self.alloc("yo%d" % i, [128, 1024]) for i in range(2)]
        done = 0
        k = 0
        while done < nchunks:
            nch = min(4, nchunks - done)
            t0 = tok0 + done * 128
            tn = nch * 128
            self.rmsnorm(R_NF, [(t0, tn)], yT, local=True)
            for ch in range(nch):
                y = yo[k % 2]
                k += 1
                for g in range(2):
                    ps = self.psum()
                    for c in range(4):
                        self.tr(ps[:, c * 128:(c + 1) * 128], yT[:, g * 4 + c, ch * 128:(ch + 1) * 128], self.ident_f)
                    self.cp("act" if g == 0 else "dve", y[:, g * 512:(g + 1) * 512], ps)
                self.store(dst[(done + ch) * 128:(done + ch + 1) * 128, :], y)
            done += nch
        self.release(m)

    def run_block(self, blk):
        io = self.io
        if blk == 0:
            tiles = [(0, 512), (512, 512)]
            NT = 1024
            self.load_x(0, self.xpre)
            self.release(self.xpre_mark)
        else:
            tiles = [(0, 512), (512, 512), (1024, 128)]
            NT = 1152
            self.load_x(1, self.xpre)
            self.release(self.xpre_mark)
        self.ffn(io["w_up1"], io["w_dn1"], R_N1, tiles, NT)
        if blk == 0:
            self.setup_late()
        if self.stage >= 2:
            self.mixer(blk, tiles, NT)
        if self.stage >= 3:
            self.ffn(io["w_up2"], io["w_dn2"], R_N2, tiles, NT)
        if blk == 0:
            self.xpre_mark = self.mark()
            self.xpre = self.x_prefetch(1)
            self.final_out(io["y_p"][0:1024, :], 0, 8)
        else:
            self.final_out(io["y_p"][1024:2048, :], 0, 8)
            self.final_out(io["y_s"], 1024, 1)
        self.release(self.mark(), hard=(os.environ.get("HARD_BLK", "0") == "1"))

    def mixer(self, blk, tiles, NT):
        self.rmsnorm(R_NM, tiles, self.hnT)
        m0 = self.mark()
        ogT = self.alloc("ogT", [128, 4, 1152], BF16)
        self.wv_in = self.io["w_in"].rearrange("(c p) f -> p c f", p=128)
        import os
        parts = os.environ.get("MIX_PARTS", "conv,xattn,rwkv,merge").split(",")
        if "rwkv" in parts:
            self.rwkv_branch(blk, ogT)
        else:
            self.memset("dve", ogT, 0.0)
        csT = self.alloc("csT", [128, 2, 1152], BF16)
        ocT = self.alloc("ocT", [128, 2, 1152], BF16)
        for nm, tl in (("conv", csT), ("xattn", ocT)):
            if nm not in parts:
                self.memset("dve", tl, 0.0)
        KTs = None
        if blk == 1 and "xattn" in parts:
            KTs = self.alloc("KTs", [128, 2, 16, 256], BF16)
            mk_ = self.mark()
            ckb = self.alloc("ckb", [128, 16, 2, 256], BF16)
            ld_ck = lambda: self.load(ckb, self.io["ck"].rearrange("q (mc p) c -> p q mc c", p=128), q="pool")
            if "conv" not in parts:
                ld_ck()
        else:
            ld_ck = None
        if "conv" in parts:
            self.conv_branch(blk, tiles, csT, ld_ck)
        if KTs is not None:
            for q in range(16):
                ps = self.psum()
                pb16 = ps.v(ps.ap.bitcast(BF16))
                for cc in range(2):
                    for mc in range(2):
                        o = cc * 256 + mc * 128
                        self.tr(pb16[:, o:o + 128], ckb[:, q, mc, cc * 128:(cc + 1) * 128], self.ident_b)
                self.cp("act" if q % 2 == 0 else "dve", KTs[:, :, q, :], pb16.v(pb16.ap[:, 0:512].rearrange("p (c m) -> p c m", c=2)))
            self.release(mk_)
        pre_w = None
        if "xattn" in parts and "merge" in parts:
            pw0 = self.alloc("wg0", [128, 8, 3, 128], BF16)
            pwr = self.alloc("wro", [128, 4, 1024], BF16)
            pre_w = (pw0, pwr)
            gv_ = self.wv_in[:, :, 2592:5664].rearrange("p c (b f) -> p c b f", b=3)

            def ld_pre():
                for b in range(3):
                    self.load(pw0[:, :, b, :], gv_[:, :, b, 0:128], q="pool")
                self.load(pwr, self.io["w_ro"].rearrange("(c p) f -> p c f", p=128), q="pool")
        else:
            ld_pre = None
        if "xattn" in parts:
            self.xattn_branch(blk, tiles, ocT, KTs, ld_pre)
        if os.environ.get("DBG_BLK") == str(blk):
            self.dump("csT", csT, 2 * 1152)
            self.dump("ocT", ocT, 2 * 1152)
            self.dump("ogT", ogT, 4 * 1152)
            self.dump("hnT", self.hnT, 8 * 1152)
        if "merge" in parts:
            self.merge(blk, tiles, csT, ocT, ogT, pre_w)
        self.release(m0)

    def conv_branch(self, blk, tiles, csT, after_wc=None):
        io = self.io
        m = self.mark()
        wc = self.alloc("wc", [128, 8, 512], BF16)
        self.load(wc, self.wv_in[:, :, 1824:2336], q="pool")
        if after_wc is not None:
            after_wc()
        uP = self.alloc("uP", [128, 2, 1054])
        cT = self.alloc("cT", [128, 2, 1152])
        sgl = self.alloc("sgl", [128, 512])
        self.cp("act", uP[:, :, 0:30], self.utail)
        if blk == 1:
            uS = self.alloc("uS", [128, 2, 16, 38])
            sc = self.alloc("sc", [120, 4, 256])
            self.load(sc, io["sconv"].rearrange("(g r) c -> r g c", r=120))
            for g in range(4):
                ps = self.psum()
                for ch in range(2):
                    self.tr(ps[:, ch * 120:(ch + 1) * 120], sc[0:120, g, ch * 128:(ch + 1) * 128], self.ident_f[0:120, 0:120])
                self.cp("act", uS[:, :, 4 * g:4 * g + 4, 0:30], ps.v(ps.ap[:, 0:240].rearrange("p (c q t) -> p c q t", c=2, q=4)))
        for (t0, tn) in tiles:
            for ch in range(2):
                pa = self.psum()
                pb = self.psum()
                for c in range(8):
                    self.mm(pa[:, 0:tn], wc[:, c, ch * 128:(ch + 1) * 128], self.hn(t0)[:, c, t0:t0 + tn], start=(c == 0), stop=(c == 7))
                for c in range(8):
                    self.mm(pb[:, 0:tn], wc[:, c, 256 + ch * 128:256 + (ch + 1) * 128], self.hn(t0)[:, c, t0:t0 + tn], start=(c == 0), stop=(c == 7))
                self.act(sgl[:, 0:tn], pb[:, 0:tn], AF.Sigmoid, bias=self.pc(R_GLU + 2 + ch))
                if t0 < 1024:
                    self.stt(uP[:, ch, 30 + t0:30 + t0 + tn], pa[:, 0:tn], self.pc(R_GLU + ch), sgl[:, 0:tn], ALU.add, ALU.mult)
                else:
                    self.stt(uS[:, ch, :, 30:38], pa.v(pa.ap[:, 0:128].rearrange("p (q t) -> p q t", q=16)), self.pc(R_GLU + ch),
                             sgl.v(sgl.ap[:, 0:128].rearrange("p (q t) -> p q t", q=16)), ALU.add, ALU.mult)
        uPb = self.alloc("uPb", [128, 2, 1054], BF16)
        self.cp("act", uPb[:, 0, :], uP[:, 0, :])
        self.cp("dve", uPb[:, 1, :], uP[:, 1, :])
        if blk == 1:
            uSb = self.alloc("uSb", [128, 2, 16, 38], BF16)
            self.cp("act", uSb, uS)
        dg = [self.alloc("dg%d" % i, [128, 128], BF16) for i in range(4)]
        nd = 0
        for ch in range(2):
            pts = [self.psum(), self.psum()]
            pss_ = self.psum() if blk == 1 else None
            for w in range(31):
                d = dg[nd % 4]
                nd += 1
                self.ts("dve", d, self.ident_b, self.pc(R_CW + 2 * w + ch), ALU.mult)
                for ti in range(2):
                    self.mm(pts[ti], d, uPb[:, ch, w + ti * 512:w + ti * 512 + 512], start=(w == 0), stop=(w == 30))
                if blk == 1:
                    self.mm(pss_[:, 0:128], d, uSb[:, ch, :, w:w + 8], start=(w == 0), stop=(w == 30))
            for ti in range(2):
                self.act(cT[:, ch, ti * 512:(ti + 1) * 512], pts[ti], AF.Identity, bias=self.pc(R_CB + ch))
            if blk == 1:
                self.act(cT[:, ch, 1024:1152], pss_[:, 0:128], AF.Identity, bias=self.pc(R_CB + ch))
        self.cp("act", self.utail, uP[:, :, 1024:1054])
        if blk == 1:
            cvo = self.alloc("cvo", [30, 256])
            ps = self.psum()
            for ch in range(2):
                self.tr(ps[0:30, ch * 128:(ch + 1) * 128], uP[:, ch, 1024:1054], self.ident_f)
            self.cp("act", cvo, ps[0:30, 0:256])
            self.store(io["conv_p"], cvo)
            cso = self.alloc("cso", [120, 4, 256])
            tmpc = self.alloc("tmpc", [128, 2, 120])
            for g in range(4):
                ps = self.psum()
                self.cp("act", tmpc.v(tmpc.ap.rearrange("p c (q t) -> p c q t", q=4)), uS[:, :, 4 * g:4 * g + 4, 8:38])
                for ch in range(2):
                    self.tr(ps[0:120, ch * 128:(ch + 1) * 128], tmpc[:, ch, :], self.ident_f)
                self.cp("act", cso[:, g, :], ps[0:120, 0:256])
            self.store(io["conv_s"].rearrange("(g r) c -> r g c", r=120), cso)
        nt_ = len(tiles)
        sqf = [self.alloc("sqf%d" % i, [128, 2, 512]) for i in range(nt_)]
        rsd = [self.alloc("rsd%d" % i, [128, 512]) for i in range(nt_)]
        cres = [self.sub(cT, "cT%d" % i) for i in range(nt_)]
        cTt = [T(cT.ap, cres[i]) for i in range(nt_)]
        pms = []
        for i, (t0, tn) in enumerate(tiles):
            pm = self.psum()
            for ch in range(2):
                self.mm(pm[:, 0:tn], self.ones256, cT[:, ch, t0:t0 + tn], start=(ch == 0), stop=(ch == 1))
            pms.append(pm)
        for i, (t0, tn) in enumerate(tiles):
            for ch in range(2):
                self.tt("dve", cTt[i][:, ch, t0:t0 + tn], cT[:, ch, t0:t0 + tn], pms[i][:, 0:tn], ALU.subtract)
            self.act(sqf[i][:, :, 0:tn], cTt[i][:, :, t0:t0 + tn], AF.Square)
        pvs = []
        for i, (t0, tn) in enumerate(tiles):
            pv = self.psum()
            for ch in range(2):
                self.mm(pv[:, 0:tn], self.ones256, sqf[i][:, ch, 0:tn], start=(ch == 0), stop=(ch == 1))
            pvs.append(pv)
        for i, (t0, tn) in enumerate(tiles):
            self.act(rsd[i][:, 0:tn], pvs[i][:, 0:tn], AF.Ln, bias=self.eps[:, 1:2])
            self.act(rsd[i][:, 0:tn], rsd[i][:, 0:tn], AF.Exp, scale=-0.5)
        for i, (t0, tn) in enumerate(tiles):
            for ch in range(2):
                self.tt("dve", cTt[i][:, ch, t0:t0 + tn], cTt[i][:, ch, t0:t0 + tn], rsd[i][:, 0:tn], ALU.mult)
                self.act(csT[:, ch, t0:t0 + tn], cTt[i][:, ch, t0:t0 + tn], AF.Silu, bias=self.pc(R_LB + ch), scale=self.pc(R_LG + ch))
        self.release(m)

    def xattn_branch(self, blk, tiles, ocT, KTs=None, after_loads=None):
        io = self.io
        m = self.mark()
        wq = self.alloc("wq", [128, 8, 256], BF16)
        self.load(wq, self.wv_in[:, :, 2336:2592], q="pool")
        qT = self.alloc("qT", [128, 2, 1152], BF16)
        for (t0, tn) in tiles:
            for cc in range(2):
                ps = self.psum()
                for c in range(8):
                    self.mm(ps[:, 0:tn], wq[:, c, cc * 128:(cc + 1) * 128], self.hn(t0)[:, c, t0:t0 + tn], start=(c == 0), stop=(c == 7))
                self.cp("act", qT[:, cc, t0:t0 + tn], ps[:, 0:tn])
        if blk == 0:
            m1 = self.mark()
            memT = self.alloc("memT", [128, 8, 256], BF16)
            wkv = self.alloc("wkv", [128, 8, 512], BF16)
            self.load(wkv, io["w_kv"].rearrange("(c p) f -> p c f", p=128), q="pool")
            mi = self.alloc("mi", [128, 2, 1024])
            kvo = self.alloc("kvo", [128, 2, 512])
            for mc in range(2):
                self.load(mi[:, mc, :], io["mem"][mc * 128:(mc + 1) * 128, :])
                for g in range(2):
                    ps = self.psum()
                    for c in range(4):
                        self.tr(ps[:, c * 128:(c + 1) * 128], mi[:, mc, (g * 4 + c) * 128:(g * 4 + c + 1) * 128], self.ident_f)
                    self.cp("act", memT[:, g * 4:(g + 1) * 4, mc * 128:(mc + 1) * 128], ps.v(ps.ap.rearrange("p (c t) -> p c t", c=4)))
            for cc in range(2):
                ps = self.psum()
                for c in range(8):
                    self.mm(ps[:, 0:256], wkv[:, c, cc * 128:(cc + 1) * 128], memT[:, c, :], start=(c == 0), stop=(c == 7))
                self.cp("act", self.KTp[:, cc, :], ps[:, 0:256])
            for mc in range(2):
                ps = self.psum()
                for c in range(8):
                    self.mm(ps, memT[:, c, mc * 128:(mc + 1) * 128], wkv[:, c, :], start=(c == 0), stop=(c == 7))
                self.cp("act", kvo[:, mc, :], ps)
                self.cp("dve", self.Vp[:, mc, :], ps[:, 256:512])
                self.store(io["mk_p"][mc * 128:(mc + 1) * 128, :], kvo[:, mc, 0:256])
                self.store(io["mv_p"][mc * 128:(mc + 1) * 128, :], kvo[:, mc, 256:512])
            self.release(m1)
        WT = []
        for i in range(2):
            WT.append(dict(mx=self.alloc("mx%d" % i, [128, 4]), nmx=self.alloc("nmx%d" % i, [128, 4]), rsum=self.alloc("rsum%d" % i, [128, 4]),
                           rinv=self.alloc("rinv%d" % i, [128, 4]), Pb=self.alloc("Pb%d" % i, [128, 4, 256], BF16),
                           PT=self.alloc("PT%d" % i, [128, 8, 128], BF16),
                           octok=self.alloc("octok%d" % i, [128, 256], BF16)))
        chunks = [(ch * 128, False) for ch in range(8)]
        if blk == 1:
            chunks.append((1024, True))
            cvb = self.alloc("cvb", [128, 16, 2, 256], BF16)
            self.load(cvb, io["cv"].rearrange("q (mc p) c -> p q mc c", p=128), q="pool")
        if after_loads is not None:
            after_loads()
        if blk == 1:
            qmask = self.alloc("qmask", [128, 2, 16, 128], BF16)
            PTm = [self.alloc("PTm%d" % i, [128, 2, 16, 128], BF16) for i in range(2)]
            smv = self.seqmask_b.v(self.seqmask_b.ap.rearrange("p q a b -> p q (a b)"))
        def xchunk(t0, samp, W):
            mx, nmx, rsum, rinv, Pb, PT, octok = W["mx"], W["nmx"], W["rsum"], W["rinv"], W["Pb"], W["PT"], W["octok"]
            psA = [self.psum(), self.psum()]
            if samp:
                for cc in range(2):
                    self.tt("dve", qmask[:, cc], qT[:, cc:cc + 1, t0:t0 + 128].bc([128, 16, 128]), smv, ALU.mult)
            for h in range(4):
                rows = slice((h % 2) * 64, (h % 2) * 64 + 64)
                out = psA[h % 2][:, (h // 2) * 256:(h // 2) * 256 + 256]
                if not samp:
                    self.mm(out, qT[rows, h // 2, t0:t0 + 128], self.KTp[rows, h // 2, :])
                else:
                    for q in range(16):
                        self.mm(out, qmask[rows, h // 2, q, :], KTs[rows, h // 2, q, :], start=(q == 0), stop=(q == 15))
            for i in range(2):
                self.S.op("dve", lambda e: e.tensor_reduce(out=_ap(mx[:, i:4:2]), in_=psA[i].ap.rearrange("p (h m) -> p h m", h=2),
                                                           axis=AX.X, op=ALU.max), reads=_res(psA[i]), writes=_res(mx))
            self.ts("dve", nmx, mx, -0.125, ALU.mult)
            for h in range(4):
                self.act(Pb[:, h, :], psA[h % 2][:, (h // 2) * 256:(h // 2) * 256 + 256], AF.Exp, bias=nmx[:, h:h + 1], scale=0.125)
            self.S.op("dve", lambda e: e.tensor_reduce(out=_ap(rsum), in_=_ap(Pb), axis=AX.X, op=ALU.add), reads=_res(Pb), writes=_res(rsum))
            self.recip(rinv, rsum)
            yield
            ps = self.psum()
            pb16 = ps.v(ps.ap.bitcast(BF16))
            for h in range(4):
                for mc in range(2):
                    j = h * 2 + mc
                    self.tr(pb16[:, j * 128:(j + 1) * 128], Pb[:, h, mc * 128:(mc + 1) * 128], self.ident_b)
            self.cp("act", PT, pb16.v(pb16.ap.rearrange("p (j t) -> p j t", j=8)))
            yield
            pso = self.psum()
            for h in range(4):
                o = pso[:, h * 64:(h + 1) * 64]
                if not samp:
                    for mc in range(2):
                        self.mm(o, PT[:, h * 2 + mc, :], self.Vp[:, mc, h * 64:(h + 1) * 64], start=(mc == 0), stop=(mc == 1))
                else:
                    pm = PTm[h % 2]
                    for mc in range(2):
                        self.tt("dve", pm[:, mc], PT[:, h * 2 + mc:h * 2 + mc + 1, :].bc([128, 16, 128]), smv, ALU.mult)
                    n = 0
                    for q in range(16):
                        for mc in range(2):
                            self.mm(o, pm[:, mc, q, :], cvb[:, q, mc, h * 64:(h + 1) * 64], start=(n == 0), stop=(n == 31))
                            n += 1
            self.tt("dve", octok.v(octok.ap.rearrange("p (h d) -> p h d", h=4)), pso.v(pso.ap[:, 0:256].rearrange("p (h d) -> p h d", h=4)),
                    rinv.v(rinv.ap.rearrange("p (h o) -> p h o", o=1)).bc([128, 4, 64]), ALU.mult)
            yield
            ps2 = self.psum()
            p216 = ps2.v(ps2.ap.bitcast(BF16))
            for cc in range(2):
                self.tr(p216[:, cc * 128:(cc + 1) * 128], octok[:, cc * 128:(cc + 1) * 128], self.ident_b)
            self.cp("act", ocT[:, :, t0:t0 + 128], p216.v(p216.ap[:, 0:256].rearrange("p (c t) -> p c t", c=2)))
            yield
        gens = [xchunk(t0, samp, WT[i % 2]) for i, (t0, samp) in enumerate(chunks)]
        active = []
        gi = 0
        while gi < len(gens) or active:
            while len(active) < 2 and gi < len(gens):
                active.append(gens[gi])
                gi += 1
            for g in list(active):
                try:
                    next(g)
                except StopIteration:
                    active.remove(g)
        self.release(m)

    def merge(self, blk, tiles, csT, ocT, ogT, pre_w=None):
        io = self.io
        m = self.mark()
        wro = pre_w[1] if pre_w is not None else self.alloc("wro", [128, 4, 1024], BF16)
        wco = self.alloc("wco", [128, 2, 1024], BF16)
        wxo = self.alloc("wxo", [128, 2, 1024], BF16)
        wo = self.alloc("wo", [128, 8, 1024], BF16)
        wg = [pre_w[0] if (i == 0 and pre_w is not None) else self.alloc("wg%d" % i, [128, 8, 3, 128], BF16) for i in range(3)]
        mT = self.alloc("mT", [128, 8, 1152], BF16)
        gs = [self.alloc("gs%d" % i, [128, 512], BF16) for i in range(6)]
        tm = [self.alloc("tm%d" % i, [128, 512]) for i in range(6)]
        gv = self.wv_in[:, :, 2592:5664].rearrange("p c (b f) -> p c b f", b=3)
        ng = 0
        nt = 0
        def ldg(dc_):
            for b in range(3):
                self.load(wg[dc_ % 3][:, :, b, :], gv[:, :, b, dc_ * 128:(dc_ + 1) * 128], q="pool")
        if pre_w is None:
            ldg(0)
            self.load(wro, io["w_ro"].rearrange("(c p) f -> p c f", p=128), q="pool")
        self.load(wco, io["w_co"].rearrange("(c p) f -> p c f", p=128), q="pool")
        self.load(wxo, io["w_xo"].rearrange("(c p) f -> p c f", p=128), q="pool")
        ldg(1)
        self.load(wo, io["w_o"].rearrange("(c p) f -> p c f", p=128), q="pool")
        for dc in range(8):
            w = wg[dc % 3]
            if dc + 2 < 8:
                ldg(dc + 2)
            for (t0, tn) in tiles:
                g3 = []
                for b in range(3):
                    ps = self.psum()
                    for c in range(8):
                        self.mm(ps[:, 0:tn], w[:, c, b, :], self.hn(t0)[:, c, t0:t0 + tn], start=(c == 0), stop=(c == 7))
                    g = gs[ng % 6]
                    ng += 1
                    self.act(g[:, 0:tn], ps[:, 0:tn], AF.Sigmoid)
                    g3.append(g)
                ys = []
                for (wt, src, nk) in ((wro, ogT, 4), (wco, csT, 2), (wxo, ocT, 2)):
                    ps = self.psum()
                    for c in range(nk):
                        self.mm(ps[:, 0:tn], wt[:, c, dc * 128:(dc + 1) * 128], src[:, c, t0:t0 + tn], start=(c == 0), stop=(c == nk - 1))
                    ys.append(ps)
                t0_ = tm[nt % 6]
                t1_ = tm[(nt + 1) % 6]
                t2_ = tm[(nt + 2) % 6]
                nt += 3
                self.tt("dve", t0_[:, 0:tn], g3[0][:, 0:tn], ys[0][:, 0:tn], ALU.mult)
                self.tt("dve", t1_[:, 0:tn], g3[1][:, 0:tn], ys[1][:, 0:tn], ALU.mult)
                self.tt("dve", t2_[:, 0:tn], g3[2][:, 0:tn], ys[2][:, 0:tn], ALU.mult)
                self.tt("dve", t0_[:, 0:tn], t0_[:, 0:tn], t1_[:, 0:tn], ALU.add)
                self.tt("dve", mT[:, dc, t0:t0 + tn], t0_[:, 0:tn], t2_[:, 0:tn], ALU.add)
        import os
        if os.environ.get("DBG_BLK") == str(blk):
            self.dump("mT", mT, 8 * 1152)
        for (t0, tn) in tiles:
            for dc in range(8):
                ps = self.psum()
                for c in range(8):
                    self.mm(ps[:, 0:tn], wo[:, c, dc * 128:(dc + 1) * 128], mT[:, c, t0:t0 + tn], start=(c == 0), stop=(c == 7))
                self.tt("dve", self.xt(t0)[:, dc, t0:t0 + tn], self.xt(t0)[:, dc, t0:t0 + tn], ps[:, 0:tn], ALU.add)
        self.release(m)

    def rwkv_branch(self, blk, ogT):
        io = self.io
        mtop = self.mark()
        wzs = self.alloc("wzs", [128, 8, 1920], BF16)
        wzg = [None] * 4
        for gi_, (c0, cn) in ((3, (1536, 288)), (1, (512, 512)), (0, (0, 512)), (2, (1024, 512))):
            tg = T(wzs.ap, self.sub(wzs, "wzs%d" % c0))
            if c0 == 1536:
                self.memset("dve", T(wzs.ap[:, :, 1824:1920], tg.res), 0.0)
            self.S.dma("pool", wzs.ap[:, :, c0:c0 + cn], self.wv_in[:, :, c0:c0 + cn], writes=[tg.res])
            wzg[gi_] = tg
        Ks = []
        for i in range(2):
            K = {}
            for nm, shp, dt in (("arT", [128, 4, 2, 128], SD), ("btT", [128, 4, 128], SD), ("ktT", [128, 4, 128], SD),
                                ("vsd", [128, 4, 128], SD), ("Et", [128, 4, 128], F32), ("gT", [128, 4, 128], F32),
                                ("bv", [128, 4, 128], F32), ("Vtok", [128, 512], SD), ("Btok", [128, 512], SD),
                                ("Ktok", [128, 512], SD), ("oT", [128, 4, 128], F32), ("Usd", [128, 512], SD)):
                K[nm] = self.alloc(nm + str(i), shp, dt)
            K["wzs"] = wzg
            K["ogT"] = ogT
            Ks.append(K)
        K = Ks[0]
        mp = self.mark()
        cache = {}
        alias = {"Sp": "aT"}

        def A_cached(name, shape, dt=F32):
            if name in alias:
                base = cache[alias[name]]
                ap = base.ap
                if len(ap.shape) > 2:
                    names = "abcdef"[:len(ap.shape) - 1]
                    ap = ap.rearrange("p " + " ".join(names) + " -> p (" + " ".join(names) + ")")
                return T(ap[0:shape[0], 0:shape[1]], base.res)
            if name not in cache:
                cache[name] = self.alloc(name, shape, dt)
            return cache[name]

        gens = [self._rwkv_chunk(Ks[ch % 2], A_cached, False, ch * 128, False, blk == 1 and ch == 7, None) for ch in range(8)]

        def run_until(g, tag):
            for t in g:
                if t == tag:
                    return

        st = [0] * 9

        def step(k):
            if st[k] == 4:
                return
            try:
                t = next(gens[k])
            except StopIteration:
                st[k] = 4
                return
            if t == "ZS_done":
                st[k] = 1
            elif t == "XM_done":
                st[k] = 2
            elif t == "R1_done":
                st[k] = 3
        st[8] = 4
        while st[0] < 3:
            step(0)
        for i in range(8):
            while st[i] < 4 or (i + 1 < 8 and st[i + 1] < 3):
                if st[i] < 4:
                    step(i)
                if i + 1 < 8 and st[i + 1] < 3:
                    step(i + 1)
                if i + 2 < 8 and st[i + 1] >= 2 and st[i + 2] < 1:
                    step(i + 2)
        self.release(mp)
        if blk == 1:
            A = self.alloc
            H0f = A("H0f", [128, 16, 4, 64])
            ssT = A("ssT", [128, 15, 16])
            lastc = A("lastc", [128, 15, 16])
            m1 = self.mark()
            sst = A("sst", [16, 1824])
            self.load(sst, io["sshift"])
            ps = self.psum()
            for cc in range(15):
                n = 128 if cc < 14 else 32
                self.tr(ps[0:n, cc * 16:(cc + 1) * 16], sst[0:16, cc * 128:cc * 128 + n], self.ident_f[0:16, 0:16])
            self.cp("act", ssT[:, 0:14, :], ps.v(ps.ap[:, 0:224].rearrange("p (c q) -> p c q", c=14)))
            self.cp("act", ssT[0:32, 14, :], ps[0:32, 224:240])
            Sall = A("Sall", [64, 16, 8, 64])
            sv = io["swkv"].rearrange("q h v k -> v q h k")
            for g in range(4):
                self.load(T(Sall.ap[:, 4 * g:4 * g + 4], self.sub(Sall, "Sall%d" % g)), sv[:, 4 * g:4 * g + 4])
            self.S.barrier()
            for q0 in range(0, 16, 2):
                ps = self.psum()
                for qi in range(2):
                    for c in range(4):
                        g = qi * 4 + c
                        self.tr(ps[:, g * 64:(g + 1) * 64], Sall.v(Sall.ap[:, q0 + qi, 2 * c:2 * c + 2, :].rearrange("p h k -> p (h k)")),
                                self.ident_f[0:64, 0:64])
                self.cp("act" if (q0 // 2) % 2 == 0 else "dve", H0f[:, q0:q0 + 2], ps.v(ps.ap.rearrange("p (q c v) -> p q c v", q=2, c=4)))
            self.release(m1)
            for _ in self._rwkv_chunk(K, self.alloc, True, 1024, True, False, (H0f, ssT, lastc)):
                pass
        self.release(mtop)
        if blk == 1:
            So = self.alloc("So", [64, 16, 8, 64])
            for q in range(16):
                ps = self.psum()
                for c in range(4):
                    self.tr(ps[0:64, c * 128:(c + 1) * 128], H0f[:, q, c, :], self.ident_f)
                self.cp("act" if q % 2 == 0 else "dve", So.v(So.ap[:, q].rearrange("p h k -> p (h k)")), ps[0:64, :])
                if q % 4 == 3:
                    self.store(io["wkv_s"].rearrange("q h v k -> v q h k")[:, q - 3:q + 1], So[:, q - 3:q + 1])
            self.release(mtop)

    def _rwkv_chunk(self, K, A, scoped, t0, samp, last_prompt, sx):
        io = self.io
        arT, btT, ktT, vsd, Et, gT, bv = K["arT"], K["btT"], K["ktT"], K["vsd"], K["Et"], K["gT"], K["bv"]
        Vtok, Btok, Ktok, oT, Usd, wzs, ogT = K["Vtok"], K["Btok"], K["Ktok"], K["oT"], K["Usd"], K["wzs"], K["ogT"]
        if samp:
            H0f, ssT, lastc = sx
        mk = (lambda: self.mark()) if scoped else (lambda: None)
        rl = (lambda m: self.release(m)) if scoped else (lambda m: None)
        m1 = mk()
        GR = {"L": [12, 13, 14], "K": [4, 5, 6, 7], "R": [0, 1, 2, 3], "V": [8, 9, 10, 11]}
        zs = {g: A("zs" + g, [128, len(GR[g]), 144]) for g in GR}
        xm = {g: A("xm" + g, [128, len(GR[g]), 128]) for g in GR}
        dzt = [A("dzt%d" % i, [128, 128]) for i in range(2)]
        tw = A("tw", [64, 128])
        sg0 = A("sg0", [128, 128], BF16)
        sg1 = A("sg1", [32, 128], BF16)
        adb = A("adb", [128, 128], BF16)
        sgw = A("sgw", [128, 4, 128])
        aT = A("aT", [128, 4, 128])
        cs = A("cs", [128, 4, 128])
        Ei = A("Ei", [128, 4, 128])
        Ep = A("Ep", [128, 4, 128])
        rn = A("rn", [128, 4, 128])
        kk = A("kk", [128, 4, 128])
        kh = A("kh", [128, 4, 128])
        sqb = A("sqb", [128, 4, 128], BF16)
        carry3 = self.carry.v(self.carry.ap.rearrange("p (c o) -> p c o", o=1))
        if samp:
            v3 = lambda t: t.v(t.ap.rearrange("p (q l) -> p q l", l=8))
            zq = {g: zs[g].v(zs[g].ap.rearrange("p c (q l) -> p c q l", l=9)) for g in GR}
            ss4 = ssT.v(ssT.ap.rearrange("p c (q o) -> p c q o", o=1))
            lc4 = lastc.v(lastc.ap.rearrange("p c (q o) -> p c q o", o=1))
        else:
            v3 = lambda t: t
        ndz = [0]

        def zs_group(g):
            ccs = GR[g]
            c0 = ccs[0]
            if not samp:
                self.cp("act", zs[g][:, :, 0:1], carry3[:, c0:c0 + len(ccs), :])
            else:
                self.cp("act", zq[g][:, :, :, 0:1], ss4[:, c0:c0 + len(ccs)])
            for i, cc in enumerate(ccs):
                n = 128 if cc < 14 else 32
                P = slice(0, n)
                ps = self.psum()
                for c in range(8):
                    self.mm(ps[:, 0:128], wzs[cc // 4][:, c, cc * 128:(cc + 1) * 128], self.hn(t0)[:, c, t0:t0 + 128], start=(c == 0), stop=(c == 7))
                if not samp:
                    self.cp("act", zs[g][P, i, 1:129], ps[P, 0:128])
                else:
                    self.cp("act", zq[g][P, i, :, 1:9], v3(ps[P, 0:128]))
                yield "s"
            if not samp:
                self.cp("act", carry3[:, c0:c0 + len(ccs), :], zs[g][:, :, 128:129])
            else:
                self.cp("act", lc4[:, c0:c0 + len(ccs)], zq[g][:, :, :, 8:9])

        def xm_group(g):
            for i, cc in enumerate(GR[g]):
                n = 128 if cc < 14 else 32
                P = slice(0, n)
                d = dzt[ndz[0] % 2]
                ndz[0] += 1
                if not samp:
                    cur_, prv_ = zs[g][P, i, 1:129], zs[g][P, i, 0:128]
                else:
                    cur_, prv_ = zq[g][P, i, :, 1:9], zq[g][P, i, :, 0:8]
                self.tt("dve", v3(d[P, :]), prv_, cur_, ALU.subtract)
                self.stt(v3(xm[g][P, i, :]), v3(d[P, :]), self.pc(R_MU + cc)[P], cur_, ALU.mult, ALU.add)

        r_, k_, v_, xl = xm["R"], xm["K"], xm["V"], xm["L"]
        yield from zs_group("L")
        yield from zs_group("K")
        yield from zs_group("R")
        yield from zs_group("V")
        yield "ZS_done"
        xm_group("L")
        self.act(tw, xl[0:64, 0, :], AF.Tanh)
        self.act(sg0, xl[:, 1, :], AF.Sigmoid)
        self.act(sg1, xl[0:32, 2, :], AF.Sigmoid)
        self.cp("act", adb[64:128, :], xl[64:128, 0, :])
        xm_group("K")
        yield "s"
        psW = self.psum()
        psA = self.psum()
        psG = self.psum()
        for c in range(4):
            cs_ = slice(c * 128, (c + 1) * 128)
            self.mm(psW[:, cs_], self.wdu[0:64, cs_], tw)
            self.mm(psA[:, cs_], self.wau[64:128, cs_], adb[64:128, :])
            self.mm(psG[:, cs_], self.wgu[:, 0, cs_], sg0, start=True, stop=False)
            self.mm(psG[:, cs_], self.wgu[0:32, 1, cs_], sg1, start=False, stop=True)
        for c in range(4):
            cs_ = slice(c * 128, (c + 1) * 128)
            self.act(sgw[:, c, :], psW[:, cs_], AF.Sigmoid, bias=self.pc(R_W0 + c))
            self.act(aT[:, c, :], psA[:, cs_], AF.Sigmoid, bias=self.pc(R_A0 + c))
        self.cp("act", gT, psG.v(psG.ap.rearrange("p (c t) -> p c t", c=4)))
        yield "s"
        for c in range(4):
            self.act(sqb[:, c, :], k_[:, c, :], AF.Square, scale=self.pc(R_KK + c))
        xm_group("R")
        xm_group("V")
        yield "XM_done"
        ps = self.psum()
        self.mm(ps, self.blk1b, sqb.v(sqb.ap.rearrange("p c t -> p (c t)")))
        rnf = rn.v(rn.ap.rearrange("p c t -> p (c t)"))
        self.ts("dve", rnf, ps, 1e-24, ALU.max)
        self.act(rnf, rnf, AF.Ln)
        self.act(rnf, rnf, AF.Exp, scale=-0.5)
        yield "s"
        rsm = self.rs_s.v(self.rs_s.ap.rearrange("p q l -> p (q l)")) if samp else self.rs_p
        for c in range(4):
            self.S.op("dve", lambda e: e.tensor_tensor_scan(out=_ap(cs[:, c, :]), data0=_ap(rsm), data1=_ap(sgw[:, c, :]), initial=0.0,
                                                            op0=ALU.mult, op1=ALU.add), reads=_res(rsm, sgw), writes=_res(cs))
        self.act(Et, cs, AF.Exp, scale=-C_DEC)
        self.act(Ei, cs, AF.Exp, scale=C_DEC)
        self.tt("dve", sgw, cs, sgw, ALU.subtract)
        self.act(Ep, sgw, AF.Exp, scale=-C_DEC)
        yield "s"
        for c in range(4):
            self.stt(kk[:, c, :], k_[:, c, :], self.pc(R_KK + c), rn[:, c, :], ALU.mult, ALU.mult)
        for c in range(4):
            self.ts("dve", kh[:, c, :], aT[:, c, :], self.pc(R_KA + c), ALU.mult, self.omka[:, c:c + 1], ALU.add)
        self.tt("dve", kh, kh, k_, ALU.mult)
        yield "s"
        self.tt("dve", rn, kk, aT, ALU.mult)
        self.stt(arT[:, :, 0, :], kk, -1.0, Ep, ALU.mult, ALU.mult)
        self.tt("dve", arT[:, :, 1, :], r_, Et, ALU.mult)
        self.tt("dve", btT, rn, Ei, ALU.mult)
        self.tt("dve", ktT, kh, Ei, ALU.mult)
        self.cp("act", vsd, v_)
        yield "s"
        for c in range(4):
            self.stt(sqb[:, c, :], r_[:, c, :], self.pc(R_RK + c), kh[:, c, :], ALU.mult, ALU.mult)
        ps = self.psum()
        self.mm(ps, self.blk1b, sqb.v(sqb.ap.rearrange("p c t -> p (c t)")))
        self.tt("dve", bv, ps.v(ps.ap.rearrange("p (c t) -> p c t", c=4)), v_, ALU.mult)
        yield "s"
        for (src_, dst) in ((vsd, Vtok), (btT, Btok), (ktT, Ktok)):
            ps = self.psum()
            pv_ = ps.v(ps.ap.bitcast(SD)) if SD != F32 else ps
            for c in range(4):
                self.tr(pv_[:, c * 128:(c + 1) * 128], src_[:, c, :], self.ident_s)
            self.cp("act", dst, pv_[:, 0:512])
            yield "s"
        rl(m1)
        yield "R1_done"
        m2 = mk()
        AR = A("AR", [128, 8, 512], SD)
        Pk = [[A("P%d_%d" % (i, g), [128, 4, 128], SD) for g in range(2)] for i in range(2)]
        Ptk = [[A("Pt%d_%d" % (i, g), [128, 4, 128], SD) for g in range(2)] for i in range(2)]
        TT = [A("TT%d" % g, [128, 4, 128], SD) for g in range(2)]
        Xsd = A("Xsd", [128, 512], SD)
        M4 = self.M4s if samp else self.M4p
        M4f = M4.v(M4.ap.rearrange("p a t -> p (a t)"))
        if samp:
            smv = self.seqmask.v(self.seqmask.ap.rearrange("p q a b -> p q (a b)"))
            amk = [A("amk%d" % i, [128, 16, 128], SD) for i in range(2)]
            rmk = [A("rmk%d" % i, [128, 16, 128], SD) for i in range(2)]
            H0s = [A("H0s%d" % i, [128, 16, 64], SD) for i in range(2)]
        hrows = lambda h: slice((h % 2) * 64, (h % 2) * 64 + 64)
        idb = self.ident_s.v(self.ident_s.ap.rearrange("p (o t) -> p o t", o=1)).bc([128, 4, 128])
        for g in range(2):
            pss = []
            for hi in range(4):
                h = g * 4 + hi
                c = h // 2
                rows = hrows(h)
                ps = self.psum()
                rhs = arT.v(arT.ap[rows, c].rearrange("p a t -> p (a t)"))
                self.mm(ps[:, 0:256], btT[rows, c, :], rhs)
                self.mm(ps[:, 256:512], ktT[rows, c, :], rhs)
                pss.append(ps)
            for hi in range(4):
                self.tt("dve", AR[:, g * 4 + hi, :], pss[hi], M4f, ALU.mult)
            ps2 = self.psum()
            p2 = ps2.v(ps2.ap.bitcast(SD)) if SD != F32 else ps2
            for hi in range(4):
                self.tr(p2[:, hi * 128:(hi + 1) * 128], AR[:, g * 4 + hi, 0:128], self.ident_s)
            self.cp("act", Pk[0][g], p2.v(p2.ap[:, 0:512].rearrange("p (h t) -> p h t", h=4)))
            self.tt("dve", TT[g], AR[:, g * 4:g * 4 + 4, 0:128], idb, ALU.add)
            yield "s"
        L = 3 if samp else 7
        for j in range(1, L + 1):
            cur_i, prev_i = j % 2, (j - 1) % 2
            for g in range(2):
                bP = self.psum() if j <= L - 1 else None
                bPt = self.psum() if j <= L - 2 else None
                bT = self.psum() if j >= 2 else None
                for hi in range(4):
                    h = g * 4 + hi
                    hs = slice(hi * 128, (hi + 1) * 128)
                    Pp = Pk[prev_i][g][:, hi, :]
                    Ptp = AR[:, h, 0:128] if j == 1 else Ptk[prev_i][g][:, hi, :]
                    if bP is not None:
                        self.mm(bP[:, hs], Ptp, Pp)
                    if bPt is not None:
                        self.mm(bPt[:, hs], Pp, Ptp)
                    if bT is not None:
                        self.mm(bT[:, hs], self.ident_s, TT[g][:, hi, :], start=True, stop=False)
                        self.mm(bT[:, hs], Pp, TT[g][:, hi, :], start=False, stop=True)
                v4 = lambda b: b.v(b.ap.rearrange("p (h t) -> p h t", h=4))
                if bP is not None:
                    self.cp("act", Pk[cur_i][g], v4(bP))
                if bPt is not None:
                    self.cp("act", Ptk[cur_i][g], v4(bPt))
                if bT is not None:
                    self.cp("dve" if g == 0 else "act", TT[g], v4(bT))
                yield "s"
        psX = self.psum()
        for h in range(8):
            c = h // 2
            rows = hrows(h)
            hs = slice(h * 64, (h + 1) * 64)
            if samp and h % 2 == 0:
                i = c % 2
                self.tt("dve", amk[i], arT[:, c, 0:1, :].bc([128, 16, 128]), smv, ALU.mult)
                self.cp("act", H0s[i], H0f[:, :, c, :])
            self.mm(psX[:, hs], AR[:, h, 256:384], Vtok[:, hs], start=True, stop=False)
            if not samp:
                self.mm(psX[:, hs], arT[rows, c, 0, :], self.Hsd[rows, c, :], start=False, stop=True)
            else:
                i = c % 2
                for q in range(16):
                    self.mm(psX[:, hs], amk[i][rows, q, :], H0s[i][rows, q, :], start=False, stop=(q == 15))
        self.cp("act", Xsd, psX)
        yield "s"
        psU = self.psum()
        for h in range(8):
            hs = slice(h * 64, (h + 1) * 64)
            self.mm(psU[:, hs], TT[h // 4][:, h % 4, :], Xsd[:, hs])
        self.cp("act", Usd, psU)
        yield "s"
        psO = self.psum()
        for h in range(8):
            c = h // 2
            rows = hrows(h)
            hs = slice(h * 64, (h + 1) * 64)
            out = psO[rows, c * 128:(c + 1) * 128]
            self.mm(out, Usd[:, hs], AR[:, h, 128:256], start=True, stop=False)
            self.mm(out, Vtok[:, hs], AR[:, h, 384:512], start=False, stop=False)
            if not samp:
                self.mm(out, self.Hsd[rows, c, :], arT[rows, c, 1, :], start=False, stop=True)
            else:
                i = c % 2
                if h % 2 == 0:
                    self.tt("dve", rmk[i], arT[:, c, 1:2, :].bc([128, 16, 128]), smv, ALU.mult)
                    self.cp("act", H0s[i], H0f[:, :, c, :])
                for q in range(16):
                    self.mm(out, H0s[i][rows, q, :], rmk[i][rows, q, :], start=False, stop=(q == 15))
        self.cp("act", oT, psO.v(psO.ap.rearrange("p (c t) -> p c t", c=4)))
        yield "s"
        if not samp:
            psH = self.psum()
            for h in range(8):
                c = h // 2
                rows = hrows(h)
                hs = slice(h * 64, (h + 1) * 64)
                out = psH[rows, c * 64:(c + 1) * 64]
                self.mm(out, Btok[:, hs], Usd[:, hs], start=True, stop=False)
                self.mm(out, Ktok[:, hs], Vtok[:, hs], start=False, stop=True)
            self.tt("dve", self.H, self.H, psH.v(psH.ap[:, 0:256].rearrange("p (c v) -> p c v", c=4)), ALU.add)
            self.tt("dve", self.H, self.H, Et[:, :, 127:128].bc([128, 4, 64]), ALU.mult)
            self.cp("act", self.Hsd, self.H)
        rl(m2)
        yield "R2a_done"
        m3 = mk()
        if scoped:
            dd = A("dd", [128, 512])
            sq2 = A("sq2", [128, 512])
            rs2 = A("rs2", [128, 512])
            o3 = A("o3", [128, 512])
        else:
            arf = AR.v(AR.ap.rearrange("p h t -> p (h t)").bitcast(F32))
            dd, sq2, rs2, o3 = arf[:, 0:512], arf[:, 512:1024], arf[:, 1024:1536], arf[:, 1536:2048]
        oTf = oT.v(oT.ap.rearrange("p c t -> p (c t)"))
        psM = self.psum()
        self.mm(psM, self.blk64, oTf)
        self.tt("dve", dd, oTf, psM, ALU.subtract)
        self.act(sq2, dd, AF.Square)
        psV = self.psum()
        self.mm(psV, self.blk64, sq2)
        self.act(rs2, psV, AF.Ln, bias=self.eps[:, 2:3])
        self.act(rs2, rs2, AF.Exp, scale=-0.5)
        yield "s"
        self.tt("dve", dd, dd, rs2, ALU.mult)
        for c in range(4):
            cs_ = slice(c * 128, (c + 1) * 128)
            self.stt(o3[:, cs_], dd[:, cs_], self.pc(R_GG + c), bv[:, c, :], ALU.mult, ALU.add)
            self.stt(ogT[:, c, t0:t0 + 128], o3[:, cs_], self.pc(R_GB + c), gT[:, c, :], ALU.add, ALU.mult)
        yield "s"
        if samp:
            shs = A("shs", [16, 1920])
            for g in range(4):
                ps = self.psum()
                for i in range(4):
                    cc = g * 4 + i
                    if cc >= 15:
                        break
                    n = 128 if cc < 14 else 32
                    self.tr(ps[0:16, i * 128:i * 128 + n], lastc[0:n, cc, :], self.ident_f[0:n, 0:n])
                w = 512 if g < 3 else 288
                self.cp("act", shs[0:16, g * 512:g * 512 + w], ps[0:16, 0:w])
            self.store(io["shift_s"], shs[0:16, 0:1824])
        if last_prompt:
            sho = A("sho", [1, 1920]) if scoped else arf[0:1, 0:1920]
            for g in range(4):
                ps = self.psum()
                for i in range(4):
                    cc = g * 4 + i
                    if cc >= 15:
                        break
                    n = 128 if cc < 14 else 32
                    self.tr(ps[0:1, i * 128:i * 128 + n], self.carry[0:n, cc:cc + 1], self.ident_f[0:n, 0:n])
                w = 512 if g < 3 else 288
                self.cp("act", sho[0:1, g * 512:g * 512 + w], ps[0:1, 0:w])
            self.store(io["shift_p"], sho[0:1, 0:1824])
            Sp = A("Sp", [64, 512])
            ps = self.psum()
            for c in range(4):
                self.tr(ps[0:64, c * 128:(c + 1) * 128], self.H[:, c, :], self.ident_f)
            self.cp("act", Sp, ps[0:64, :])
            self.store(io["wkv_p"].rearrange("(c hp) v k -> v c hp k", hp=2), Sp.v(Sp.ap.rearrange("p (c hp k) -> p c hp k", c=4, hp=2)))
        if samp:
            UVs = [A("UVm%d" % i, [128, 2, 2, 512], SD) for i in range(2)]
            wcs = Et.v(Et.ap.rearrange("p c (q l) -> p q c l", l=8))

            def build_uv(q0_):
                UVm_ = UVs[(q0_ // 2) % 2]
                for qi in range(2):
                    self.ts("dve", UVm_[:, qi, 0, :], Usd, self.rowmask[:, q0_ + qi:q0_ + qi + 1], ALU.mult)
                    self.ts("dve", UVm_[:, qi, 1, :], Vtok, self.rowmask[:, q0_ + qi:q0_ + qi + 1], ALU.mult)
            build_uv(0)
            for q0 in range(0, 16, 2):
                UVm = UVs[(q0 // 2) % 2]
                psH = self.psum()
                for qi in range(2):
                    for h in range(8):
                        c = h // 2
                        rows = hrows(h)
                        hs = slice(h * 64, (h + 1) * 64)
                        g = qi * 4 + c
                        out = psH[rows, g * 64:(g + 1) * 64]
                        self.mm(out, Btok[:, hs], UVm[:, qi, 0, hs], start=True, stop=False)
                        self.mm(out, Ktok[:, hs], UVm[:, qi, 1, hs], start=False, stop=True)
                if q0 + 2 < 16:
                    build_uv(q0 + 2)
                hv = H0f[:, q0:q0 + 2]
                self.tt("dve", hv, hv, psH.v(psH.ap.rearrange("p (q c v) -> p q c v", q=2, c=4)), ALU.add)
                self.tt("dve", hv, hv, wcs[:, q0:q0 + 2, :, 7:8].bc([128, 2, 4, 64]), ALU.mult)
        rl(m3)

    def finish(self):
        S = self.S
        need = {}
        for R in self.outres:
            for s, v in R.r.items():
                need[s] = max(need.get(s, 0), v)
        S._emit_waits("sp", need)
        S.barrier()


IN_SPECS = [
    ("xp", [2048, 1024]), ("xs", [128, 1024]), ("mem", [256, 1024]), ("swkv", [16, 8, 64, 64]),
    ("sshift", [16, 1824]), ("sconv", [480, 256]), ("ck", [16, 256, 256]), ("cv", [16, 256, 256]),
    ("pp", [NROWS, 128]), ("w_up1", [1024, 5632]), ("w_dn1", [2816, 1024]), ("w_in", [1024, 5664]),
    ("w_du", [64, 512]), ("w_au", [64, 512]), ("w_gu", [160, 512]), ("w_ro", [512, 1024]),
    ("w_co", [256, 1024]), ("w_kv", [1024, 512]), ("w_xo", [256, 1024]), ("w_o", [1024, 1024]),
    ("w_up2", [1024, 5632]), ("w_dn2", [2816, 1024]),
]
OUT_SPECS = [
    ("y_p", [2048, 1024]), ("y_s", [128, 1024]), ("wkv_p", [8, 64, 64]), ("shift_p", [1, 1824]),
    ("conv_p", [30, 256]), ("mk_p", [256, 256]), ("mv_p", [256, 256]), ("wkv_s", [16, 8, 64, 64]),
    ("shift_s", [16, 1824]), ("conv_s", [480, 256]),
]


def build_program(stage=99):
    nc = bass.Bass("TRN2", target_bir_lowering=False)
    io = {}
    for name, shape in IN_SPECS:
        io[name] = nc.dram_tensor(name, shape, F32, kind="ExternalInput").ap()
    for name, shape in OUT_SPECS:
        io[name] = nc.dram_tensor(name, shape, F32, kind="ExternalOutput").ap()
    with contextlib.ExitStack() as st:
        B = Builder(nc, st, io, stage=stage)
        B.setup()
        B.run_block(0)
        B.run_block(1)
        B.finish()
        print("ninst", B.S.ninst, "nwait", B.S.nwait, "ndma", B.S.ndma, "sems", len(B.S.sems))
    return nc


def pack_params(inp):
    def rows(a, pad=None):
        a = np.asarray(a, np.float32).reshape(-1)
        if pad is not None:
            a = np.concatenate([a, np.zeros(pad - a.size, np.float32)])
        return a.reshape(-1, 128)
    parts = [rows(inp["ffn1_norm"]), rows(inp["mix_norm"]), rows(inp["ffn2_norm"]), rows(inp["final_norm"]),
             rows(inp["mu_shift"], 15 * 128), rows(inp["w0"]), rows(inp["a0"]), rows(inp["k_k"]), rows(inp["k_a"]),
             rows(inp["r_k"]), rows(inp["gn_g"]), rows(inp["gn_b"]), rows(inp["glu_b"]), rows(inp["conv_w"]),
             rows(inp["conv_b"]), rows(inp["conv_ln_g"]), rows(inp["conv_ln_b"])]
    pp = np.concatenate(parts, 0)
    assert pp.shape == (NROWS, 128), pp.shape
    return np.ascontiguousarray(pp)


def make_in_maps(inp):
    f = lambda a: np.ascontiguousarray(np.asarray(a, np.float32))
    pp = pack_params(inp)
    shared = {
        "pp": pp, "w_up1": f(inp["ffn1_w_up"][0]), "w_dn1": f(inp["ffn1_w_down"][0]), "w_in": f(inp["w_in"][0]),
        "w_du": f(inp["w_decay_up"][0]), "w_au": f(inp["w_a_up"][0]), "w_gu": f(inp["w_g_up"][0]),
        "w_ro": f(inp["w_rwkv_out"][0]), "w_co": f(inp["w_conv_out"][0]), "w_kv": f(inp["w_mem_kv"][0]),
        "w_xo": f(inp["w_xattn_out"][0]), "w_o": f(inp["w_o"][0]), "w_up2": f(inp["ffn2_w_up"][0]),
        "w_dn2": f(inp["ffn2_w_down"][0]),
    }
    maps = []
    for c in range(8):
        sl = slice(16 * c, 16 * c + 16)
        m = dict(shared)
        m["xp"] = f(inp["x_prompt"][c])
        m["xs"] = f(inp["x_sample"][sl]).reshape(128, 1024)
        m["mem"] = f(inp["mem_prompt"][c])
        m["swkv"] = f(inp["state_wkv"][0, sl])
        m["sshift"] = f(inp["state_shift"][0, sl])
        m["sconv"] = f(inp["state_conv"][0, sl]).reshape(480, 256)
        m["ck"] = f(inp["cache_mem_k"][0, sl]).reshape(16, 256, 256)
        m["cv"] = f(inp["cache_mem_v"][0, sl]).reshape(16, 256, 256)
        maps.append(m)
    return maps


def gather(results):
    g = lambda k: [np.asarray(r[k], np.float32) for r in results]
    y_p = np.stack(g("y_p"), 0)
    y_s = np.concatenate(g("y_s"), 0).reshape(128, 8, 1024)
    wkv_p = np.stack(g("wkv_p"), 0)[None]
    shift_p = np.concatenate(g("shift_p"), 0)[None]
    conv_p = np.stack(g("conv_p"), 0)[None]
    mk_p = np.stack(g("mk_p"), 0).reshape(8, 256, 4, 64)[None]
    mv_p = np.stack(g("mv_p"), 0).reshape(8, 256, 4, 64)[None]
    wkv_s = np.concatenate(g("wkv_s"), 0)[None]
    shift_s = np.concatenate(g("shift_s"), 0)[None]
    conv_s = np.concatenate(g("conv_s"), 0).reshape(128, 30, 256)[None]
    return (y_p, y_s, wkv_p, shift_p, conv_p, mk_p, mv_p, wkv_s, shift_s, conv_s)


_NC_CACHE = {}


def kernel(**inputs):
    if "nc" not in _NC_CACHE:
        _NC_CACHE["nc"] = build_program()
    nc = _NC_CACHE["nc"]
    in_maps = make_in_maps(inputs)
    res = run_bass_kernel_spmd(nc, in_maps, core_ids=list(range(8)))
    return gather(res.results)
```

```python
import contextlib
import os
import numpy as np
import concourse.bass as bass
import concourse.mybir as mybir
from concourse.bass_utils import run_bass_kernel_spmd

F32 = mybir.dt.float32
BF16 = mybir.dt.bfloat16
AF = mybir.ActivationFunctionType
ALU = mybir.AluOpType
AX = mybir.AxisListType

SD = BF16
SAME_ENGINE_SYNC = True
ARENA_WORDS = 53200
DFF = 2816
NJ = 22
C_DEC = 0.6065306597126334

R_N1, R_NM, R_N2, R_NF = 0, 8, 16, 24
R_MU = 32
R_W0, R_A0, R_KK, R_KA, R_RK, R_GG, R_GB = 47, 51, 55, 59, 63, 67, 71
R_GLU = 75
R_CW = 79
R_CB, R_LG, R_LB = 141, 143, 145
NROWS = 147


class Res:
    __slots__ = ("name", "w", "r", "excl")

    def __init__(self, name="", excl=False):
        self.name = name
        self.w = None
        self.r = {}
        self.excl = excl


class T:
    __slots__ = ("ap", "res")

    def __init__(self, ap, res):
        self.ap = ap
        self.res = res

    def __getitem__(self, idx):
        return T(self.ap[idx], self.res)

    def v(self, ap):
        return T(ap, self.res)

    def bc(self, shape):
        return T(self.ap.broadcast_to(shape), self.res)


def _ap(x):
    return x.ap if isinstance(x, T) else x


def _res(*xs):
    out = []
    for x in xs:
        if isinstance(x, T):
            out.append(x.res)
    return out


class Sched:
    def __init__(self, nc, stack):
        self.nc = nc
        self.stack = stack
        self.eng = {"pe": nc.tensor, "act": nc.scalar, "dve": nc.vector, "pool": nc.gpsimd, "sp": nc.sync}
        self.sems = {}
        self.cnt = {}
        for k in self.eng:
            self.sems[k] = stack.enter_context(nc.semaphore("sem_" + k))
            self.cnt[k] = 0
        self.known = {k: {} for k in self.eng}
        self.ninst = {k: 0 for k in self.eng}
        self.nwait = 0
        self.ndma = 0
        self.res2sem = {}
        self.dma_free = []
        self.keep = []

    def _need(self, reads, writes, e=None):
        need = {}
        for R in reads:
            if R.w is not None:
                s, v = R.w
                if need.get(s, 0) < v:
                    need[s] = v
        for R in writes:
            if R.w is not None:
                s, v = R.w
                if s != e and need.get(s, 0) < v:
                    need[s] = v
            for s, v in R.r.items():
                if s != e and need.get(s, 0) < v:
                    need[s] = v
        return need

    def _emit_waits(self, e, need):
        kn = self.known[e]
        for s, v in need.items():
            if s == e and (not SAME_ENGINE_SYNC or e in ("pe", "sp")):
                continue
            if kn.get(s, 0) >= v:
                continue
            self.eng[e].wait_ge(self.sems[s], v)
            self.nwait += 1
            kn[s] = v

    def op(self, e, fn, reads=(), writes=()):
        ex = [R for R in reads if R.excl]
        if ex:
            writes = list(writes) + [R for R in ex if R not in writes]
            reads = [R for R in reads if not R.excl]
        self._emit_waits(e, self._need(reads, writes, e))
        ins = fn(self.eng[e])
        self.cnt[e] += 1
        ins.then_inc(self.sems[e], 1)
        tok = (e, self.cnt[e])
        self.ninst[e] += 1
        for R in writes:
            R.w = tok
            R.r = {}
        for R in reads:
            if R.r.get(e, 0) < tok[1]:
                R.r[e] = tok[1]

    def dma(self, q, out, in_, reads=(), writes=(), **kw):
        self._emit_waits(q, self._need(reads, writes))
        key = writes[0] if writes else reads[0]
        sk = self.res2sem.get(id(key))
        if sk is None:
            if self.dma_free:
                sk = self.dma_free.pop()
            else:
                sk = "dma_%d" % len(self.sems)
                self.sems[sk] = self.stack.enter_context(self.nc.semaphore("sd%d" % len(self.sems)))
                self.cnt[sk] = 0
            self.res2sem[id(key)] = sk
            self.keep.append(key)
        ins = self.eng[q].dma_start(out=out, in_=in_, **kw)
        self.cnt[sk] += 16
        ins.then_inc(self.sems[sk], 16)
        tok = (sk, self.cnt[sk])
        self.ndma += 1
        for R in writes:
            R.w = tok
            R.r = {}
        for R in reads:
            if R.r.get(sk, 0) < tok[1]:
                R.r[sk] = tok[1]

    def barrier(self):
        allc = {k: v for k, v in self.cnt.items() if v > 0}
        for e in self.eng:
            self._emit_waits(e, allc)
        self.dma_free.extend(self.res2sem.values())
        self.res2sem = {}


class Builder:
    def __init__(self, nc, st, io, stage=99, dbg=None):
        self.nc = nc
        self.st = st
        self.io = io
        self.stage = stage
        self.S = Sched(nc, st)
        self.arena = st.enter_context(nc.sbuf_tensor("arena", [128, ARENA_WORDS], F32))
        self.off = 0
        self.banks = []
        for i in range(8):
            p = st.enter_context(nc.psum_tensor("bank%d" % i, [128, 512], F32))
            self.banks.append(T(p[:], Res("bank%d" % i, excl=True)))
        self.bi = 0
        self.outres = []
        self.live = []
        self.freed = []
        self.rng_of = {}

    def alloc(self, name, shape, dt=F32):
        P = shape[0]
        fs = list(shape[1:])
        n = 1
        for d in fs:
            n *= d
        words = n if dt == F32 else (n + 1) // 2
        words = (words + 7) // 8 * 8
        assert self.off + words <= ARENA_WORDS, "arena overflow at %s (%d + %d)" % (name, self.off, words)
        ap = self.arena[0:P, self.off:self.off + words]
        rng = (self.off, self.off + words)
        self.off += words
        if dt != F32:
            ap = ap.bitcast(dt)
        ap = ap[:, 0:n]
        if len(fs) > 1:
            names = "abcdef"[:len(fs)]
            pat = "p (" + " ".join(names) + ") -> p " + " ".join(names)
            ap = ap.rearrange(pat, **{names[i]: fs[i] for i in range(len(fs))})
        res = Res(name)
        self._inherit(res, rng)
        self.live.append((rng[0], rng[1], res))
        t = T(ap, res)
        self.rng_of[id(res)] = rng
        return t

    def _inherit(self, res, rng):
        for (a, b, old) in self.freed:
            if a < rng[1] and rng[0] < b:
                toks = list(old.r.items())
                if old.w is not None:
                    toks.append(old.w)
                for s, v in toks:
                    if res.r.get(s, 0) < v:
                        res.r[s] = v

    def sub(self, base, name):
        rng = self.rng_of[id(base.res)]
        res = Res(name)
        self._inherit(res, rng)
        self.live.append((rng[0], rng[1], res))
        self.rng_of[id(res)] = rng
        return res

    def mark(self):
        return self.off

    def release(self, m, hard=False):
        keep = []
        for rec in self.live:
            if rec[0] >= m:
                self.freed.append(rec)
            else:
                keep.append(rec)
        self.live = keep
        self.off = m
        if hard:
            self.S.barrier()
            self.freed = []

    def psum(self):
        b = self.banks[self.bi]
        self.bi = (self.bi + 1) % 8
        return b

    def mm(self, out, lhsT, rhs, start=True, stop=True):
        self.S.op("pe", lambda e: e.matmul(_ap(out), lhsT=_ap(lhsT), rhs=_ap(rhs), start=start, stop=stop),
                  reads=_res(lhsT, rhs), writes=_res(out))

    def tr(self, out, in_, ident):
        self.S.op("pe", lambda e: e.transpose(_ap(out), _ap(in_), _ap(ident)), reads=_res(in_, ident), writes=_res(out))

    def act(self, out, in_, func, bias=None, scale=None, accum=None):
        kw = {}
        if bias is not None:
            kw["bias"] = _ap(bias)
        if scale is not None:
            kw["scale"] = _ap(scale)
        if accum is not None:
            kw["accum_out"] = _ap(accum)
        self.S.op("act", lambda e: e.activation(out=_ap(out), in_=_ap(in_), func=func, **kw),
                  reads=_res(in_, bias, scale), writes=_res(out, accum))

    def cp(self, eng, out, in_):
        if eng == "act":
            self.S.op("act", lambda e: e.copy(out=_ap(out), in_=_ap(in_)), reads=_res(in_), writes=_res(out))
        else:
            self.S.op(eng, lambda e: e.tensor_copy(out=_ap(out), in_=_ap(in_)), reads=_res(in_), writes=_res(out))

    def tt(self, eng, out, a, b, op):
        self.S.op(eng, lambda e: e.tensor_tensor(out=_ap(out), in0=_ap(a), in1=_ap(b), op=op), reads=_res(a, b), writes=_res(out))

    def ts(self, eng, out, a, s1, op0, s2=None, op1=None):
        if op1 is None:
            self.S.op(eng, lambda e: e.tensor_scalar(out=_ap(out), in0=_ap(a), scalar1=_ap(s1), scalar2=None, op0=op0),
                      reads=_res(a, s1), writes=_res(out))
        else:
            self.S.op(eng, lambda e: e.tensor_scalar(out=_ap(out), in0=_ap(a), scalar1=_ap(s1), scalar2=_ap(s2), op0=op0, op1=op1),
                      reads=_res(a, s1, s2), writes=_res(out))

    def stt(self, out, a, scalar, b, op0, op1):
        self.S.op("dve", lambda e: e.scalar_tensor_tensor(out=_ap(out), in0=_ap(a), scalar=_ap(scalar), in1=_ap(b), op0=op0, op1=op1),
                  reads=_res(a, scalar, b), writes=_res(out))

    def recip(self, out, in_):
        self.S.op("dve", lambda e: e.reciprocal(out=_ap(out), in_=_ap(in_)), reads=_res(in_), writes=_res(out))

    def memset(self, eng, out, val):
        self.S.op(eng, lambda e: e.memset(_ap(out), val), writes=_res(out))

    def asel(self, out, in_, pattern, cmp, fill, base, cm):
        self.S.op("pool", lambda e: e.affine_select(out=_ap(out), in_=_ap(in_), pattern=pattern, compare_op=cmp, fill=fill,
                                                    base=base, channel_multiplier=cm), reads=_res(in_), writes=_res(out))

    def load(self, out, src, q="sp"):
        self.S.dma(q, _ap(out), src, writes=_res(out))

    def store(self, dst, in_, q="sp"):
        self.S.dma(q, dst, _ap(in_), reads=_res(in_))
        self.outres.append(in_.res)

    def setup(self):
        io = self.io
        A = self.alloc
        self.ident_f = A("ident_f", [128, 128])
        self.memset("pool", self.ident_f, 0.0)
        self.asel(self.ident_f, self.ident_f, [[-1, 128]], ALU.not_equal, 1.0, 0, 1)
        self.ident_b = A("ident_b", [128, 128], BF16)
        self.cp("dve", self.ident_b, self.ident_f)
        self.ident_s = self.ident_b if SD == BF16 else self.ident_f
        self.ones_b = A("ones_b", [128, 128], BF16)
        self.memset("pool", self.ones_b, 1.0)
        self.blk64 = A("blk64", [128, 128])
        self.blk1 = A("blk1", [128, 128])
        self.memset("pool", self.blk64, 0.0)
        self.memset("pool", self.blk1, 0.0)
        for lo in (0, 64):
            self.memset("pool", self.blk64[lo:lo + 64, lo:lo + 64], 1.0 / 64)
            self.memset("pool", self.blk1[lo:lo + 64, lo:lo + 64], 1.0)
        self.blk1b = A("blk1b", [128, 128], BF16)
        self.cp("pool", self.blk1b, self.blk1)
        self.ones256 = A("ones256", [128, 128])
        self.memset("pool", self.ones256, 1.0 / 256)
        self.M4p = A("M4p", [128, 4, 128])
        self.M4s = A("M4s", [128, 4, 128])
        self.seqmask = A("seqmask", [128, 16, 16, 8], SD)
        self.seqmask_b = self.seqmask
        self.rowmask = A("rowmask", [128, 16])
        self.rs_p = A("rs_p", [128, 128])
        self.rs_s = A("rs_s", [128, 16, 8])
        self.eps = A("eps", [128, 4])
        self.memset("pool", self.eps[:, 0:1], 1e-6)
        self.memset("pool", self.eps[:, 1:2], 1e-5)
        self.memset("pool", self.eps[:, 2:3], 64e-5)
        self.memset("pool", self.eps[:, 3:4], 0.0)
        self.PC = A("PC", [128, NROWS])
        self.omka = A("omka", [128, 4])
        self.xT = A("xT", [128, 8, 1152])
        self.hnT = A("hnT", [128, 8, 1152], BF16)
        self.xres = [self.sub(self.xT, "xT%d" % i) for i in range(3)]
        self.hres_ = [self.sub(self.hnT, "hnT%d" % i) for i in range(3)]
        self.H = A("H", [128, 4, 64])
        self.Hsd = A("Hsd", [128, 4, 64], SD)
        self.memset("dve", self.H, 0.0)
        self.memset("dve", self.Hsd, 0.0)
        self.carry = A("carry", [128, 15])
        self.memset("dve", self.carry, 0.0)
        self.utail = A("utail", [128, 2, 30])
        self.memset("dve", self.utail, 0.0)
        self.KTp = A("KTp", [128, 2, 256], BF16)
        self.Vp = A("Vp", [128, 2, 256], BF16)
        self.wdu = A("wdu", [64, 512])
        self.wau = A("wau", [128, 512], BF16)
        self.wgu = A("wgu", [128, 2, 512], BF16)
        self.load(self.wdu, io["w_du"])
        self.load(self.wau[64:128, :], io["w_au"], q="pool")
        self.load(self.wgu[:, 0, :], io["w_gu"][0:128, :], q="pool")
        self.load(self.wgu[0:32, 1, :], io["w_gu"][128:160, :], q="pool")
        self.xpre_mark = self.mark()
        self.xpre = self.x_prefetch(0)
        mtmp = self.mark()
        pin = A("pin", [128, 2, 128])
        self.load(pin[:, 0, :], io["pp"][0:128, :])
        self.load(pin[0:NROWS - 128, 1, :], io["pp"][128:NROWS, :])
        ps = self.psum()
        self.tr(ps[:, 0:128], pin[:, 0, :], self.ident_f)
        self.tr(ps[:, 128:128 + NROWS - 128], pin[0:NROWS - 128, 1, :], self.ident_f[0:NROWS - 128, 0:NROWS - 128])
        self.cp("dve", self.PC, ps[:, 0:NROWS])
        self.ts("dve", self.omka, self.PC[:, R_KA:R_KA + 4], -1.0, ALU.mult, 1.0, ALU.add)
        self.release(mtmp)

    def setup_late(self):
        A = self.alloc
        self.memset("pool", self.seqmask, 1.0)
        self.asel(self.seqmask, self.seqmask, [[-1, 16], [1, 16], [0, 8]], ALU.is_equal, 0.0, 0, 0)
        self.memset("pool", self.rowmask, 1.0)
        self.asel(self.rowmask, self.rowmask, [[-8, 16]], ALU.is_ge, 0.0, 0, 1)
        self.asel(self.rowmask, self.rowmask, [[8, 16]], ALU.is_ge, 0.0, 7, -1)
        self.memset("pool", self.rs_p, 1.0)
        self.memset("pool", self.rs_p[:, 0:1], 0.0)
        self.memset("pool", self.rs_s, 1.0)
        self.memset("pool", self.rs_s[:, :, 0:1], 0.0)
        mtmp = self.mark()
        su = A("su", [128, 128])
        iu = A("iu", [128, 128])
        for m in (su, iu):
            self.memset("pool", m, 1.0)
        self.asel(su, su, [[1, 128]], ALU.is_gt, 0.0, 0, -1)
        self.asel(iu, iu, [[1, 128]], ALU.is_ge, 0.0, 0, -1)
        bm = A("bm", [128, 16, 8])
        self.memset("pool", bm, 1.0)
        self.asel(bm, bm, [[-8, 16], [0, 8]], ALU.is_ge, 0.0, 0, 1)
        self.asel(bm, bm, [[8, 16], [0, 8]], ALU.is_ge, 0.0, 7, -1)
        bm2 = bm.v(bm.ap.rearrange("p a b -> p (a b)"))
        for i in range(4):
            self.cp("pool", self.M4p[:, i, :], su if i % 2 == 0 else iu)
            self.tt("pool", self.M4s[:, i, :], su if i % 2 == 0 else iu, bm2, ALU.mult)
        self.release(mtmp)

    def dump(self, name, t, n):
        import os
        if os.environ.get("DBG_BLK") is None:
            return
        dt = t.ap.dtype
        d = self.nc.dram_tensor("dbg_" + name, [t.ap.shape[0], n], dt, kind="ExternalOutput").ap()
        src_ap = t.ap
        if len(src_ap.shape) > 2:
            names = "abcdef"[:len(src_ap.shape) - 1]
            src_ap = src_ap.rearrange("p " + " ".join(names) + " -> p (" + " ".join(names) + ")")
        self.S.dma("sp", d, src_ap, reads=_res(t))
        self.outres.append(t.res)

    def xt(self, t0):
        return T(self.xT.ap, self.xres[t0 // 512])

    def hn(self, t0):
        return T(self.hnT.ap, self.hres_[t0 // 512])

    def pc(self, row):
        return self.PC[:, row:row + 1]

    def x_sources(self, blk):
        io = self.io
        if blk == 0:
            return [(io["xp"][ch * 128:(ch + 1) * 128, :], ch * 128) for ch in range(8)]
        return [(io["xp"][1024 + ch * 128:1024 + (ch + 1) * 128, :], ch * 128) for ch in range(8)] + [(io["xs"], 1024)]

    def x_prefetch(self, blk, nslots=4):
        xin = [self.alloc("xin%d" % i, [128, 1024]) for i in range(nslots)]
        srcs = self.x_sources(blk)
        for i in range(min(nslots, len(srcs))):
            self.load(xin[i], srcs[i][0], q="sp")
        return xin

    def load_x(self, blk, pre=None):
        srcs = self.x_sources(blk)
        m = self.mark()
        if pre is None:
            xin = self.x_prefetch(blk)
        else:
            xin = pre
        ns = len(xin)
        for ch, (sap, tok) in enumerate(srcs):
            xi = xin[ch % ns]
            if ch >= ns:
                self.load(xi, sap, q="pool")
            for g in range(2):
                ps = self.psum()
                for c in range(4):
                    cc = g * 4 + c
                    self.tr(ps[:, c * 128:(c + 1) * 128], xi[:, cc * 128:(cc + 1) * 128], self.ident_f)
                dst = self.xt(tok)[:, g * 4:(g + 1) * 4, tok:tok + 128]
                self.cp("act" if g == 0 else "dve", dst, ps.v(ps.ap.rearrange("p (c t) -> p c t", c=4)))
        if pre is None:
            self.release(m)

    def rmsnorm(self, row, tiles, out, local=False):
        m_ = self.mark()
        sqs = [self.alloc("sq%d" % i, [128, 8, 512], BF16) for i in range(2)]
        rstds = [self.alloc("rstd%d" % i, [128, 512]) for i in range(2)]
        for k, (t0, tn) in enumerate(tiles):
            sq, rstd = sqs[k % 2], rstds[k % 2]
            self.act(sq[:, :, 0:tn], self.xt(t0)[:, :, t0:t0 + tn], AF.Square)
            ps = self.psum()
            for c in range(8):
                self.mm(ps[:, 0:tn], self.ones_b, sq[:, c, 0:tn], start=(c == 0), stop=(c == 7))
            self.act(rstd[:, 0:tn], ps[:, 0:tn], AF.Ln, bias=self.eps[:, 0:1], scale=1.0 / 1024)
            self.act(rstd[:, 0:tn], rstd[:, 0:tn], AF.Exp, scale=-0.5)
            o0 = 0 if local else t0
            for c in range(8):
                o_ = out if local else self.hn(t0)
                self.stt(o_[:, c, o0:o0 + tn], self.xt(t0)[:, c, t0:t0 + tn], self.pc(row + c), rstd[:, 0:tn], ALU.mult, ALU.mult)
        self.release(m_)

    def ffn(self, w_up, w_dn, nrow, tiles, NT):
        io = self.io
        self.rmsnorm(nrow, tiles, self.hnT)
        m = self.mark()
        hT = self.alloc("hT", [128, NJ, 1152], BF16)
        hres = [self.sub(hT, "hT%d" % i) for i in range(len(tiles))]
        sg = [self.alloc("sg%d" % i, [128, 512], BF16) for i in range(2)]
        wdn = [self.alloc("wdn%d" % i, [128, NJ, 128], BF16) for i in range(3)]
        m_up = self.mark()
        wup = [self.alloc("wup%d" % i, [128, 8, 2, 256], BF16) for i in range(3)]
        wv = w_up.rearrange("(c p) f -> p c f", p=128)
        dv = w_dn.rearrange("(j p) d -> p j d", p=128)
        nsg = 0
        for jj in range(NJ // 2):
            wb = wup[jj % 3]
            self.load(wb[:, :, 0, :], wv[:, :, jj * 256:(jj + 1) * 256], q="pool")
            self.load(wb[:, :, 1, :], wv[:, :, DFF + jj * 256: DFF + (jj + 1) * 256], q="pool")
            if jj < 3:
                self.load(wdn[jj], dv[:, :, jj * 128:(jj + 1) * 128], q="pool")
            for j2 in range(2):
                j = jj * 2 + j2
                for ti, (t0, tn) in enumerate(tiles):
                    pg = self.psum()
                    pv = self.psum()
                    for c in range(8):
                        self.mm(pg[:, 0:tn], wb[:, c, 0, j2 * 128:(j2 + 1) * 128], self.hn(t0)[:, c, t0:t0 + tn], start=(c == 0), stop=(c == 7))
                    for c in range(8):
                        self.mm(pv[:, 0:tn], wb[:, c, 1, j2 * 128:(j2 + 1) * 128], self.hn(t0)[:, c, t0:t0 + tn], start=(c == 0), stop=(c == 7))
                    s = sg[nsg % 2]
                    nsg += 1
                    self.act(s[:, 0:tn], pg[:, 0:tn], AF.Silu)
                    self.tt("dve", T(hT.ap[:, j, t0:t0 + tn], hres[ti]), s[:, 0:tn], pv[:, 0:tn], ALU.mult)
        self.release(m_up)
        wdn = wdn + [self.alloc("wdn%d" % i, [128, NJ, 128], BF16) for i in range(3, 8)]
        for dc in range(3, 8):
            self.load(wdn[dc], dv[:, :, dc * 128:(dc + 1) * 128], q="pool")
        for ti, (t0, tn) in enumerate(tiles):
            for dc in range(8):
                wd = wdn[dc]
                ps = self.psum()
                for j in range(NJ):
                    self.mm(ps[:, 0:tn], wd[:, j, :], T(hT.ap[:, j, t0:t0 + tn], hres[ti]), start=(j == 0), stop=(j == NJ - 1))
                self.stt(self.xt(t0)[:, dc, t0:t0 + tn], ps[:, 0:tn], 0.5, self.xt(t0)[:, dc, t0:t0 + tn], ALU.mult, ALU.add)
        self.release(m)

    def final_out(self, dst, tok0, nchunks):
        m = self.mark()
        yTs = [self.alloc("yT%d" % i, [128, 8, 512]) for i in range(2 if nchunks > 4 else 1)]
        yo = [self.alloc("yo%d" % i, [128, 1024]) for i in range(4)]
        done = 0
        k = 0
        kt = 0
        while done < nchunks:
            nch = min(4, nchunks - done)
            t0 = tok0 + done * 128
            tn = nch * 128
            yT = yTs[kt % len(yTs)]
            kt += 1
            self.rmsnorm(R_NF, [(t0, tn)], yT, local=True)
            for ch in range(nch):
                y = yo[k % 4]
                k += 1
                for g in range(2):
                    ps = self.psum()
                    for c in range(4):
                        self.tr(ps[:, c * 128:(c + 1) * 128], yT[:, g * 4 + c, ch * 128:(ch + 1) * 128], self.ident_f)
                    self.cp("act" if g == 0 else "dve", y[:, g * 512:(g + 1) * 512], ps)
                self.store(dst[(done + ch) * 128:(done + ch + 1) * 128, :], y)
            done += nch
        self.release(m)

    def run_block(self, blk):
        io = self.io
        if blk == 0:
            tiles = [(0, 512), (512, 512)]
            NT = 1024
            self.load_x(0, self.xpre)
            self.release(self.xpre_mark)
        else:
            tiles = [(0, 512), (512, 512), (1024, 128)]
            NT = 1152
            self.load_x(1, self.xpre)
            self.release(self.xpre_mark)
        self.ffn(io["w_up1"], io["w_dn1"], R_N1, tiles, NT)
        if blk == 0:
            self.setup_late()
        if self.stage >= 2:
            self.mixer(blk, tiles, NT)
        if self.stage >= 3:
            self.ffn(io["w_up2"], io["w_dn2"], R_N2, tiles, NT)
        if blk == 0:
            self.xpre_mark = self.mark()
            self.xpre = self.x_prefetch(1)
            self.final_out(io["y_p"][0:1024, :], 0, 8)
        else:
            self.final_out(io["y_p"][1024:2048, :], 0, 8)
            self.final_out(io["y_s"], 1024, 1)
        self.release(self.mark(), hard=(os.environ.get("HARD_BLK", "0") == "1"))

    def mixer(self, blk, tiles, NT):
        self.rmsnorm(R_NM, tiles, self.hnT)
        m0 = self.mark()
        ogT = self.alloc("ogT", [128, 4, 1152], BF16)
        self.wv_in = self.io["w_in"].rearrange("(c p) f -> p c f", p=128)
        import os
        parts = os.environ.get("MIX_PARTS", "conv,xattn,rwkv,merge").split(",")
        if "rwkv" in parts:
            self.rwkv_branch(blk, ogT)
        else:
            self.memset("dve", ogT, 0.0)
        csT = self.alloc("csT", [128, 2, 1152], BF16)
        ocT = self.alloc("ocT", [128, 2, 1152], BF16)
        for nm, tl in (("conv", csT), ("xattn", ocT)):
            if nm not in parts:
                self.memset("dve", tl, 0.0)
        KTs = None
        if blk == 1 and "xattn" in parts:
            KTs = self.alloc("KTs", [128, 2, 16, 256], BF16)
            mk_ = self.mark()
            ckb = self.alloc("ckb", [128, 16, 2, 256], BF16)
            ld_ck = lambda: self.load(ckb, self.io["ck"].rearrange("q (mc p) c -> p q mc c", p=128), q="pool")
            if "conv" not in parts:
                ld_ck()
        else:
            ld_ck = None
        if "conv" in parts:
            self.conv_branch(blk, tiles, csT, ld_ck)
        if KTs is not None:
            for q in range(16):
                ps = self.psum()
                pb16 = ps.v(ps.ap.bitcast(BF16))
                for cc in range(2):
                    for mc in range(2):
                        o = cc * 256 + mc * 128
                        self.tr(pb16[:, o:o + 128], ckb[:, q, mc, cc * 128:(cc + 1) * 128], self.ident_b)
                self.cp("act" if q % 2 == 0 else "dve", KTs[:, :, q, :], pb16.v(pb16.ap[:, 0:512].rearrange("p (c m) -> p c m", c=2)))
            self.release(mk_)
        pre_w = None
        if "xattn" in parts and "merge" in parts:
            pw0 = self.alloc("wg0", [128, 8, 3, 128], BF16)
            pwr = self.alloc("wro", [128, 4, 1024], BF16)
            pre_w = (pw0, pwr)
            gv_ = self.wv_in[:, :, 2592:5664].rearrange("p c (b f) -> p c b f", b=3)

            def ld_pre():
                for b in range(3):
                    self.load(pw0[:, :, b, :], gv_[:, :, b, 0:128], q="pool")
                self.load(pwr, self.io["w_ro"].rearrange("(c p) f -> p c f", p=128), q="pool")
        else:
            ld_pre = None
        if "xattn" in parts:
            self.xattn_branch(blk, tiles, ocT, KTs, ld_pre)
        if os.environ.get("DBG_BLK") == str(blk):
            self.dump("csT", csT, 2 * 1152)
            self.dump("ocT", ocT, 2 * 1152)
            self.dump("ogT", ogT, 4 * 1152)
            self.dump("hnT", self.hnT, 8 * 1152)
        if "merge" in parts:
            self.merge(blk, tiles, csT, ocT, ogT, pre_w)
        self.release(m0)

    def conv_branch(self, blk, tiles, csT, after_wc=None):
        io = self.io
        m = self.mark()
        wc = self.alloc("wc", [128, 8, 512], BF16)
        self.load(wc, self.wv_in[:, :, 1824:2336], q="pool")
        if after_wc is not None:
            after_wc()
        uP = self.alloc("uP", [128, 2, 1054])
        cT = self.alloc("cT", [128, 2, 1152])
        sgl = self.alloc("sgl", [128, 512])
        self.cp("act", uP[:, :, 0:30], self.utail)
        if blk == 1:
            uS = self.alloc("uS", [128, 2, 16, 38])
            sc = self.alloc("sc", [120, 4, 256])
            self.load(sc, io["sconv"].rearrange("(g r) c -> r g c", r=120))
            for g in range(4):
                ps = self.psum()
                for ch in range(2):
                    self.tr(ps[:, ch * 120:(ch + 1) * 120], sc[0:120, g, ch * 128:(ch + 1) * 128], self.ident_f[0:120, 0:120])
                self.cp("act", uS[:, :, 4 * g:4 * g + 4, 0:30], ps.v(ps.ap[:, 0:240].rearrange("p (c q t) -> p c q t", c=2, q=4)))
        for (t0, tn) in tiles:
            for ch in range(2):
                pa = self.psum()
                pb = self.psum()
                for c in range(8):
                    self.mm(pa[:, 0:tn], wc[:, c, ch * 128:(ch + 1) * 128], self.hn(t0)[:, c, t0:t0 + tn], start=(c == 0), stop=(c == 7))
                for c in range(8):
                    self.mm(pb[:, 0:tn], wc[:, c, 256 + ch * 128:256 + (ch + 1) * 128], self.hn(t0)[:, c, t0:t0 + tn], start=(c == 0), stop=(c == 7))
                self.act(sgl[:, 0:tn], pb[:, 0:tn], AF.Sigmoid, bias=self.pc(R_GLU + 2 + ch))
                if t0 < 1024:
                    self.stt(uP[:, ch, 30 + t0:30 + t0 + tn], pa[:, 0:tn], self.pc(R_GLU + ch), sgl[:, 0:tn], ALU.add, ALU.mult)
                else:
                    self.stt(uS[:, ch, :, 30:38], pa.v(pa.ap[:, 0:128].rearrange("p (q t) -> p q t", q=16)), self.pc(R_GLU + ch),
                             sgl.v(sgl.ap[:, 0:128].rearrange("p (q t) -> p q t", q=16)), ALU.add, ALU.mult)
        uPb = self.alloc("uPb", [128, 2, 1054], BF16)
        self.cp("act", uPb[:, 0, :], uP[:, 0, :])
        self.cp("dve", uPb[:, 1, :], uP[:, 1, :])
        if blk == 1:
            uSb = self.alloc("uSb", [128, 2, 16, 38], BF16)
            self.cp("act", uSb, uS)
        dg = [self.alloc("dg%d" % i, [128, 128], BF16) for i in range(4)]
        nd = 0
        for ch in range(2):
            pts = [self.psum(), self.psum()]
            pss_ = self.psum() if blk == 1 else None
            for w in range(31):
                d = dg[nd % 4]
                nd += 1
                self.ts("dve", d, self.ident_b, self.pc(R_CW + 2 * w + ch), ALU.mult)
                for ti in range(2):
                    self.mm(pts[ti], d, uPb[:, ch, w + ti * 512:w + ti * 512 + 512], start=(w == 0), stop=(w == 30))
                if blk == 1:
                    self.mm(pss_[:, 0:128], d, uSb[:, ch, :, w:w + 8], start=(w == 0), stop=(w == 30))
            for ti in range(2):
                self.act(cT[:, ch, ti * 512:(ti + 1) * 512], pts[ti], AF.Identity, bias=self.pc(R_CB + ch))
            if blk == 1:
                self.act(cT[:, ch, 1024:1152], pss_[:, 0:128], AF.Identity, bias=self.pc(R_CB + ch))
        self.cp("act", self.utail, uP[:, :, 1024:1054])
        if blk == 1:
            cvo = self.alloc("cvo", [30, 256])
            ps = self.psum()
            for ch in range(2):
                self.tr(ps[0:30, ch * 128:(ch + 1) * 128], uP[:, ch, 1024:1054], self.ident_f)
            self.cp("act", cvo, ps[0:30, 0:256])
            self.store(io["conv_p"], cvo)
            cso = self.alloc("cso", [120, 4, 256])
            tmpc = self.alloc("tmpc", [128, 2, 120])
            for g in range(4):
                ps = self.psum()
                self.cp("act", tmpc.v(tmpc.ap.rearrange("p c (q t) -> p c q t", q=4)), uS[:, :, 4 * g:4 * g + 4, 8:38])
                for ch in range(2):
                    self.tr(ps[0:120, ch * 128:(ch + 1) * 128], tmpc[:, ch, :], self.ident_f)
                self.cp("act", cso[:, g, :], ps[0:120, 0:256])
            self.store(io["conv_s"].rearrange("(g r) c -> r g c", r=120), cso)
        nt_ = len(tiles)
        sqf = [self.alloc("sqf%d" % i, [128, 2, 512]) for i in range(nt_)]
        rsd = [self.alloc("rsd%d" % i, [128, 512]) for i in range(nt_)]
        cres = [self.sub(cT, "cT%d" % i) for i in range(nt_)]
        cTt = [T(cT.ap, cres[i]) for i in range(nt_)]
        pms = []
        for i, (t0, tn) in enumerate(tiles):
            pm = self.psum()
            for ch in range(2):
                self.mm(pm[:, 0:tn], self.ones256, cT[:, ch, t0:t0 + tn], start=(ch == 0), stop=(ch == 1))
            pms.append(pm)
        for i, (t0, tn) in enumerate(tiles):
            for ch in range(2):
                self.tt("dve", cTt[i][:, ch, t0:t0 + tn], cT[:, ch, t0:t0 + tn], pms[i][:, 0:tn], ALU.subtract)
            self.act(sqf[i][:, :, 0:tn], cTt[i][:, :, t0:t0 + tn], AF.Square)
        pvs = []
        for i, (t0, tn) in enumerate(tiles):
            pv = self.psum()
            for ch in range(2):
                self.mm(pv[:, 0:tn], self.ones256, sqf[i][:, ch, 0:tn], start=(ch == 0), stop=(ch == 1))
            pvs.append(pv)
        for i, (t0, tn) in enumerate(tiles):
            self.act(rsd[i][:, 0:tn], pvs[i][:, 0:tn], AF.Ln, bias=self.eps[:, 1:2])
            self.act(rsd[i][:, 0:tn], rsd[i][:, 0:tn], AF.Exp, scale=-0.5)
        for i, (t0, tn) in enumerate(tiles):
            for ch in range(2):
                self.tt("dve", cTt[i][:, ch, t0:t0 + tn], cTt[i][:, ch, t0:t0 + tn], rsd[i][:, 0:tn], ALU.mult)
                self.act(csT[:, ch, t0:t0 + tn], cTt[i][:, ch, t0:t0 + tn], AF.Silu, bias=self.pc(R_LB + ch), scale=self.pc(R_LG + ch))
        self.release(m)

    def xattn_branch(self, blk, tiles, ocT, KTs=None, after_loads=None):
        io = self.io
        m = self.mark()
        wq = self.alloc("wq", [128, 8, 256], BF16)
        self.load(wq, self.wv_in[:, :, 2336:2592], q="pool")
        qT = self.alloc("qT", [128, 2, 1152], BF16)
        for (t0, tn) in tiles:
            for cc in range(2):
                ps = self.psum()
                for c in range(8):
                    self.mm(ps[:, 0:tn], wq[:, c, cc * 128:(cc + 1) * 128], self.hn(t0)[:, c, t0:t0 + tn], start=(c == 0), stop=(c == 7))
                self.cp("act", qT[:, cc, t0:t0 + tn], ps[:, 0:tn])
        if blk == 0:
            m1 = self.mark()
            memT = self.alloc("memT", [128, 8, 256], BF16)
            wkv = self.alloc("wkv", [128, 8, 512], BF16)
            self.load(wkv, io["w_kv"].rearrange("(c p) f -> p c f", p=128), q="pool")
            mi = self.alloc("mi", [128, 2, 1024])
            kvo = self.alloc("kvo", [128, 2, 512])
            for mc in range(2):
                self.load(mi[:, mc, :], io["mem"][mc * 128:(mc + 1) * 128, :])
                for g in range(2):
                    ps = self.psum()
                    for c in range(4):
                        self.tr(ps[:, c * 128:(c + 1) * 128], mi[:, mc, (g * 4 + c) * 128:(g * 4 + c + 1) * 128], self.ident_f)
                    self.cp("act", memT[:, g * 4:(g + 1) * 4, mc * 128:(mc + 1) * 128], ps.v(ps.ap.rearrange("p (c t) -> p c t", c=4)))
            for cc in range(2):
                ps = self.psum()
                for c in range(8):
                    self.mm(ps[:, 0:256], wkv[:, c, cc * 128:(cc + 1) * 128], memT[:, c, :], start=(c == 0), stop=(c == 7))
                self.cp("act", self.KTp[:, cc, :], ps[:, 0:256])
            for mc in range(2):
                ps = self.psum()
                for c in range(8):
                    self.mm(ps, memT[:, c, mc * 128:(mc + 1) * 128], wkv[:, c, :], start=(c == 0), stop=(c == 7))
                self.cp("act", kvo[:, mc, :], ps)
                self.cp("dve", self.Vp[:, mc, :], ps[:, 256:512])
                self.store(io["mk_p"][mc * 128:(mc + 1) * 128, :], kvo[:, mc, 0:256])
                self.store(io["mv_p"][mc * 128:(mc + 1) * 128, :], kvo[:, mc, 256:512])
            self.release(m1)
        WT = []
        for i in range(2):
            WT.append(dict(mx=self.alloc("mx%d" % i, [128, 4]), nmx=self.alloc("nmx%d" % i, [128, 4]), rsum=self.alloc("rsum%d" % i, [128, 4]),
                           rinv=self.alloc("rinv%d" % i, [128, 4]), Pb=self.alloc("Pb%d" % i, [128, 4, 256], BF16),
                           PT=self.alloc("PT%d" % i, [128, 8, 128], BF16),
                           octok=self.alloc("octok%d" % i, [128, 256], BF16)))
        chunks = [(ch * 128, False) for ch in range(8)]
        if blk == 1:
            chunks.append((1024, True))
            cvb = self.alloc("cvb", [128, 16, 2, 256], BF16)
            self.load(cvb, io["cv"].rearrange("q (mc p) c -> p q mc c", p=128), q="pool")
        if after_loads is not None:
            after_loads()
        if blk == 1:
            qmask = self.alloc("qmask", [128, 2, 16, 128], BF16)
            PTm = [self.alloc("PTm%d" % i, [128, 2, 16, 128], BF16) for i in range(2)]
            smv = self.seqmask_b.v(self.seqmask_b.ap.rearrange("p q a b -> p q (a b)"))
        def xchunk(t0, samp, W):
            mx, nmx, rsum, rinv, Pb, PT, octok = W["mx"], W["nmx"], W["rsum"], W["rinv"], W["Pb"], W["PT"], W["octok"]
            psA = [self.psum(), self.psum()]
            if samp:
                for cc in range(2):
                    self.tt("dve", qmask[:, cc], qT[:, cc:cc + 1, t0:t0 + 128].bc([128, 16, 128]), smv, ALU.mult)
            for h in range(4):
                rows = slice((h % 2) * 64, (h % 2) * 64 + 64)
                out = psA[h % 2][:, (h // 2) * 256:(h // 2) * 256 + 256]
                if not samp:
                    self.mm(out, qT[rows, h // 2, t0:t0 + 128], self.KTp[rows, h // 2, :])
                else:
                    for q in range(16):
                        self.mm(out, qmask[rows, h // 2, q, :], KTs[rows, h // 2, q, :], start=(q == 0), stop=(q == 15))
            for i in range(2):
                self.S.op("dve", lambda e: e.tensor_reduce(out=_ap(mx[:, i:4:2]), in_=psA[i].ap.rearrange("p (h m) -> p h m", h=2),
                                                           axis=AX.X, op=ALU.max), reads=_res(psA[i]), writes=_res(mx))
            self.ts("dve", nmx, mx, -0.125, ALU.mult)
            for h in range(4):
                self.act(Pb[:, h, :], psA[h % 2][:, (h // 2) * 256:(h // 2) * 256 + 256], AF.Exp, bias=nmx[:, h:h + 1], scale=0.125)
            self.S.op("dve", lambda e: e.tensor_reduce(out=_ap(rsum), in_=_ap(Pb), axis=AX.X, op=ALU.add), reads=_res(Pb), writes=_res(rsum))
            self.recip(rinv, rsum)
            yield
            ps = self.psum()
            pb16 = ps.v(ps.ap.bitcast(BF16))
            for h in range(4):
                for mc in range(2):
                    j = h * 2 + mc
                    self.tr(pb16[:, j * 128:(j + 1) * 128], Pb[:, h, mc * 128:(mc + 1) * 128], self.ident_b)
            self.cp("act", PT, pb16.v(pb16.ap.rearrange("p (j t) -> p j t", j=8)))
            yield
            pso = self.psum()
            for h in range(4):
                o = pso[:, h * 64:(h + 1) * 64]
                if not samp:
                    for mc in range(2):
                        self.mm(o, PT[:, h * 2 + mc, :], self.Vp[:, mc, h * 64:(h + 1) * 64], start=(mc == 0), stop=(mc == 1))
                else:
                    pm = PTm[h % 2]
                    for mc in range(2):
                        self.tt("dve", pm[:, mc], PT[:, h * 2 + mc:h * 2 + mc + 1, :].bc([128, 16, 128]), smv, ALU.mult)
                    n = 0
                    for q in range(16):
                        for mc in range(2):
                            self.mm(o, pm[:, mc, q, :], cvb[:, q, mc, h * 64:(h + 1) * 64], start=(n == 0), stop=(n == 31))
                            n += 1
            self.tt("dve", octok.v(octok.ap.rearrange("p (h d) -> p h d", h=4)), pso.v(pso.ap[:, 0:256].rearrange("p (h d) -> p h d", h=4)),
                    rinv.v(rinv.ap.rearrange("p (h o) -> p h o", o=1)).bc([128, 4, 64]), ALU.mult)
            yield
            ps2 = self.psum()
            p216 = ps2.v(ps2.ap.bitcast(BF16))
            for cc in range(2):
                self.tr(p216[:, cc * 128:(cc + 1) * 128], octok[:, cc * 128:(cc + 1) * 128], self.ident_b)
            self.cp("act", ocT[:, :, t0:t0 + 128], p216.v(p216.ap[:, 0:256].rearrange("p (c t) -> p c t", c=2)))
            yield
        gens = [xchunk(t0, samp, WT[i % 2]) for i, (t0, samp) in enumerate(chunks)]
        active = []
        gi = 0
        while gi < len(gens) or active:
            while len(active) < 2 and gi < len(gens):
                active.append(gens[gi])
                gi += 1
            for g in list(active):
                try:
                    next(g)
                except StopIteration:
                    active.remove(g)
        self.release(m)

    def merge(self, blk, tiles, csT, ocT, ogT, pre_w=None):
        io = self.io
        m = self.mark()
        wro = pre_w[1] if pre_w is not None else self.alloc("wro", [128, 4, 1024], BF16)
        wco = self.alloc("wco", [128, 2, 1024], BF16)
        wxo = self.alloc("wxo", [128, 2, 1024], BF16)
        wo = self.alloc("wo", [128, 8, 1024], BF16)
        wg = [pre_w[0] if (i == 0 and pre_w is not None) else self.alloc("wg%d" % i, [128, 8, 3, 128], BF16) for i in range(3)]
        mT = self.alloc("mT", [128, 8, 1152], BF16)
        gs = [self.alloc("gs%d" % i, [128, 512], BF16) for i in range(6)]
        tm = [self.alloc("tm%d" % i, [128, 512]) for i in range(6)]
        gv = self.wv_in[:, :, 2592:5664].rearrange("p c (b f) -> p c b f", b=3)
        ng = 0
        nt = 0
        def ldg(dc_):
            for b in range(3):
                self.load(wg[dc_ % 3][:, :, b, :], gv[:, :, b, dc_ * 128:(dc_ + 1) * 128], q="pool")
        if pre_w is None:
            ldg(0)
            self.load(wro, io["w_ro"].rearrange("(c p) f -> p c f", p=128), q="pool")
        self.load(wco, io["w_co"].rearrange("(c p) f -> p c f", p=128), q="pool")
        self.load(wxo, io["w_xo"].rearrange("(c p) f -> p c f", p=128), q="pool")
        ldg(1)
        self.load(wo, io["w_o"].rearrange("(c p) f -> p c f", p=128), q="pool")
        for dc in range(8):
            w = wg[dc % 3]
            if dc + 2 < 8:
                ldg(dc + 2)
            for (t0, tn) in tiles:
                g3 = []
                for b in range(3):
                    ps = self.psum()
                    for c in range(8):
                        self.mm(ps[:, 0:tn], w[:, c, b, :], self.hn(t0)[:, c, t0:t0 + tn], start=(c == 0), stop=(c == 7))
                    g = gs[ng % 6]
                    ng += 1
                    self.act(g[:, 0:tn], ps[:, 0:tn], AF.Sigmoid)
                    g3.append(g)
                ys = []
                for (wt, src, nk) in ((wro, ogT, 4), (wco, csT, 2), (wxo, ocT, 2)):
                    ps = self.psum()
                    for c in range(nk):
                        self.mm(ps[:, 0:tn], wt[:, c, dc * 128:(dc + 1) * 128], src[:, c, t0:t0 + tn], start=(c == 0), stop=(c == nk - 1))
                    ys.append(ps)
                t0_ = tm[nt % 6]
                t1_ = tm[(nt + 1) % 6]
                t2_ = tm[(nt + 2) % 6]
                nt += 3
                self.tt("dve", t0_[:, 0:tn], g3[0][:, 0:tn], ys[0][:, 0:tn], ALU.mult)
                self.tt("dve", t1_[:, 0:tn], g3[1][:, 0:tn], ys[1][:, 0:tn], ALU.mult)
                self.tt("dve", t2_[:, 0:tn], g3[2][:, 0:tn], ys[2][:, 0:tn], ALU.mult)
                self.tt("dve", t0_[:, 0:tn], t0_[:, 0:tn], t1_[:, 0:tn], ALU.add)
                self.tt("dve", mT[:, dc, t0:t0 + tn], t0_[:, 0:tn], t2_[:, 0:tn], ALU.add)
        import os
        if os.environ.get("DBG_BLK") == str(blk):
            self.dump("mT", mT, 8 * 1152)
        for (t0, tn) in tiles:
            for dc in range(8):
                ps = self.psum()
                for c in range(8):
                    self.mm(ps[:, 0:tn], wo[:, c, dc * 128:(dc + 1) * 128], mT[:, c, t0:t0 + tn], start=(c == 0), stop=(c == 7))
                self.tt("dve", self.xt(t0)[:, dc, t0:t0 + tn], self.xt(t0)[:, dc, t0:t0 + tn], ps[:, 0:tn], ALU.add)
        self.release(m)

    def rwkv_branch(self, blk, ogT):
        io = self.io
        mtop = self.mark()
        wzs = self.alloc("wzs", [128, 8, 1920], BF16)
        wzg = [None] * 4
        for gi_, (c0, cn) in ((3, (1536, 288)), (1, (512, 512)), (0, (0, 512)), (2, (1024, 512))):
            tg = T(wzs.ap, self.sub(wzs, "wzs%d" % c0))
            if c0 == 1536:
                self.memset("dve", T(wzs.ap[:, :, 1824:1920], tg.res), 0.0)
            self.S.dma("pool", wzs.ap[:, :, c0:c0 + cn], self.wv_in[:, :, c0:c0 + cn], writes=[tg.res])
            wzg[gi_] = tg
        Ks = []
        for i in range(2):
            K = {}
            for nm, shp, dt in (("arT", [128, 4, 2, 128], SD), ("btT", [128, 4, 128], SD), ("ktT", [128, 4, 128], SD),
                                ("vsd", [128, 4, 128], SD), ("Et", [128, 4, 128], F32), ("gT", [128, 4, 128], F32),
                                ("bv", [128, 4, 128], F32), ("Vtok", [128, 512], SD), ("Btok", [128, 512], SD),
                                ("Ktok", [128, 512], SD), ("oT", [128, 4, 128], F32), ("Usd", [128, 512], SD)):
                K[nm] = self.alloc(nm + str(i), shp, dt)
            K["wzs"] = wzg
            K["ogT"] = ogT
            Ks.append(K)
        K = Ks[0]
        mp = self.mark()
        cache = {}
        alias = {"Sp": "aT"}

        def A_cached(name, shape, dt=F32):
            if name in alias:
                base = cache[alias[name]]
                ap = base.ap
                if len(ap.shape) > 2:
                    names = "abcdef"[:len(ap.shape) - 1]
                    ap = ap.rearrange("p " + " ".join(names) + " -> p (" + " ".join(names) + ")")
                return T(ap[0:shape[0], 0:shape[1]], base.res)
            if name not in cache:
                cache[name] = self.alloc(name, shape, dt)
            return cache[name]

        gens = [self._rwkv_chunk(Ks[ch % 2], A_cached, False, ch * 128, False, blk == 1 and ch == 7, None) for ch in range(8)]

        def run_until(g, tag):
            for t in g:
                if t == tag:
                    return

        st = [0] * 9

        def step(k):
            if st[k] == 4:
                return
            try:
                t = next(gens[k])
            except StopIteration:
                st[k] = 4
                return
            if t == "ZS_done":
                st[k] = 1
            elif t == "XM_done":
                st[k] = 2
            elif t == "R1_done":
                st[k] = 3
        st[8] = 4
        while st[0] < 3:
            step(0)
        for i in range(8):
            while st[i] < 4 or (i + 1 < 8 and st[i + 1] < 3):
                if st[i] < 4:
                    step(i)
                if i + 1 < 8 and st[i + 1] < 3:
                    step(i + 1)
                if i + 2 < 8 and st[i + 1] >= 2 and st[i + 2] < 1:
                    step(i + 2)
        self.release(mp)
        if blk == 1:
            A = self.alloc
            H0f = A("H0f", [128, 16, 4, 64])
            ssT = A("ssT", [128, 15, 16])
            lastc = A("lastc", [128, 15, 16])
            m1 = self.mark()
            sst = A("sst", [16, 1824])
            self.load(sst, io["sshift"])
            ps = self.psum()
            for cc in range(15):
                n = 128 if cc < 14 else 32
                self.tr(ps[0:n, cc * 16:(cc + 1) * 16], sst[0:16, cc * 128:cc * 128 + n], self.ident_f[0:16, 0:16])
            self.cp("act", ssT[:, 0:14, :], ps.v(ps.ap[:, 0:224].rearrange("p (c q) -> p c q", c=14)))
            self.cp("act", ssT[0:32, 14, :], ps[0:32, 224:240])
            Sall = A("Sall", [64, 16, 8, 64])
            sv = io["swkv"].rearrange("q h v k -> v q h k")
            for g in range(4):
                self.load(T(Sall.ap[:, 4 * g:4 * g + 4], self.sub(Sall, "Sall%d" % g)), sv[:, 4 * g:4 * g + 4])
            self.S.barrier()
            for q0 in range(0, 16, 2):
                ps = self.psum()
                for qi in range(2):
                    for c in range(4):
                        g = qi * 4 + c
                        self.tr(ps[:, g * 64:(g + 1) * 64], Sall.v(Sall.ap[:, q0 + qi, 2 * c:2 * c + 2, :].rearrange("p h k -> p (h k)")),
                                self.ident_f[0:64, 0:64])
                self.cp("act" if (q0 // 2) % 2 == 0 else "dve", H0f[:, q0:q0 + 2], ps.v(ps.ap.rearrange("p (q c v) -> p q c v", q=2, c=4)))
            self.release(m1)
            for _ in self._rwkv_chunk(K, self.alloc, True, 1024, True, False, (H0f, ssT, lastc)):
                pass
        self.release(mtop)
        if blk == 1:
            So = self.alloc("So", [64, 16, 8, 64])
            for q in range(16):
                ps = self.psum()
                for c in range(4):
                    self.tr(ps[0:64, c * 128:(c + 1) * 128], H0f[:, q, c, :], self.ident_f)
                self.cp("act" if q % 2 == 0 else "dve", So.v(So.ap[:, q].rearrange("p h k -> p (h k)")), ps[0:64, :])
                if q % 4 == 3:
                    self.store(io["wkv_s"].rearrange("q h v k -> v q h k")[:, q - 3:q + 1], So[:, q - 3:q + 1])
            self.release(mtop)

    def _rwkv_chunk(self, K, A, scoped, t0, samp, last_prompt, sx):
        io = self.io
        arT, btT, ktT, vsd, Et, gT, bv = K["arT"], K["btT"], K["ktT"], K["vsd"], K["Et"], K["gT"], K["bv"]
        Vtok, Btok, Ktok, oT, Usd, wzs, ogT = K["Vtok"], K["Btok"], K["Ktok"], K["oT"], K["Usd"], K["wzs"], K["ogT"]
        if samp:
            H0f, ssT, lastc = sx
        mk = (lambda: self.mark()) if scoped else (lambda: None)
        rl = (lambda m: self.release(m)) if scoped else (lambda m: None)
        m1 = mk()
        GR = {"L": [12, 13, 14], "K": [4, 5, 6, 7], "R": [0, 1, 2, 3], "V": [8, 9, 10, 11]}
        zs = {g: A("zs" + g, [128, len(GR[g]), 144]) for g in GR}
        xm = {g: A("xm" + g, [128, len(GR[g]), 128]) for g in GR}
        dzt = [A("dzt%d" % i, [128, 128]) for i in range(2)]
        tw = A("tw", [64, 128])
        sg0 = A("sg0", [128, 128], BF16)
        sg1 = A("sg1", [32, 128], BF16)
        adb = A("adb", [128, 128], BF16)
        sgw = A("sgw", [128, 4, 128])
        aT = A("aT", [128, 4, 128])
        cs = A("cs", [128, 4, 128])
        Ei = A("Ei", [128, 4, 128])
        Ep = A("Ep", [128, 4, 128])
        rn = A("rn", [128, 4, 128])
        kk = A("kk", [128, 4, 128])
        kh = A("kh", [128, 4, 128])
        sqb = A("sqb", [128, 4, 128], BF16)
        carry3 = self.carry.v(self.carry.ap.rearrange("p (c o) -> p c o", o=1))
        if samp:
            v3 = lambda t: t.v(t.ap.rearrange("p (q l) -> p q l", l=8))
            zq = {g: zs[g].v(zs[g].ap.rearrange("p c (q l) -> p c q l", l=9)) for g in GR}
            ss4 = ssT.v(ssT.ap.rearrange("p c (q o) -> p c q o", o=1))
            lc4 = lastc.v(lastc.ap.rearrange("p c (q o) -> p c q o", o=1))
        else:
            v3 = lambda t: t
        ndz = [0]

        def zs_group(g):
            ccs = GR[g]
            c0 = ccs[0]
            if not samp:
                self.cp("act", zs[g][:, :, 0:1], carry3[:, c0:c0 + len(ccs), :])
            else:
                self.cp("act", zq[g][:, :, :, 0:1], ss4[:, c0:c0 + len(ccs)])
            for i, cc in enumerate(ccs):
                n = 128 if cc < 14 else 32
                P = slice(0, n)
                ps = self.psum()
                for c in range(8):
                    self.mm(ps[:, 0:128], wzs[cc // 4][:, c, cc * 128:(cc + 1) * 128], self.hn(t0)[:, c, t0:t0 + 128], start=(c == 0), stop=(c == 7))
                if not samp:
                    self.cp("act", zs[g][P, i, 1:129], ps[P, 0:128])
                else:
                    self.cp("act", zq[g][P, i, :, 1:9], v3(ps[P, 0:128]))
                yield "s"
            if not samp:
                self.cp("act", carry3[:, c0:c0 + len(ccs), :], zs[g][:, :, 128:129])
            else:
                self.cp("act", lc4[:, c0:c0 + len(ccs)], zq[g][:, :, :, 8:9])

        def xm_group(g):
            for i, cc in enumerate(GR[g]):
                n = 128 if cc < 14 else 32
                P = slice(0, n)
                d = dzt[ndz[0] % 2]
                ndz[0] += 1
                if not samp:
                    cur_, prv_ = zs[g][P, i, 1:129], zs[g][P, i, 0:128]
                else:
                    cur_, prv_ = zq[g][P, i, :, 1:9], zq[g][P, i, :, 0:8]
                self.tt("dve", v3(d[P, :]), prv_, cur_, ALU.subtract)
                self.stt(v3(xm[g][P, i, :]), v3(d[P, :]), self.pc(R_MU + cc)[P], cur_, ALU.mult, ALU.add)

        r_, k_, v_, xl = xm["R"], xm["K"], xm["V"], xm["L"]
        yield from zs_group("L")
        yield from zs_group("K")
        yield from zs_group("R")
        yield from zs_group("V")
        yield "ZS_done"
        xm_group("L")
        self.act(tw, xl[0:64, 0, :], AF.Tanh)
        self.act(sg0, xl[:, 1, :], AF.Sigmoid)
        self.act(sg1, xl[0:32, 2, :], AF.Sigmoid)
        self.cp("act", adb[64:128, :], xl[64:128, 0, :])
        xm_group("K")
        yield "s"
        psW = self.psum()
        psA = self.psum()
        psG = self.psum()
        for c in range(4):
            cs_ = slice(c * 128, (c + 1) * 128)
            self.mm(psW[:, cs_], self.wdu[0:64, cs_], tw)
            self.mm(psA[:, cs_], self.wau[64:128, cs_], adb[64:128, :])
            self.mm(psG[:, cs_], self.wgu[:, 0, cs_], sg0, start=True, stop=False)
            self.mm(psG[:, cs_], self.wgu[0:32, 1, cs_], sg1, start=False, stop=True)
        for c in range(4):
            cs_ = slice(c * 128, (c + 1) * 128)
            self.act(sgw[:, c, :], psW[:, cs_], AF.Sigmoid, bias=self.pc(R_W0 + c))
            self.act(aT[:, c, :], psA[:, cs_], AF.Sigmoid, bias=self.pc(R_A0 + c))
        self.cp("act", gT, psG.v(psG.ap.rearrange("p (c t) -> p c t", c=4)))
        yield "s"
        for c in range(4):
            self.act(sqb[:, c, :], k_[:, c, :], AF.Square, scale=self.pc(R_KK + c))
        xm_group("R")
        xm_group("V")
        yield "XM_done"
        ps = self.psum()
        self.mm(ps, self.blk1b, sqb.v(sqb.ap.rearrange("p c t -> p (c t)")))
        rnf = rn.v(rn.ap.rearrange("p c t -> p (c t)"))
        self.ts("dve", rnf, ps, 1e-24, ALU.max)
        self.act(rnf, rnf, AF.Ln)
        self.act(rnf, rnf, AF.Exp, scale=-0.5)
        yield "s"
        rsm = self.rs_s.v(self.rs_s.ap.rearrange("p q l -> p (q l)")) if samp else self.rs_p
        for c in range(4):
            self.S.op("dve", lambda e: e.tensor_tensor_scan(out=_ap(cs[:, c, :]), data0=_ap(rsm), data1=_ap(sgw[:, c, :]), initial=0.0,
                                                            op0=ALU.mult, op1=ALU.add), reads=_res(rsm, sgw), writes=_res(cs))
        self.act(Et, cs, AF.Exp, scale=-C_DEC)
        self.act(Ei, cs, AF.Exp, scale=C_DEC)
        self.tt("dve", sgw, cs, sgw, ALU.subtract)
        self.act(Ep, sgw, AF.Exp, scale=-C_DEC)
        yield "s"
        for c in range(4):
            self.stt(kk[:, c, :], k_[:, c, :], self.pc(R_KK + c), rn[:, c, :], ALU.mult, ALU.mult)
        for c in range(4):
            self.ts("dve", kh[:, c, :], aT[:, c, :], self.pc(R_KA + c), ALU.mult, self.omka[:, c:c + 1], ALU.add)
        self.tt("dve", kh, kh, k_, ALU.mult)
        yield "s"
        self.tt("dve", rn, kk, aT, ALU.mult)
        self.stt(arT[:, :, 0, :], kk, -1.0, Ep, ALU.mult, ALU.mult)
        self.tt("dve", arT[:, :, 1, :], r_, Et, ALU.mult)
        self.tt("dve", btT, rn, Ei, ALU.mult)
        self.tt("dve", ktT, kh, Ei, ALU.mult)
        self.cp("act", vsd, v_)
        yield "s"
        for c in range(4):
            self.stt(sqb[:, c, :], r_[:, c, :], self.pc(R_RK + c), kh[:, c, :], ALU.mult, ALU.mult)
        ps = self.psum()
        self.mm(ps, self.blk1b, sqb.v(sqb.ap.rearrange("p c t -> p (c t)")))
        self.tt("dve", bv, ps.v(ps.ap.rearrange("p (c t) -> p c t", c=4)), v_, ALU.mult)
        yield "s"
        for (src_, dst) in ((vsd, Vtok), (btT, Btok), (ktT, Ktok)):
            ps = self.psum()
            pv_ = ps.v(ps.ap.bitcast(SD)) if SD != F32 else ps
            for c in range(4):
                self.tr(pv_[:, c * 128:(c + 1) * 128], src_[:, c, :], self.ident_s)
            self.cp("act", dst, pv_[:, 0:512])
            yield "s"
        rl(m1)
        yield "R1_done"
        m2 = mk()
        AR = A("AR", [128, 8, 512], SD)
        Pk = [[A("P%d_%d" % (i, g), [128, 4, 128], SD) for g in range(2)] for i in range(2)]
        Ptk = [[A("Pt%d_%d" % (i, g), [128, 4, 128], SD) for g in range(2)] for i in range(2)]
        TT = [A("TT%d" % g, [128, 4, 128], SD) for g in range(2)]
        Xsd = A("Xsd", [128, 512], SD)
        M4 = self.M4s if samp else self.M4p
        M4f = M4.v(M4.ap.rearrange("p a t -> p (a t)"))
        if samp:
            smv = self.seqmask.v(self.seqmask.ap.rearrange("p q a b -> p q (a b)"))
            amk = [A("amk%d" % i, [128, 16, 128], SD) for i in range(2)]
            rmk = [A("rmk%d" % i, [128, 16, 128], SD) for i in range(2)]
            H0s = [A("H0s%d" % i, [128, 16, 64], SD) for i in range(2)]
        hrows = lambda h: slice((h % 2) * 64, (h % 2) * 64 + 64)
        idb = self.ident_s.v(self.ident_s.ap.rearrange("p (o t) -> p o t", o=1)).bc([128, 4, 128])
        for g in range(2):
            pss = []
            for hi in range(4):
                h = g * 4 + hi
                c = h // 2
                rows = hrows(h)
                ps = self.psum()
                rhs = arT.v(arT.ap[rows, c].rearrange("p a t -> p (a t)"))
                self.mm(ps[:, 0:256], btT[rows, c, :], rhs)
                self.mm(ps[:, 256:512], ktT[rows, c, :], rhs)
                pss.append(ps)
            for hi in range(4):
                self.tt("dve", AR[:, g * 4 + hi, :], pss[hi], M4f, ALU.mult)
            ps2 = self.psum()
            p2 = ps2.v(ps2.ap.bitcast(SD)) if SD != F32 else ps2
            for hi in range(4):
                self.tr(p2[:, hi * 128:(hi + 1) * 128], AR[:, g * 4 + hi, 0:128], self.ident_s)
            self.cp("act", Pk[0][g], p2.v(p2.ap[:, 0:512].rearrange("p (h t) -> p h t", h=4)))
            self.tt("dve", TT[g], AR[:, g * 4:g * 4 + 4, 0:128], idb, ALU.add)
            yield "s"
        L = 3 if samp else 7
        for j in range(1, L + 1):
            cur_i, prev_i = j % 2, (j - 1) % 2
            for g in range(2):
                bP = self.psum() if j <= L - 1 else None
                bPt = self.psum() if j <= L - 2 else None
                bT = self.psum() if j >= 2 else None
                for hi in range(4):
                    h = g * 4 + hi
                    hs = slice(hi * 128, (hi + 1) * 128)
                    Pp = Pk[prev_i][g][:, hi, :]
                    Ptp = AR[:, h, 0:128] if j == 1 else Ptk[prev_i][g][:, hi, :]
                    if bP is not None:
                        self.mm(bP[:, hs], Ptp, Pp)
                    if bPt is not None:
                        self.mm(bPt[:, hs], Pp, Ptp)
                    if bT is not None:
                        self.mm(bT[:, hs], self.ident_s, TT[g][:, hi, :], start=True, stop=False)
                        self.mm(bT[:, hs], Pp, TT[g][:, hi, :], start=False, stop=True)
                v4 = lambda b: b.v(b.ap.rearrange("p (h t) -> p h t", h=4))
                if bP is not None:
                    self.cp("act", Pk[cur_i][g], v4(bP))
                if bPt is not None:
                    self.cp("act", Ptk[cur_i][g], v4(bPt))
                if bT is not None:
                    self.cp("dve" if g == 0 else "act", TT[g], v4(bT))
                yield "s"
        psX = self.psum()
        for h in range(8):
            c = h // 2
            rows = hrows(h)
            hs = slice(h * 64, (h + 1) * 64)
            if samp and h % 2 == 0:
                i = c % 2
                self.tt("dve", amk[i], arT[:, c, 0:1, :].bc([128, 16, 128]), smv, ALU.mult)
                self.cp("act", H0s[i], H0f[:, :, c, :])
            self.mm(psX[:, hs], AR[:, h, 256:384], Vtok[:, hs], start=True, stop=False)
            if not samp:
                self.mm(psX[:, hs], arT[rows, c, 0, :], self.Hsd[rows, c, :], start=False, stop=True)
            else:
                i = c % 2
                for q in range(16):
                    self.mm(psX[:, hs], amk[i][rows, q, :], H0s[i][rows, q, :], start=False, stop=(q == 15))
        self.cp("act", Xsd, psX)
        yield "s"
        psU = self.psum()
        for h in range(8):
            hs = slice(h * 64, (h + 1) * 64)
            self.mm(psU[:, hs], TT[h // 4][:, h % 4, :], Xsd[:, hs])
        self.cp("act", Usd, psU)
        yield "s"
        psO = self.psum()
        for h in range(8):
            c = h // 2
            rows = hrows(h)
            hs = slice(h * 64, (h + 1) * 64)
            out = psO[rows, c * 128:(c + 1) * 128]
            self.mm(out, Usd[:, hs], AR[:, h, 128:256], start=True, stop=False)
            self.mm(out, Vtok[:, hs], AR[:, h, 384:512], start=False, stop=False)
            if not samp:
                self.mm(out, self.Hsd[rows, c, :], arT[rows, c, 1, :], start=False, stop=True)
            else:
                i = c % 2
                if h % 2 == 0:
                    self.tt("dve", rmk[i], arT[:, c, 1:2, :].bc([128, 16, 128]), smv, ALU.mult)
                    self.cp("act", H0s[i], H0f[:, :, c, :])
                for q in range(16):
                    self.mm(out, H0s[i][rows, q, :], rmk[i][rows, q, :], start=False, stop=(q == 15))
        self.cp("act", oT, psO.v(psO.ap.rearrange("p (c t) -> p c t", c=4)))
        yield "s"
        if not samp:
            psH = self.psum()
            for h in range(8):
                c = h // 2
                rows = hrows(h)
                hs = slice(h * 64, (h + 1) * 64)
                out = psH[rows, c * 64:(c + 1) * 64]
                self.mm(out, Btok[:, hs], Usd[:, hs], start=True, stop=False)
                self.mm(out, Ktok[:, hs], Vtok[:, hs], start=False, stop=True)
            self.tt("dve", self.H, self.H, psH.v(psH.ap[:, 0:256].rearrange("p (c v) -> p c v", c=4)), ALU.add)
            self.tt("dve", self.H, self.H, Et[:, :, 127:128].bc([128, 4, 64]), ALU.mult)
            self.cp("act", self.Hsd, self.H)
        rl(m2)
        yield "R2a_done"
        m3 = mk()
        if scoped:
            dd = A("dd", [128, 512])
            sq2 = A("sq2", [128, 512])
            rs2 = A("rs2", [128, 512])
            o3 = A("o3", [128, 512])
        else:
            arf = AR.v(AR.ap.rearrange("p h t -> p (h t)").bitcast(F32))
            dd, sq2, rs2, o3 = arf[:, 0:512], arf[:, 512:1024], arf[:, 1024:1536], arf[:, 1536:2048]
        oTf = oT.v(oT.ap.rearrange("p c t -> p (c t)"))
        psM = self.psum()
        self.mm(psM, self.blk64, oTf)
        self.tt("dve", dd, oTf, psM, ALU.subtract)
        self.act(sq2, dd, AF.Square)
        psV = self.psum()
        self.mm(psV, self.blk64, sq2)
        self.act(rs2, psV, AF.Ln, bias=self.eps[:, 2:3])
        self.act(rs2, rs2, AF.Exp, scale=-0.5)
        yield "s"
        self.tt("dve", dd, dd, rs2, ALU.mult)
        for c in range(4):
            cs_ = slice(c * 128, (c + 1) * 128)
            self.stt(o3[:, cs_], dd[:, cs_], self.pc(R_GG + c), bv[:, c, :], ALU.mult, ALU.add)
            self.stt(ogT[:, c, t0:t0 + 128], o3[:, cs_], self.pc(R_GB + c), gT[:, c, :], ALU.add, ALU.mult)
        yield "s"
        if samp:
            shs = A("shs", [16, 1920])
            for g in range(4):
                ps = self.psum()
                for i in range(4):
                    cc = g * 4 + i
                    if cc >= 15:
                        break
                    n = 128 if cc < 14 else 32
                    self.tr(ps[0:16, i * 128:i * 128 + n], lastc[0:n, cc, :], self.ident_f[0:n, 0:n])
                w = 512 if g < 3 else 288
                self.cp("act", shs[0:16, g * 512:g * 512 + w], ps[0:16, 0:w])
            self.store(io["shift_s"], shs[0:16, 0:1824])
        if last_prompt:
            sho = A("sho", [1, 1920]) if scoped else arf[0:1, 0:1920]
            for g in range(4):
                ps = self.psum()
                for i in range(4):
                    cc = g * 4 + i
                    if cc >= 15:
                        break
                    n = 128 if cc < 14 else 32
                    self.tr(ps[0:1, i * 128:i * 128 + n], self.carry[0:n, cc:cc + 1], self.ident_f[0:n, 0:n])
                w = 512 if g < 3 else 288
                self.cp("act", sho[0:1, g * 512:g * 512 + w], ps[0:1, 0:w])
            self.store(io["shift_p"], sho[0:1, 0:1824])
            Sp = A("Sp", [64, 512])
            ps = self.psum()
            for c in range(4):
                self.tr(ps[0:64, c * 128:(c + 1) * 128], self.H[:, c, :], self.ident_f)
            self.cp("act", Sp, ps[0:64, :])
            self.store(io["wkv_p"].rearrange("(c hp) v k -> v c hp k", hp=2), Sp.v(Sp.ap.rearrange("p (c hp k) -> p c hp k", c=4, hp=2)))
        if samp:
            UVs = [A("UVm%d" % i, [128, 2, 2, 512], SD) for i in range(2)]
            wcs = Et.v(Et.ap.rearrange("p c (q l) -> p q c l", l=8))

            def build_uv(q0_):
                UVm_ = UVs[(q0_ // 2) % 2]
                for qi in range(2):
                    self.ts("dve", UVm_[:, qi, 0, :], Usd, self.rowmask[:, q0_ + qi:q0_ + qi + 1], ALU.mult)
                    self.ts("dve", UVm_[:, qi, 1, :], Vtok, self.rowmask[:, q0_ + qi:q0_ + qi + 1], ALU.mult)
            build_uv(0)
            for q0 in range(0, 16, 2):
                UVm = UVs[(q0 // 2) % 2]
                psH = self.psum()
                for qi in range(2):
                    for h in range(8):
                        c = h // 2
                        rows = hrows(h)
                        hs = slice(h * 64, (h + 1) * 64)
                        g = qi * 4 + c
                        out = psH[rows, g * 64:(g + 1) * 64]
                        self.mm(out, Btok[:, hs], UVm[:, qi, 0, hs], start=True, stop=False)
                        self.mm(out, Ktok[:, hs], UVm[:, qi, 1, hs], start=False, stop=True)
                if q0 + 2 < 16:
                    build_uv(q0 + 2)
                hv = H0f[:, q0:q0 + 2]
                self.tt("dve", hv, hv, psH.v(psH.ap.rearrange("p (q c v) -> p q c v", q=2, c=4)), ALU.add)
                self.tt("dve", hv, hv, wcs[:, q0:q0 + 2, :, 7:8].bc([128, 2, 4, 64]), ALU.mult)
        rl(m3)

    def finish(self):
        S = self.S
        need = {}
        for R in self.outres:
            for s, v in R.r.items():
                need[s] = max(need.get(s, 0), v)
        S._emit_waits("sp", need)
        S.barrier()


IN_SPECS = [
    ("xp", [2048, 1024]), ("xs", [128, 1024]), ("mem", [256, 1024]), ("swkv", [16, 8, 64, 64]),
    ("sshift", [16, 1824]), ("sconv", [480, 256]), ("ck", [16, 256, 256]), ("cv", [16, 256, 256]),
    ("pp", [NROWS, 128]), ("w_up1", [1024, 5632]), ("w_dn1", [2816, 1024]), ("w_in", [1024, 5664]),
    ("w_du", [64, 512]), ("w_au", [64, 512]), ("w_gu", [160, 512]), ("w_ro", [512, 1024]),
    ("w_co", [256, 1024]), ("w_kv", [1024, 512]), ("w_xo", [256, 1024]), ("w_o", [1024, 1024]),
    ("w_up2", [1024, 5632]), ("w_dn2", [2816, 1024]),
]
OUT_SPECS = [
    ("y_p", [2048, 1024]), ("y_s", [128, 1024]), ("wkv_p", [8, 64, 64]), ("shift_p", [1, 1824]),
    ("conv_p", [30, 256]), ("mk_p", [256, 256]), ("mv_p", [256, 256]), ("wkv_s", [16, 8, 64, 64]),
    ("shift_s", [16, 1824]), ("conv_s", [480, 256]),
]


def build_program(stage=99):
    nc = bass.Bass("TRN2", target_bir_lowering=False)
    io = {}
    for name, shape in IN_SPECS:
        io[name] = nc.dram_tensor(name, shape, F32, kind="ExternalInput").ap()
    for name, shape in OUT_SPECS:
        io[name] = nc.dram_tensor(name, shape, F32, kind="ExternalOutput").ap()
    with contextlib.ExitStack() as st:
        B = Builder(nc, st, io, stage=stage)
        B.setup()
        B.run_block(0)
        B.run_block(1)
        B.finish()
        print("ninst", B.S.ninst, "nwait", B.S.nwait, "ndma", B.S.ndma, "sems", len(B.S.sems))
    return nc


def pack_params(inp):
    def rows(a, pad=None):
        a = np.asarray(a, np.float32).reshape(-1)
        if pad is not None:
            a = np.concatenate([a, np.zeros(pad - a.size, np.float32)])
        return a.reshape(-1, 128)
    parts = [rows(inp["ffn1_norm"]), rows(inp["mix_norm"]), rows(inp["ffn2_norm"]), rows(inp["final_norm"]),
             rows(inp["mu_shift"], 15 * 128), rows(inp["w0"]), rows(inp["a0"]), rows(inp["k_k"]), rows(inp["k_a"]),
             rows(inp["r_k"]), rows(inp["gn_g"]), rows(inp["gn_b"]), rows(inp["glu_b"]), rows(inp["conv_w"]),
             rows(inp["conv_b"]), rows(inp["conv_ln_g"]), rows(inp["conv_ln_b"])]
    pp = np.concatenate(parts, 0)
    assert pp.shape == (NROWS, 128), pp.shape
    return np.ascontiguousarray(pp)


def make_in_maps(inp):
    f = lambda a: np.ascontiguousarray(np.asarray(a, np.float32))
    pp = pack_params(inp)
    shared = {
        "pp": pp, "w_up1": f(inp["ffn1_w_up"][0]), "w_dn1": f(inp["ffn1_w_down"][0]), "w_in": f(inp["w_in"][0]),
        "w_du": f(inp["w_decay_up"][0]), "w_au": f(inp["w_a_up"][0]), "w_gu": f(inp["w_g_up"][0]),
        "w_ro": f(inp["w_rwkv_out"][0]), "w_co": f(inp["w_conv_out"][0]), "w_kv": f(inp["w_mem_kv"][0]),
        "w_xo": f(inp["w_xattn_out"][0]), "w_o": f(inp["w_o"][0]), "w_up2": f(inp["ffn2_w_up"][0]),
        "w_dn2": f(inp["ffn2_w_down"][0]),
    }
    maps = []
    for c in range(8):
        sl = slice(16 * c, 16 * c + 16)
        m = dict(shared)
        m["xp"] = f(inp["x_prompt"][c])
        m["xs"] = f(inp["x_sample"][sl]).reshape(128, 1024)
        m["mem"] = f(inp["mem_prompt"][c])
        m["swkv"] = f(inp["state_wkv"][0, sl])
        m["sshift"] = f(inp["state_shift"][0, sl])
        m["sconv"] = f(inp["state_conv"][0, sl]).reshape(480, 256)
        m["ck"] = f(inp["cache_mem_k"][0, sl]).reshape(16, 256, 256)
        m["cv"] = f(inp["cache_mem_v"][0, sl]).reshape(16, 256, 256)
        maps.append(m)
    return maps


def gather(results):
    g = lambda k: [np.asarray(r[k], np.float32) for r in results]
    y_p = np.stack(g("y_p"), 0)
    y_s = np.concatenate(g("y_s"), 0).reshape(128, 8, 1024)
    wkv_p = np.stack(g("wkv_p"), 0)[None]
    shift_p = np.concatenate(g("shift_p"), 0)[None]
    conv_p = np.stack(g("conv_p"), 0)[None]
    mk_p = np.stack(g("mk_p"), 0).reshape(8, 256, 4, 64)[None]
    mv_p = np.stack(g("mv_p"), 0).reshape(8, 256, 4, 64)[None]
    wkv_s = np.concatenate(g("wkv_s"), 0)[None]
    shift_s = np.concatenate(g("shift_s"), 0)[None]
    conv_s = np.concatenate(g("conv_s"), 0).reshape(128, 30, 256)[None]
    return (y_p, y_s, wkv_p, shift_p, conv_p, mk_p, mv_p, wkv_s, shift_s, conv_s)


_NC_CACHE = {}


def kernel(**inputs):
    if "nc" not in _NC_CACHE:
        _NC_CACHE["nc"] = build_program()
    nc = _NC_CACHE["nc"]
    in_maps = make_in_maps(inputs)
    res = run_bass_kernel_spmd(nc, in_maps, core_ids=list(range(8)))
    return gather(res.results)
```
